# Optimizing a Trainium2 kernel written in Bass

```python
import math
import jax, jax.numpy as jnp
from jax import lax
import numpy as np

D_MODEL = 1024
BATCH = 16
SEQ = 2048
DEPTH = 1
DEC_BATCH = 8
DEC_SEQ = 8192
PAST_LEN = 128

A_W = 512
N_HEADS = 4
QK_NOPE = 128
QK_ROPE = 64
V_DIM = 128
Q_LORA = 384
KV_LORA = 256
MLA_W = N_HEADS * V_DIM
MIX_W = A_W + MLA_W
IN_COLS = 3 * A_W + Q_LORA + KV_LORA + QK_ROPE
D_FF = 2816
CONV_W = 3
QBLK = 128
ROPE_THETA = 10000.0
ATTN_SCALE = 1.0 / math.sqrt(QK_NOPE + QK_ROPE)
EPS = 1e-6

kernel_name = "hybrid_conv_mla_adaln_encoder"


def rmsnorm(x, g):
    xf = x.astype(jnp.float32)
    y = xf * lax.rsqrt(jnp.mean(xf * xf, axis=-1, keepdims=True) + EPS)
    return (y * g.astype(jnp.float32)).astype(x.dtype)


def dwconv3(x, w):
    xp = jnp.pad(x, ((0, 0), (1, 1), (0, 0)))
    return xp[:, :-2] * w[0] + xp[:, 1:-1] * w[1] + xp[:, 2:] * w[2]


def rope_tables(s):
    inv = 1.0 / (ROPE_THETA ** (jnp.arange(0, QK_ROPE, 2, dtype=jnp.float32) / QK_ROPE))
    ang = jnp.arange(s, dtype=jnp.float32)[:, None] * inv[None, :]
    return jnp.cos(ang), jnp.sin(ang)


def apply_rope(x, cos, sin):
    c = cos[None, :, None, :].astype(x.dtype)
    s = sin[None, :, None, :].astype(x.dtype)
    x1, x2 = jnp.split(x, 2, axis=-1)
    return jnp.concatenate([x1 * c - x2 * s, x2 * c + x1 * s], axis=-1)


def mla_attention(q_nope, q_rope, k_nope, k_rope, v):
    b, s, h, _ = q_nope.shape
    nb = s // QBLK
    qn_b = q_nope.reshape(b, nb, QBLK, h, QK_NOPE).transpose(1, 0, 2, 3, 4)
    qr_b = q_rope.reshape(b, nb, QBLK, h, QK_ROPE).transpose(1, 0, 2, 3, 4)

    def block(args):
        qn, qr = args
        sc = (jnp.einsum('bqhd,bkhd->bhqk', qn, k_nope)
              + jnp.einsum('bqhr,bkr->bhqk', qr, k_rope)).astype(jnp.float32) * ATTN_SCALE
        p = jax.nn.softmax(sc, axis=-1).astype(v.dtype)
        return jnp.einsum('bhqk,bkhd->bqhd', p, v)

    o = lax.map(block, (qn_b, qr_b))
    return o.transpose(1, 0, 2, 3, 4).reshape(b, s, h * V_DIM)


def forward(x, c, w_ada, b_ada, norm1_g, w_in, conv_a_w, q_norm_g, w_uq, kv_norm_g, w_ukv,
            out_norm_a_g, out_norm_b_g, w_o, norm2_g, w_up, ffn_conv_w, w_down, final_g):
    b, s, _ = x.shape
    cos, sin = rope_tables(s)
    c_act = jax.nn.silu(c)
    for l in range(DEPTH):
        mod = (c_act @ w_ada[l] + b_ada[l])[:, None, :]
        sh1, sc1, g1, sh2, sc2, g2 = jnp.split(mod, 6, axis=-1)

        h = rmsnorm(x, norm1_g[l]) * (1.0 + sc1) + sh1
        z = h @ w_in[l]
        cuts = np.cumsum([A_W, A_W, A_W, Q_LORA, KV_LORA]).tolist()
        h_a, b_a, c_a, c_q, c_kv, k_r = jnp.split(z, cuts, axis=-1)

        y_a = b_a * dwconv3(c_a * h_a, conv_a_w[l])

        q = (rmsnorm(c_q, q_norm_g[l]) @ w_uq[l]).reshape(b, s, N_HEADS, QK_NOPE + QK_ROPE)
        q_nope, q_rope = q[..., :QK_NOPE], apply_rope(q[..., QK_NOPE:], cos, sin)
        kv = (rmsnorm(c_kv, kv_norm_g[l]) @ w_ukv[l]).reshape(b, s, N_HEADS, QK_NOPE + V_DIM)
        k_nope, v = kv[..., :QK_NOPE], kv[..., QK_NOPE:]
        k_rope = apply_rope(k_r[:, :, None, :], cos, sin)[:, :, 0, :]
        y_b = mla_attention(q_nope, q_rope, k_nope, k_rope, v)

        y = jnp.concatenate([rmsnorm(y_a, out_norm_a_g[l]), rmsnorm(y_b, out_norm_b_g[l])], axis=-1) @ w_o[l]
        x = x + g1 * y

        h = rmsnorm(x, norm2_g[l]) * (1.0 + sc2) + sh2
        u = dwconv3(h @ w_up[l], ffn_conv_w[l])
        gate, val = jnp.split(u, 2, axis=-1)
        x = x + g2 * ((jax.nn.silu(gate) * val) @ w_down[l])
    return rmsnorm(x, final_g)


def setup_inputs(seed: int = 0) -> dict:
    key = jax.random.key(seed)
    ks = jax.random.split(key, 24)
    f32 = jnp.float32

    def nrm(k, shape, scale):
        return jax.random.normal(k, shape, f32) * scale

    def gain(k, shape):
        return 1.0 + 0.05 * jax.random.normal(k, shape, f32)

    L = DEPTH
    return {
        "x_prompt": nrm(ks[0], (BATCH, SEQ, D_MODEL), 1.0),
        "x_sample": nrm(ks[1], (DEC_BATCH, DEC_SEQ, D_MODEL), 1.0),
        "c_prompt": nrm(ks[2], (BATCH, D_MODEL), 1.0),
        "c_sample": nrm(ks[3], (DEC_BATCH, D_MODEL), 1.0),
        "w_ada": nrm(ks[4], (L, D_MODEL, 6 * D_MODEL), 0.3 * D_MODEL ** -0.5),
        "b_ada": nrm(ks[5], (L, 6 * D_MODEL), 0.02),
        "norm1_g": gain(ks[6], (L, D_MODEL)),
        "w_in": nrm(ks[7], (L, D_MODEL, IN_COLS), D_MODEL ** -0.5),
        "conv_a_w": nrm(ks[8], (L, CONV_W, A_W), 0.5),
        "q_norm_g": gain(ks[9], (L, Q_LORA)),
        "w_uq": nrm(ks[10], (L, Q_LORA, N_HEADS * (QK_NOPE + QK_ROPE)), Q_LORA ** -0.5),
        "kv_norm_g": gain(ks[11], (L, KV_LORA)),
        "w_ukv": nrm(ks[12], (L, KV_LORA, N_HEADS * (QK_NOPE + V_DIM)), KV_LORA ** -0.5),
        "out_norm_a_g": gain(ks[13], (L, A_W)),
        "out_norm_b_g": gain(ks[14], (L, MLA_W)),
        "w_o": nrm(ks[15], (L, MIX_W, D_MODEL), MIX_W ** -0.5),
        "norm2_g": gain(ks[16], (L, D_MODEL)),
        "w_up": nrm(ks[17], (L, D_MODEL, 2 * D_FF), D_MODEL ** -0.5),
        "ffn_conv_w": nrm(ks[18], (L, CONV_W, 2 * D_FF), 0.5),
        "w_down": nrm(ks[19], (L, D_FF, D_MODEL), D_FF ** -0.5),
        "final_g": gain(ks[20], (D_MODEL,)),
    }


def reference(x_prompt, x_sample, c_prompt, c_sample, w_ada, b_ada, norm1_g, w_in, conv_a_w,
              q_norm_g, w_uq, kv_norm_g, w_ukv, out_norm_a_g, out_norm_b_g, w_o, norm2_g,
              w_up, ffn_conv_w, w_down, final_g):
    y_prompt = forward(x_prompt, c_prompt, w_ada, b_ada, norm1_g, w_in, conv_a_w, q_norm_g, w_uq,
                       kv_norm_g, w_ukv, out_norm_a_g, out_norm_b_g, w_o, norm2_g, w_up,
                       ffn_conv_w, w_down, final_g)
    y_sample = forward(x_sample, c_sample, w_ada, b_ada, norm1_g, w_in, conv_a_w, q_norm_g, w_uq,
                       kv_norm_g, w_ukv, out_norm_a_g, out_norm_b_g, w_o, norm2_g, w_up,
                       ffn_conv_w, w_down, final_g)
    return (y_prompt, y_sample)
```

```python
import math
from contextlib import ExitStack

import numpy as np
import concourse.bass as bass
import concourse.mybir as mybir
from concourse.bass_utils import run_bass_kernel_spmd

F32 = mybir.dt.float32
BF16 = mybir.dt.bfloat16
AF = mybir.ActivationFunctionType
ALU = mybir.AluOpType

D = 1024
A_W = 512
NH = 4
DN = 128
DR = 64
DV = 128
QL = 384
KVL = 256
INC = 2240
DFF = 2816
NU = DFF // 128
ATTN_SCALE = 1.0 / math.sqrt(DN + DR)
EPS = 1e-6
THETA = 10000.0
N_CORES = 8
SAME_ENGINE_SYNC = True

ENGS = ["pe", "act", "dve", "pool", "sp"]


class Buf:
    __slots__ = ("w", "r", "war")

    def __init__(self):
        self.w = {}
        self.r = {}
        self.war = {}


def _mg(d, k, v):
    if d.get(k, -1) < v:
        d[k] = v


class DSem:
    __slots__ = ("h", "val", "val0")

    def __init__(self, h):
        self.h = h
        self.val = 0
        self.val0 = 0


class Op:
    __slots__ = ("eng", "idx", "fn", "deps", "sig", "dsem", "dval")


class Prog:
    def __init__(self, nc, es):
        self.nc = nc
        self.es = es
        self.eh = dict(pe=nc.tensor, act=nc.scalar, dve=nc.vector, pool=nc.gpsimd, sp=nc.sync)
        self.psem = {e: es.enter_context(nc.semaphore("pg_" + e)) for e in ENGS}
        self.cnt = {e: 0 for e in ENGS}
        self.waited = {e: {} for e in ENGS}
        self.ops = {e: [] for e in ENGS}
        self.dsems = []
        self.first_phase = True
        self.disabled = False

    def dsem(self, name):
        d = DSem(self.es.enter_context(self.nc.semaphore("d_" + name)))
        self.dsems.append(d)
        return d

    def op(self, eng, fn, reads=(), writes=(), pwrites=(), dsem=None):
        if self.disabled:
            return None
        self.nops = getattr(self, 'nops', 0) + 1
        if self.nops > DEBUG_MAXOPS:
            return None
        deps = {}
        for b in reads:
            for k, v in b.w.items():
                _mg(deps, k, v)
        for b in writes:
            war = {}
            for k, v in b.r.items():
                _mg(war, k, v)
            for k, v in b.w.items():
                _mg(war, k, v)
            b.war = war
            for k, v in war.items():
                _mg(deps, k, v)
        newver = []
        for b in pwrites:
            if b.r or not b.w:
                war = {}
                for k, v in b.r.items():
                    _mg(war, k, v)
                for k, v in b.w.items():
                    _mg(war, k, v)
                b.war = war
                newver.append(b)
            for k, v in b.war.items():
                _mg(deps, k, v)
        o = Op()
        o.eng = eng
        o.idx = len(self.ops[eng])
        o.fn = fn
        o.deps = deps
        o.sig = False
        o.dsem = dsem
        if dsem is not None:
            dsem.val += 16
            o.dval = dsem.val
            key, val = dsem, dsem.val
        else:
            o.dval = 0
            key, val = eng, o.idx
        self.ops[eng].append(o)
        for k, v in deps.items():
            if isinstance(k, str):
                self.ops[k][v].sig = True
        for b in reads:
            _mg(b.r, key, val)
        for b in writes:
            b.w = {key: val}
            b.r = {}
        for b in newver:
            b.w = {}
            b.r = {}
        for b in pwrites:
            _mg(b.w, key, val)
        return o

    def dma(self, q, out, in_, dsem, reads=(), writes=(), pwrites=(), nonc=False):
        nc = self.nc

        def fn(e):
            if nonc:
                with nc.allow_non_contiguous_dma("small strided setup load"):
                    return e.dma_start(out=out, in_=in_)
            return e.dma_start(out=out, in_=in_)

        return self.op(q, fn, reads, writes, pwrites, dsem=dsem)

    def mm(self, out, pairs, reads=(), writes=(), pwrites=(), first=True, last=True):
        def fn(e):
            n = len(pairs)
            ins = None
            for i, (l, r) in enumerate(pairs):
                ins = e.matmul(out, l, r, start=(first and i == 0), stop=(last and i == n - 1))
            return ins

        return self.op("pe", fn, reads, writes, pwrites)

    def emit_phase(self):
        if self.disabled:
            return
        nc = self.nc
        for e in ENGS:
            for o in reversed(self.ops[e]):
                if o.dsem is None:
                    o.sig = True
                    break
        signo = {}
        for e in ENGS:
            c = self.cnt[e]
            lst = []
            for o in self.ops[e]:
                if o.sig and o.dsem is None:
                    c += 1
                lst.append(c)
            signo[e] = lst
        fence = None
        if not self.first_phase:
            fence = ([(self.psem[k], self.cnt[k], k) for k in ENGS if self.cnt[k] > 0]
                     + [(d.h, d.val0, d) for d in self.dsems if getattr(d, "val0", 0) > 0])
        with nc.Block() as block:
            reg = dict(pe=block.tensor, act=block.scalar, dve=block.vector, pool=block.gpsimd, sp=block.sync)
            for e in ENGS:
                ops = self.ops[e]

                def body(engh, e=e, ops=ops):
                    wd = self.waited[e]
                    if fence is not None:
                        for h, v, k in fence:
                            if k == e:
                                continue
                            if wd.get(k, 0) >= v:
                                continue
                            engh.wait_ge(h, v)
                            wd[k] = v
                    for o in ops:
                        for k, v in o.deps.items():
                            if isinstance(k, str):
                                if k == e and (e == "pe" or not SAME_ENGINE_SYNC):
                                    continue
                                need = signo[k][v]
                                h = self.psem[k]
                            else:
                                need = v
                                h = k.h
                            if wd.get(k, 0) >= need:
                                continue
                            engh.wait_ge(h, need)
                            wd[k] = need
                        ins = o.fn(engh)
                        if o.dsem is not None:
                            ins.then_inc(o.dsem.h, 16)
                        elif o.sig:
                            ins.then_inc(self.psem[e], 1)

                reg[e](body)
        for e in ENGS:
            if self.ops[e]:
                self.cnt[e] = signo[e][-1]
            self.ops[e] = []
        for d in self.dsems:
            d.val0 = d.val
        self.first_phase = False

    def final_wait(self):
        nc = self.nc
        with nc.Block() as block:
            def body(engh):
                for k in ENGS:
                    if k != "sp" and self.cnt[k] > 0:
                        engh.wait_ge(self.psem[k], self.cnt[k])
                for d in self.dsems:
                    if d.val > 0:
                        engh.wait_ge(d.h, d.val)
            block.sync(body)


class Ring:
    def __init__(self, items):
        self.items = items
        self.bufs = [Buf() for _ in items]
        self.i = 0

    def next(self):
        k = self.i % len(self.items)
        self.i += 1
        return self.items[k], self.bufs[k]


def seq_tiles(S):
    n = -(-S // 510)
    W = -(-S // n)
    out = []
    t = 0
    while t < S:
        w = min(W, S - t)
        out.append((t, w))
        t += w
    return out


DEBUG_STOP = 99
DEBUG_MAXOPS = 10 ** 9


class _Stop(Exception):
    pass


def build_program(seqs):
    NT = sum(seqs)
    offs = [sum(seqs[:i]) for i in range(len(seqs))]
    NS = len(seqs)
    SMAX = max(seqs)
    nc = bass.Bass("TRN2", target_bir_lowering=False)

    def din(name, shape, dt=F32):
        return nc.dram_tensor(name, list(shape), dt, kind="ExternalInput").ap()

    def dscr(name, shape, dt=BF16):
        return nc.dram_tensor(name, list(shape), dt, kind="Internal").ap()

    x_d = din("x", [NT, D])
    c_d = din("c", [NS, D])
    w_ada_d = din("w_ada", [D, 6 * D])
    b_ada_d = din("b_ada", [1, 6 * D])
    n1g_d = din("norm1_g", [1, D])
    w_in_d = din("w_in", [D, INC])
    conva_d = din("conv_a_w", [3, A_W])
    qng_d = din("q_norm_g", [1, QL])
    w_uq_d = din("w_uq", [QL, NH * (DN + DR)])
    kvng_d = din("kv_norm_g", [1, KVL])
    w_ukv_d = din("w_ukv", [KVL, NH * (DN + DV)])
    ang_d = din("out_norm_a_g", [1, A_W])
    bng_d = din("out_norm_b_g", [1, A_W])
    w_o_d = din("w_o", [D, D])
    n2g_d = din("norm2_g", [1, D])
    w_up_d = din("w_up", [D, 2 * DFF])
    convf_d = din("ffn_conv_w", [3, 2 * DFF])
    w_dn_d = din("w_down", [DFF, D])
    fg_d = din("final_g", [1, D])
    cos_d = din("cos2", [64, SMAX + 2])
    sin_d = din("sin2", [64, SMAX + 2])
    y_d = nc.dram_tensor("y", [NT, D], F32, kind="ExternalOutput").ap()

    qS = dscr("qS", [NH, 128, NT])
    qrS = dscr("qrS", [NH, 64, NT])
    kS = dscr("kS", [NH, 128, NT])
    krS = dscr("krS", [64, NT])
    vS = dscr("vS", [NT, NH * DV])
    mixS = dscr("mixS", [8, 128, NT])
    modS = dscr("modS", [NS, 6 * D], F32)
    wupS = dscr("wupS", [NU, 128, 8, 256])

    with ExitStack() as es:
        P = Prog(nc, es)

        def sbg(name, shape, dt):
            return es.enter_context(nc.sbuf_tensor(name, list(shape), dt))

        ident = sbg("ident", [128, 128], BF16)
        ones = sbg("ones", [128, 128], BF16)
        epsT = sbg("epsT", [128, 1], F32)
        onesf = sbg("onesf", [128, 128], F32)
        n1g = sbg("n1g", [128, 8], F32)
        n2g = sbg("n2g", [128, 8], F32)
        qg = sbg("qg", [128, 3], F32)
        kvg = sbg("kvg", [128, 2], F32)
        ag = sbg("ag", [128, 4], F32)
        bg = sbg("bg", [128, 4], F32)
        convA = sbg("convA", [128, 3, 4], F32)
        convF = sbg("convF", [128, 3, 44], F32)
        modT = sbg("modT", [128, NS, 48], F32)
        gm1 = sbg("gm1", [128, NS, 8], F32)
        gm2 = sbg("gm2", [128, NS, 8], F32)

        def sh1(s, c):
            return modT[:, s, 0 * 8 + c:0 * 8 + c + 1]

        def sh2(s, c):
            return modT[:, s, 3 * 8 + c:3 * 8 + c + 1]

        try:
            with ExitStack() as ph:
                def sb(name, shape, dt):
                    return ph.enter_context(nc.sbuf_tensor(name, list(shape), dt))

                identf = sb("identf", [128, 128], F32)
                cT = sb("cT", [128, 8, NS], F32)
                cA = sb("cA", [128, 8, NS], F32)
                modsb = sb("modsb", [NS, 6 * D], F32)
                bsb = sb("bsb", [NS, 6 * D], F32)
                wa = [sb("wa%d" % i, [128, 8, 512], F32) for i in range(3)]
                pm = [ph.enter_context(nc.psum_tensor("p0m%d" % i, [128, 512], F32)) for i in range(2)]
                B = {k: Buf() for k in ["identf", "ident", "ones", "eps", "cT", "cA", "modsb", "bsb", "small", "modT", "gm",
                                         "modS"]}
                ds_small = P.dsem("small")
                ds_c = P.dsem("c")
                ds_b = P.dsem("b")
                ds_mod = P.dsem("mod")
                ds_modT = P.dsem("modT")
                ds_wa = [P.dsem("wa%d" % i) for i in range(3)]
                wa_ring = Ring(list(zip(wa, ds_wa)))
                pm_ring = Ring(pm)


                P.op("pool", lambda e: e.memset(identf[:], 0.0), writes=[B["identf"]])
                P.op("pool", lambda e: e.affine_select(out=identf[:], in_=identf[:], pattern=[[-1, 128]],
                                                       compare_op=ALU.not_equal, fill=1.0, base=0,
                                                       channel_multiplier=1), reads=[B["identf"]], writes=[B["identf"]])
                P.op("dve", lambda e: e.tensor_copy(out=ident[:], in_=identf[:]), reads=[B["identf"]], writes=[B["ident"]])
                P.op("dve", lambda e: e.memset(ones[:], 1.0), writes=[B["ones"]])
                P.op("dve", lambda e: e.memset(onesf[:], 1.0), writes=[B["ones"]])
                P.op("dve", lambda e: e.memset(epsT[:], EPS), writes=[B["eps"]])

                def fm(dst, src, n):
                    P.dma("sp", dst, src.rearrange("o (c p) -> p (o c)", p=128), ds_small, pwrites=[B["small"]], nonc=True)

                fm(n1g[:], n1g_d, 8)
                fm(n2g[:], n2g_d, 8)
                fm(qg[:], qng_d, 3)
                fm(kvg[:], kvng_d, 2)
                fm(ag[:], ang_d, 4)
                fm(bg[:], bng_d, 4)
                for j in range(3):
                    P.dma("sp", convA[:, j, :], conva_d[j:j + 1, :].rearrange("o (c p) -> p (o c)", p=128), ds_small,
                          pwrites=[B["small"]], nonc=True)
                    for q4 in range(4):
                        P.dma("sp", convF[:, j, q4 * 11:(q4 + 1) * 11],
                              convf_d[j:j + 1, q4 * 1408:(q4 + 1) * 1408].rearrange("o (c p) -> p (o c)", p=128), ds_small,
                              pwrites=[B["small"]], nonc=True)
                for s in range(NS):
                    P.dma("sp", cT[:, :, s], c_d[s:s + 1, :].rearrange("o (c p) -> p (o c)", p=128), ds_c, pwrites=[B["cT"]],
                          nonc=True)
                P.dma("sp", bsb[:], b_ada_d.to_broadcast([NS, 6 * D]), ds_b, writes=[B["bsb"]])
                P.op("act", lambda e: e.activation(out=cA[:], in_=cT[:], func=AF.Silu), reads=[B["cT"]], writes=[B["cA"]])
                wav = w_ada_d.rearrange("(kc p) n -> p kc n", p=128)
                for j in range(12):
                    (wt, dsw), wb = wa_ring.next()
                    P.dma("sp", wt[:], wav[:, :, j * 512:(j + 1) * 512], dsw, writes=[wb])
                    pmt, pb = pm_ring.next()
                    P.mm(pmt[0:NS, :], [(cA[:, kc, :], wt[:, kc, :]) for kc in range(8)], reads=[B["cA"], wb], writes=[pb])
                    P.op("dve", lambda e, pmt=pmt, j=j: e.tensor_tensor(out=modsb[:, j * 512:(j + 1) * 512], in0=pmt[0:NS, :],
                                                                         in1=bsb[:, j * 512:(j + 1) * 512], op=ALU.add),
                         reads=[pb, B["bsb"]], pwrites=[B["modsb"]])
                P.dma("pool", modS, modsb[:], ds_mod, reads=[B["modsb"]], writes=[B["modS"]])
                for s in range(NS):
                    for v6 in range(6):
                        P.dma("sp", modT[:, s, v6 * 8:(v6 + 1) * 8],
                              modS[s:s + 1, v6 * D:(v6 + 1) * D].rearrange("o (j p) -> p (o j)", p=128), ds_modT,
                              reads=[B["modS"]], pwrites=[B["modT"]], nonc=True)
                for s in range(NS):
                    P.op("dve", lambda e, s=s: e.scalar_tensor_tensor(out=gm1[:, s, :], in0=modT[:, s, 8:16], scalar=1.0,
                                                                      in1=n1g[:], op0=ALU.add, op1=ALU.mult),
                         reads=[B["modT"], B["small"]], pwrites=[B["gm"]])
                    P.op("dve", lambda e, s=s: e.scalar_tensor_tensor(out=gm2[:, s, :], in0=modT[:, s, 32:40], scalar=1.0,
                                                                      in1=n2g[:], op0=ALU.add, op1=ALU.mult),
                         reads=[B["modT"], B["small"]], pwrites=[B["gm"]])
                P.op("dve", lambda e: e.tensor_scalar(out=qg[:], in0=qg[:], scalar1=ATTN_SCALE, scalar2=None, op0=ALU.mult),
                     reads=[B["small"]], writes=[B["small"]])
                P.emit_phase()
                if DEBUG_STOP == 0:
                    P.disabled = True

            with ExitStack() as ph:
                def sb(name, shape, dt):
                    return ph.enter_context(nc.sbuf_tensor(name, list(shape), dt))

                def pst(name, shape, dt):
                    return ph.enter_context(nc.psum_tensor(name, list(shape), dt))

                w_in_sb = sb("w_in_sb", [128, 8, INC], BF16)
                w_krr = sb("w_krr", [128, 8, 64], BF16)
                w_uq_sb = sb("w_uq_sb", [128, 3, 768], BF16)
                w_uqr = sb("w_uqr", [128, 3, 256], BF16)
                w_uk_sb = sb("w_uk_sb", [128, 2, 512], BF16)
                w_uv_sb = sb("w_uv_sb", [128, 2, 512], BF16)
                xin = [sb("xin%d" % i, [128, D], F32) for i in range(3)]
                junk = sb("junk", [128, D], BF16)
                ms = sb("ms", [128, 8], F32)
                sd = sb("sd", [128, 8], F32)
                rstd = sb("rstd", [128, 8], F32)
                xn = [sb("xn%d" % i, [128, 4, D], BF16) for i in range(2)]
                hT = [sb("hT%d" % i, [128, 8, 512], BF16) for i in range(2)]
                hasb = sb("hasb", [128, 4, 512], F32)
                psb = sb("psb", [128, 4, 512], F32)
                yasb = sb("yasb", [128, 4, 512], F32)
                basb = sb("basb", [128, 4, 512], F32)
                cqsb = sb("cqsb", [128, 3, 512], F32)
                ckvsb = sb("ckvsb", [128, 2, 512], F32)
                sq = [sb("sq%d" % i, [128, 512], BF16) for i in range(4)]
                rs = [sb("rs%d" % i, [128, 512], F32) for i in range(2)]
                rr = [sb("rr%d" % i, [128, 512], F32) for i in range(3)]
                cqn = sb("cqn", [128, 3, 512], BF16)
                ckvn = sb("ckvn", [128, 2, 512], BF16)
                yan = sb("yan", [128, 4, 512], BF16)
                qTo = sb("qTo", [128, 4, 512], BF16)
                qro = sb("qro", [64, 4, 512], BF16)
                t1 = [sb("t1_%d" % i, [64, 512], F32) for i in range(2)]
                t2 = [sb("t2_%d" % i, [64, 512], F32) for i in range(2)]
                kTo = sb("kTo", [128, 4, 512], BF16)
                kro = sb("kro", [64, 512], BF16)
                vo = sb("vo", [128, 4, 512], BF16)
                cs = [sb("cs%d" % i, [64, 512], F32) for i in range(2)]
                sn = [sb("sn%d" % i, [64, 512], F32) for i in range(2)]
                tp = [pst("tp%d" % i, [128, 2, 512], BF16) for i in range(2)]
                zp = [pst("zp%d" % i, [128, 512], F32) for i in range(6)]

                tp_ring = Ring(tp)
                zp_ring = Ring(zp)
                sq_ring = Ring(sq)
                rs_ring = Ring(rs)
                rr_ring = Ring(rr)
                t1_ring = Ring(t1)
                t2_ring = Ring(t2)
                xin_ds = [P.dsem("xin%d" % i) for i in range(3)]
                xin_ring = Ring(list(zip(xin, xin_ds)))
                cs_ds = [P.dsem("cs%d" % i) for i in range(2)]
                cs_ring = Ring(list(zip(cs, sn, cs_ds)))
                xn_ring = Ring(xn)
                hT_ring = Ring(hT)
                ds_w = P.dsem("p1w")
                ds_wup = P.dsem("wup")
                ds_st = {k: P.dsem("st_" + k) for k in ["ya", "q", "qr", "k", "kr", "v"]}
                Bw = Buf()
                Bst1 = [Buf() for _ in range(4)]
                Bha = [Buf() for _ in range(4)]
                Bp = [Buf() for _ in range(4)]
                Byas = [Buf() for _ in range(4)]
                pend_conv = []
                Bba = [Buf() for _ in range(4)]
                Bx = {k: Buf() for k in ["cqsb", "ckvsb", "cqn", "ckvn", "yan", "qTo", "qro", "kTo", "kro", "vo"]}

                P.dma("pool", w_in_sb[:], w_in_d.rearrange("(kc p) n -> p kc n", p=128), ds_w, pwrites=[Bw])
                P.dma("pool", w_uq_sb[:], w_uq_d.rearrange("(kc p) n -> p kc n", p=128), ds_w, pwrites=[Bw])
                wkv = w_ukv_d.rearrange("(kc p) (h t d) -> p kc h t d", p=128, h=NH, t=2)
                for kc in range(2):
                    P.dma("pool", w_uk_sb[:, kc, :].rearrange("p (h d) -> p h d", h=NH), wkv[:, kc, :, 0, :], ds_w, pwrites=[Bw])
                    P.dma("pool", w_uv_sb[:, kc, :].rearrange("p (h d) -> p h d", h=NH), wkv[:, kc, :, 1, :], ds_w, pwrites=[Bw])
                wupv = w_up_d.rearrange("(kc p) n -> p kc n", p=128)
                for u in range(NU):
                    P.dma("pool", wupS[u, :, :, 0:128], wupv[:, :, u * 128:(u + 1) * 128], ds_wup)
                    P.dma("pool", wupS[u, :, :, 128:256], wupv[:, :, DFF + u * 128:DFF + (u + 1) * 128], ds_wup)
                Bw2 = Buf()
                P.op("dve", lambda e: e.tensor_scalar(out=w_krr[:, :, 0:32], in0=w_in_sb[:, :, 2208:2240], scalar1=-1.0,
                                                      scalar2=None, op0=ALU.mult), reads=[Bw], pwrites=[Bw2])
                P.op("dve", lambda e: e.tensor_copy(out=w_krr[:, :, 32:64], in_=w_in_sb[:, :, 2176:2208]), reads=[Bw],
                     pwrites=[Bw2])
                for h in range(NH):
                    b0 = h * 192 + 128
                    P.op("dve", lambda e, h=h, b0=b0: e.tensor_scalar(out=w_uqr[:, :, h * 64:h * 64 + 32],
                                                                      in0=w_uq_sb[:, :, b0 + 32:b0 + 64], scalar1=-1.0,
                                                                      scalar2=None, op0=ALU.mult), reads=[Bw], pwrites=[Bw2])
                    P.op("dve", lambda e, h=h, b0=b0: e.tensor_copy(out=w_uqr[:, :, h * 64 + 32:h * 64 + 64],
                                                                    in_=w_uq_sb[:, :, b0:b0 + 32]), reads=[Bw], pwrites=[Bw2])
                for i in range(3):
                    P.op("pool", lambda e, i=i: e.memset(xin[i][:], 0.0), writes=[xin_ring.bufs[i]])

                WR = [Bw, Bw2]

                def p1_front_a(s, ti, t0, w, ntl):
                    S = seqs[s]
                    off = offs[s]
                    C = w + 2
                    first = (ti == 0)
                    last = (ti == ntl - 1)
                    jlo = 1 if first else 0
                    jhi = C - 1 if last else C
                    G = -(-C // 128)
                    xnt, xnb = xn_ring.next()
                    hTt, hTb = hT_ring.next()
                    (cst, snt, csd), csb = cs_ring.next()
                    P.dma("sp", cst[:, 0:C], cos_d[:, t0:t0 + C], csd, writes=[csb])
                    P.dma("sp", snt[:, 0:C], sin_d[:, t0:t0 + C], csd, pwrites=[csb])
                    for g in range(G):
                        r = min(128, C - 128 * g)
                        lo = max(jlo, 128 * g) - 128 * g
                        hi = min(jhi, 128 * g + r) - 128 * g
                        (xt, xd), xb = xin_ring.next()
                        tok0 = off + t0 - 1 + 128 * g
                        P.dma("sp", xt[lo:hi, :], x_d[tok0 + lo:tok0 + hi, :], xd, writes=[xb])
                        yield
                        P.op("act", lambda e, xt=xt, g=g: e.activation(out=junk[:], in_=xt[:], func=AF.Square,
                                                                       scale=1.0 / 32.0, accum_out=ms[:, g:g + 1]),
                             reads=[xb], writes=[Bst1[g]])
                        yield
                        P.op("act", lambda e, g=g: e.activation(out=sd[:, g:g + 1], in_=ms[:, g:g + 1], func=AF.Ln,
                                                                bias=epsT[:], scale=1.0), reads=[Bst1[g]], writes=[Bst1[g]])
                        yield
                        P.op("act", lambda e, g=g: e.activation(out=rstd[:, g:g + 1], in_=sd[:, g:g + 1], func=AF.Exp,
                                                                scale=-0.5), reads=[Bst1[g]], writes=[Bst1[g]])
                        yield
                        P.op("dve", lambda e, xt=xt, g=g, xnt=xnt: e.tensor_scalar(out=xnt[:, g, :], in0=xt[:],
                                                                                   scalar1=rstd[:, g:g + 1], scalar2=None,
                                                                                   op0=ALU.mult),
                             reads=[xb, Bst1[g]], pwrites=[xnb])
                        yield
                    return (s, off, t0, w, C, first, last, jlo, jhi, G, xnt, xnb, hTt, hTb, cst, snt, csb)

                def p1_front_b(cx):
                    (s, off, t0, w, C, first, last, jlo, jhi, G, xnt, xnb, hTt, hTb, cst, snt, csb) = cx
                    for cp in range(4):
                        tpt, tpb = tp_ring.next()
                        for c2 in range(2):
                            c = 2 * cp + c2
                            for g in range(G):
                                r = min(128, C - 128 * g)
                                P.op("pe", lambda e, tpt=tpt, g=g, r=r, c=c, c2=c2, xnt=xnt: e.transpose(
                                    tpt[:, c2, 128 * g:128 * g + r], xnt[0:r, g, c * 128:(c + 1) * 128], ident[0:r, 0:r]),
                                     reads=[xnb], pwrites=[tpb])
                        for c2 in range(2):
                            c = 2 * cp + c2
                            P.op("act", lambda e, tpt=tpt, c=c, c2=c2, hTt=hTt, s=s, C=C: e.activation(
                                out=hTt[:, c, 0:C], in_=tpt[:, c2, 0:C], func=AF.Identity, bias=sh1(s, c),
                                scale=gm1[:, s, c:c + 1]), reads=[tpb], pwrites=[hTb])

                class Stepper:
                    def __init__(self, gen):
                        self.gen = gen
                        self.done = gen is None
                        self.val = None

                    def step(self, n=1):
                        for _ in range(n):
                            if self.done:
                                return
                            try:
                                next(self.gen)
                            except StopIteration as e_:
                                self.done = True
                                self.val = e_.value

                    def finish(self):
                        while not self.done:
                            self.step()
                        return self.val

                def rstd_bc(stt, stb, n, scale):
                    rst, rsb = rs_ring.next()
                    P.op("act", lambda e: e.activation(out=rst[:, 0:n], in_=stt[:, 0:n], func=AF.Ln, bias=epsT[:], scale=scale),
                         reads=[stb], writes=[rsb])
                    rqt, rqb = rr_ring.next()
                    P.op("act", lambda e: e.activation(out=rqt[:, 0:n], in_=rst[:, 0:n], func=AF.Exp, scale=-0.5),
                         reads=[rsb], writes=[rqb])
                    return rqt, rqb

                def p1_back(cx, stp):
                    (s, off, t0, w, C, first, last, jlo, jhi, G, xnt, xnb, hTt, hTb, cst, snt, csb) = cx
                    def zgroup(col0, M, wsb=None, rot=False):
                        zt, zb = zp_ring.next()
                        if rot:
                            pairs = [(w_krr[:, kc, 0:64], hTt[:, kc, 0:C]) for kc in range(8)]
                        else:
                            pairs = [(w_in_sb[:, kc, col0:col0 + M], hTt[:, kc, 0:C]) for kc in range(8)]
                        P.mm(zt[0:M, 0:C], pairs, reads=[hTb] + WR, writes=[zb])
                        stp.step(2)
                        return zt, zb

                    for c in range(4):
                        zt, zb = zgroup(c * 128, 128)
                        P.op("act", lambda e, zt=zt, c=c, C=C: e.activation(out=hasb[:, c, 0:C], in_=zt[:, 0:C], func=AF.Copy),
                             reads=[zb], writes=[Bha[c]])
                    for c in range(4):
                        zt, zb = zgroup(1024 + c * 128, 128)
                        P.op("dve", lambda e, zt=zt, c=c, jlo=jlo, jhi=jhi: e.tensor_tensor(
                            out=psb[:, c, jlo:jhi], in0=zt[:, jlo:jhi], in1=hasb[:, c, jlo:jhi], op=ALU.mult),
                             reads=[zb, Bha[c]], pwrites=[Bp[c]])
                    if first:
                        P.op("pool", lambda e: e.memset(psb[:, :, 0:1], 0.0), pwrites=Bp)
                    if last:
                        P.op("pool", lambda e, C=C: e.memset(psb[:, :, C - 1:C], 0.0), pwrites=Bp)
                    while pend_conv:
                        pend_conv.pop(0)()
                    for c in range(4):
                        zt, zb = zgroup(512 + c * 128, 128)
                        P.op("act", lambda e, zt=zt, c=c, C=C: e.activation(out=basb[:, c, 0:C], in_=zt[:, 0:C], func=AF.Copy),
                             reads=[zb], writes=[Bba[c]])
                    sqs = []
                    for c in range(3):
                        zt, zb = zgroup(1536 + c * 128, 128)
                        P.op("act", lambda e, zt=zt, c=c, C=C: e.activation(out=cqsb[:, c, 0:C], in_=zt[:, 0:C], func=AF.Copy),
                             reads=[zb], pwrites=[Bx["cqsb"]])
                        sqt, sqb = sq_ring.next()
                        P.op("act", lambda e, zt=zt, sqt=sqt, C=C: e.activation(out=sqt[:, 0:C], in_=zt[:, 0:C], func=AF.Square),
                             reads=[zb], writes=[sqb])
                        sqs.append((sqt, sqb))
                    stt, stb = zp_ring.next()
                    P.mm(stt[:, 0:C], [(ones[:], q[0][:, 0:C]) for q in sqs], reads=[q[1] for q in sqs], writes=[stb])
                    rqt, rqb = rstd_bc(stt, stb, C, 1.0 / QL)
                    for c in range(3):
                        P.op("dve", lambda e, c=c, rqt=rqt, C=C: e.scalar_tensor_tensor(
                            out=cqn[:, c, 0:C], in0=cqsb[:, c, 0:C], scalar=qg[:, c:c + 1], in1=rqt[:, 0:C],
                            op0=ALU.mult, op1=ALU.mult), reads=[Bx["cqsb"], rqb], pwrites=[Bx["cqn"]])
                    sqs = []
                    for c in range(2):
                        zt, zb = zgroup(1920 + c * 128, 128)
                        P.op("act", lambda e, zt=zt, c=c, C=C: e.activation(out=ckvsb[:, c, 0:C], in_=zt[:, 0:C], func=AF.Copy),
                             reads=[zb], pwrites=[Bx["ckvsb"]])
                        sqt, sqb = sq_ring.next()
                        P.op("act", lambda e, zt=zt, sqt=sqt, C=C: e.activation(out=sqt[:, 0:C], in_=zt[:, 0:C], func=AF.Square),
                             reads=[zb], writes=[sqb])
                        sqs.append((sqt, sqb))
                    stt, stb = zp_ring.next()
                    P.mm(stt[:, 0:C], [(ones[:], q[0][:, 0:C]) for q in sqs], reads=[q[1] for q in sqs], writes=[stb])
                    rkt, rkb = rstd_bc(stt, stb, C, 1.0 / KVL)
                    for c in range(2):
                        P.op("dve", lambda e, c=c, rkt=rkt, C=C: e.scalar_tensor_tensor(
                            out=ckvn[:, c, 0:C], in0=ckvsb[:, c, 0:C], scalar=kvg[:, c:c + 1], in1=rkt[:, 0:C],
                            op0=ALU.mult, op1=ALU.mult), reads=[Bx["ckvsb"], rkb], pwrites=[Bx["ckvn"]])
                    za, zab = zgroup(2176, 64)
                    zr, zrb = zgroup(0, 64, rot=True)
                    t1t, t1b = t1_ring.next()
                    t2t, t2b = t2_ring.next()
                    P.op("dve", lambda e, za=za, t1t=t1t, cst=cst, C=C: e.tensor_tensor(
                        out=t1t[:, 0:C], in0=za[0:64, 0:C], in1=cst[:, 0:C], op=ALU.mult), reads=[zab, csb], writes=[t1b])
                    P.op("dve", lambda e, zr=zr, t2t=t2t, snt=snt, C=C: e.tensor_tensor(
                        out=t2t[:, 0:C], in0=zr[0:64, 0:C], in1=snt[:, 0:C], op=ALU.mult), reads=[zrb, csb], writes=[t2b])
                    P.op("dve", lambda e, t1t=t1t, t2t=t2t, C=C: e.tensor_tensor(
                        out=kro[:, 0:C], in0=t1t[:, 0:C], in1=t2t[:, 0:C], op=ALU.add), reads=[t1b, t2b], writes=[Bx["kro"]])
                    P.dma("pool", krS[:, off + t0:off + t0 + w], kro[:, 1:w + 1], ds_st["kr"], reads=[Bx["kro"]])
                    ncx = stp.finish()
                    if ncx is not None:
                        p1_front_b(ncx)
                    for h in range(NH):
                        zt, zb = zp_ring.next()
                        P.mm(zt[:, 0:C], [(w_uq_sb[:, kc, h * 192:h * 192 + 128], cqn[:, kc, 0:C]) for kc in range(3)],
                             reads=[Bx["cqn"]] + WR, writes=[zb])
                        P.op("act", lambda e, zt=zt, h=h, C=C: e.activation(out=qTo[:, h, 0:C], in_=zt[:, 0:C], func=AF.Copy),
                             reads=[zb], pwrites=[Bx["qTo"]])
                        za, zab = zp_ring.next()
                        P.mm(za[0:64, 0:C], [(w_uq_sb[:, kc, h * 192 + 128:h * 192 + 192], cqn[:, kc, 0:C]) for kc in range(3)],
                             reads=[Bx["cqn"]] + WR, writes=[zab])
                        zr, zrb = zp_ring.next()
                        P.mm(zr[0:64, 0:C], [(w_uqr[:, kc, h * 64:(h + 1) * 64], cqn[:, kc, 0:C]) for kc in range(3)],
                             reads=[Bx["cqn"]] + WR, writes=[zrb])
                        t1t, t1b = t1_ring.next()
                        t2t, t2b = t2_ring.next()
                        P.op("dve", lambda e, za=za, t1t=t1t, cst=cst, C=C: e.tensor_tensor(
                            out=t1t[:, 0:C], in0=za[0:64, 0:C], in1=cst[:, 0:C], op=ALU.mult), reads=[zab, csb], writes=[t1b])
                        P.op("dve", lambda e, zr=zr, t2t=t2t, snt=snt, C=C: e.tensor_tensor(
                            out=t2t[:, 0:C], in0=zr[0:64, 0:C], in1=snt[:, 0:C], op=ALU.mult), reads=[zrb, csb], writes=[t2b])
                        P.op("dve", lambda e, t1t=t1t, t2t=t2t, h=h, C=C: e.tensor_tensor(
                            out=qro[:, h, 0:C], in0=t1t[:, 0:C], in1=t2t[:, 0:C], op=ALU.add),
                             reads=[t1b, t2b], pwrites=[Bx["qro"]])
                    P.dma("pool", qS[:, :, off + t0:off + t0 + w].rearrange("h p t -> p h t"), qTo[:, :, 1:w + 1],
                          ds_st["q"], reads=[Bx["qTo"]])
                    P.dma("pool", qrS[:, :, off + t0:off + t0 + w].rearrange("h p t -> p h t"), qro[:, :, 1:w + 1],
                          ds_st["qr"], reads=[Bx["qro"]])
                    for h in range(NH):
                        zt, zb = zp_ring.next()
                        P.mm(zt[:, 0:C], [(w_uk_sb[:, kc, h * 128:(h + 1) * 128], ckvn[:, kc, 0:C]) for kc in range(2)],
                             reads=[Bx["ckvn"]] + WR, writes=[zb])
                        P.op("act", lambda e, zt=zt, h=h, C=C: e.activation(out=kTo[:, h, 0:C], in_=zt[:, 0:C], func=AF.Copy),
                             reads=[zb], pwrites=[Bx["kTo"]])
                    P.dma("pool", kS[:, :, off + t0:off + t0 + w].rearrange("h p t -> p h t"), kTo[:, :, 1:w + 1],
                          ds_st["k"], reads=[Bx["kTo"]])
                    for g in range(G):
                        r = min(128, C - 128 * g)
                        zt, zb = zp_ring.next()
                        P.mm(zt[0:r, :], [(ckvn[:, kc, 128 * g:128 * g + r], w_uv_sb[:, kc, :]) for kc in range(2)],
                             reads=[Bx["ckvn"]] + WR, writes=[zb])
                        P.op("dve", lambda e, zt=zt, g=g, r=r: e.tensor_copy(out=vo[0:r, g, :], in_=zt[0:r, :]),
                             reads=[zb], pwrites=[Bx["vo"]])
                    for g in range(G):
                        r = min(128, C - 128 * g)
                        lo = max(1, 128 * g) - 128 * g
                        hi = min(w + 1, 128 * g + r) - 128 * g
                        if hi <= lo:
                            continue
                        tok0 = off + t0 - 1 + 128 * g
                        P.dma("pool", vS[tok0 + lo:tok0 + hi, :], vo[lo:hi, g, :], ds_st["v"], reads=[Bx["vo"]])

                    ya = yasb
                    Bya = Byas
                    for c in range(4):
                        P.op("dve", lambda e, c=c, w=w: e.tensor_scalar(out=ya[:, c, 1:w + 1], in0=psb[:, c, 1:w + 1],
                                                                        scalar1=convA[:, 1, c:c + 1], scalar2=None,
                                                                        op0=ALU.mult), reads=[Bp[c]], writes=[Bya[c]])
                    for c in range(4):
                        P.op("dve", lambda e, c=c, w=w: e.scalar_tensor_tensor(
                            out=ya[:, c, 1:w + 1], in0=psb[:, c, 0:w], scalar=convA[:, 0, c:c + 1], in1=ya[:, c, 1:w + 1],
                            op0=ALU.mult, op1=ALU.add), reads=[Bp[c], Bya[c]], writes=[Bya[c]])
                    for c in range(4):
                        P.op("dve", lambda e, c=c, w=w: e.scalar_tensor_tensor(
                            out=ya[:, c, 1:w + 1], in0=psb[:, c, 2:w + 2], scalar=convA[:, 2, c:c + 1], in1=ya[:, c, 1:w + 1],
                            op0=ALU.mult, op1=ALU.add), reads=[Bp[c], Bya[c]], writes=[Bya[c]])
                    sqs = []
                    for c in range(4):
                        P.op("dve", lambda e, c=c, w=w: e.tensor_tensor(out=ya[:, c, 1:w + 1], in0=ya[:, c, 1:w + 1],
                                                                        in1=basb[:, c, 1:w + 1], op=ALU.mult),
                             reads=[Bba[c], Bya[c]], writes=[Bya[c]])
                    for c in range(4):
                        sqt, sqb = sq_ring.next()
                        P.op("act", lambda e, c=c, w=w, sqt=sqt: e.activation(out=sqt[:, 0:w], in_=ya[:, c, 1:w + 1],
                                                                              func=AF.Square), reads=[Bya[c]], writes=[sqb])
                        sqs.append((sqt, sqb))
                    def conv_b(sqs=sqs, w=w, off=off, t0=t0):
                        stt, stb = zp_ring.next()
                        P.mm(stt[:, 0:w], [(ones[:], q[0][:, 0:w]) for q in sqs], reads=[q[1] for q in sqs], writes=[stb])
                        rat, rab = rstd_bc(stt, stb, w, 1.0 / A_W)
                        for c in range(4):
                            P.op("dve", lambda e, c=c, rat=rat, w=w: e.scalar_tensor_tensor(
                                out=yan[:, c, 0:w], in0=yasb[:, c, 1:w + 1], scalar=ag[:, c:c + 1], in1=rat[:, 0:w],
                                op0=ALU.mult, op1=ALU.mult), reads=[Byas[c], rab], pwrites=[Bx["yan"]])
                        P.dma("pool", mixS[0:4, :, off + t0:off + t0 + w].rearrange("c p t -> p c t"), yan[:, :, 0:w],
                              ds_st["ya"], reads=[Bx["yan"]])

                    pend_conv.append(conv_b)
                    return ncx

                tiles_all = []
                for s in range(NS):
                    tl = seq_tiles(seqs[s])
                    for ti, (t0, w) in enumerate(tl):
                        tiles_all.append((s, ti, t0, w, len(tl)))
                st0 = Stepper(p1_front_a(*tiles_all[0]))
                cx = st0.finish()
                p1_front_b(cx)
                for n_ in range(len(tiles_all)):
                    stp = Stepper(p1_front_a(*tiles_all[n_ + 1]) if n_ + 1 < len(tiles_all) else None)
                    cx = p1_back(cx, stp)
                while pend_conv:
                    pend_conv.pop(0)()
                P.emit_phase()
                if DEBUG_STOP == 1:
                    P.disabled = True

            with ExitStack() as ph:
                def sb(name, shape, dt):
                    return ph.enter_context(nc.sbuf_tensor(name, list(shape), dt))

                def pst(name, shape, dt):
                    return ph.enter_context(nc.psum_tensor(name, list(shape), dt))

                kT = sb("kT", [128, NH, SMAX], BF16)
                krT = sb("krT", [128, SMAX], BF16)
                V = sb("V", [128, SMAX // 128, NH * DV], BF16)
                qT = [sb("qT%d" % i, [128, NH, 512], BF16) for i in range(2)]
                qrT = [sb("qrT%d" % i, [128, NH, 512], BF16) for i in range(2)]
                pT = [sb("pT%d" % i, [128, 512], BF16) for i in range(6)]
                acc = [sb("acc%d" % i, [128, 512], F32) for i in range(4)]
                yb = sb("yb", [128, NH, 512], F32)
                rec = [sb("rec%d" % i, [128, 512], F32) for i in range(2)]
                sq = [sb("sq2_%d" % i, [128, 512], BF16) for i in range(4)]
                rs2 = sb("rs2", [128, 512], F32)
                rb = sb("rb", [128, 512], F32)
                ybn = [sb("ybn%d" % i, [128, NH, 512], BF16) for i in range(2)]
                sps = [pst("sps%d" % i, [128, 512], F32) for i in range(6)]
                ops_ = [pst("ops%d" % i, [128, 512], F32) for i in range(2)]
                s_ring = Ring(sps)
                o_ring = Ring(ops_)
                acc_ring = Ring([(acc[0], acc[1]), (acc[2], acc[3])])
                pT_ring = Ring(pT)
                rec_ring = Ring(rec)
                sq_ring = Ring(sq)
                q_ds = [P.dsem("q2_%d" % i) for i in range(2)]
                q_ring = Ring(list(zip(qT, qrT, q_ds)))
                ybn_ds = [P.dsem("ybn%d" % i) for i in range(2)]
                ybn_ring = Ring(list(zip(ybn, ybn_ds)))
                ds_kv = P.dsem("kv")
                Bkv = Buf()
                P.op("pool", lambda e: e.memset(krT[64:128, :], 0.0), writes=[Bkv])
                for i in range(2):
                    P.op("pool", lambda e, i=i: e.memset(qrT[i][64:128, :, :], 0.0), writes=[q_ring.bufs[i]])
                Byb = [Buf() for _ in range(NH)]
                Brs = Buf()
                Brb = Buf()

                for s in range(NS):
                    S = seqs[s]
                    off = offs[s]
                    NKC = S // 128
                    P.dma("sp", kT[:, :, 0:S], kS[:, :, off:off + S].rearrange("h p t -> p h t"), ds_kv, pwrites=[Bkv])
                    P.dma("sp", krT[0:64, 0:S], krS[:, off:off + S], ds_kv, pwrites=[Bkv])
                    vsrc = vS[off:off + S, :].rearrange("(c p) d -> p c d", p=128)
                    step = 16
                    for c0 in range(0, NKC, step):
                        c1 = min(NKC, c0 + step)
                        P.dma("sp", V[:, c0:c1, :], vsrc[:, c0:c1, :], ds_kv, pwrites=[Bkv])
                    for qi in range(S // 512):
                        q0 = qi * 512
                        (qt, qrt, qd), qb = q_ring.next()
                        P.dma("sp", qt[:], qS[:, :, off + q0:off + q0 + 512].rearrange("h p t -> p h t"), qd, writes=[qb])
                        P.dma("sp", qrt[0:64, :, :], qrS[:, :, off + q0:off + q0 + 512].rearrange("h p t -> p h t"), qd,
                              pwrites=[qb])
                        sqs = []
                        for h in range(NH):
                            ot, ob = o_ring.next()
                            (accA, accB), accb_ = acc_ring.next()
                            BaccA = Buf()
                            BaccB = Buf()
                            for k_, v_ in list(accb_.r.items()) + list(accb_.w.items()):
                                _mg(BaccA.r, k_, v_)
                                _mg(BaccB.r, k_, v_)
                            pend = []

                            def qk(kc):
                                st_, stb_ = s_ring.next()
                                P.mm(st_[:], [(kT[:, h, kc * 128:(kc + 1) * 128], qt[:, h, :]),
                                              (krT[:, kc * 128:(kc + 1) * 128], qrt[:, h, :])], reads=[Bkv, qb], writes=[stb_])
                                pt_, ptb_ = pT_ring.next()
                                P.op("act", lambda e, st_=st_, pt_=pt_: e.activation(out=pt_[:], in_=st_[:], func=AF.Exp),
                                     reads=[stb_], writes=[ptb_])
                                pend.append((kc, pt_, ptb_))

                            def pv():
                                kc, pt_, ptb_ = pend.pop(0)
                                f = (kc == 0)
                                l = (kc == NKC - 1)
                                if f:
                                    P.mm(ot[:], [(V[:, kc, h * DV:(h + 1) * DV], pt_[:])], reads=[Bkv, ptb_], writes=[ob],
                                         first=f, last=l)
                                else:
                                    P.mm(ot[:], [(V[:, kc, h * DV:(h + 1) * DV], pt_[:])], reads=[Bkv, ptb_], pwrites=[ob],
                                         first=f, last=l)
                                at_, ab_ = (accA, BaccA) if kc % 2 == 0 else (accB, BaccB)
                                if kc < 2:
                                    P.op("dve", lambda e, at_=at_, pt_=pt_: e.tensor_copy(out=at_[:], in_=pt_[:]),
                                         reads=[ptb_], writes=[ab_])
                                else:
                                    P.op("dve", lambda e, at_=at_, pt_=pt_: e.tensor_tensor(out=at_[:], in0=at_[:], in1=pt_[:],
                                                                                          op=ALU.add),
                                         reads=[ptb_, ab_], writes=[ab_])

                            LOOK = 3
                            for kc in range(min(LOOK, NKC)):
                                qk(kc)
                            for kc in range(NKC):
                                if kc + LOOK < NKC:
                                    qk(kc + LOOK)
                                pv()
                            if NKC > 1:
                                P.op("dve", lambda e, accA=accA, accB=accB: e.tensor_tensor(out=accA[:], in0=accA[:], in1=accB[:],
                                                                                          op=ALU.add),
                                     reads=[BaccA, BaccB], writes=[BaccA])
                            mt, mb = s_ring.next()
                            P.mm(mt[:], [(onesf[:], accA[:])], reads=[BaccA], writes=[mb])
                            for k_, v_ in list(BaccA.r.items()) + list(BaccA.w.items()) + list(BaccB.r.items()) + list(BaccB.w.items()):
                                _mg(accb_.r, k_, v_)
                            ret, reb = rec_ring.next()
                            P.op("dve", lambda e, mt=mt, ret=ret: e.reciprocal(out=ret[:], in_=mt[:]), reads=[mb], writes=[reb])
                            P.op("dve", lambda e, ot=ot, ret=ret, h=h: e.tensor_tensor(out=yb[:, h, :], in0=ot[:], in1=ret[:],
                                                                                      op=ALU.mult),
                                 reads=[ob, reb], writes=[Byb[h]])
                            sqt, sqb = sq_ring.next()
                            P.op("act", lambda e, sqt=sqt, h=h: e.activation(out=sqt[:], in_=yb[:, h, :], func=AF.Square),
                                 reads=[Byb[h]], writes=[sqb])
                            sqs.append((sqt, sqb))
                        stt, stb = s_ring.next()
                        P.mm(stt[:], [(ones[:], q[0][:]) for q in sqs], reads=[q[1] for q in sqs], writes=[stb])
                        P.op("act", lambda e, stt=stt: e.activation(out=rs2[:], in_=stt[:], func=AF.Sqrt, bias=epsT[:],
                                                                   scale=1.0 / A_W), reads=[stb], writes=[Brs])
                        P.op("dve", lambda e: e.reciprocal(out=rb[:], in_=rs2[:]), reads=[Brs], writes=[Brb])
                        (ybt, ybd), ybb = ybn_ring.next()
                        for h in range(NH):
                            P.op("dve", lambda e, h=h, ybt=ybt: e.scalar_tensor_tensor(
                                out=ybt[:, h, :], in0=yb[:, h, :], scalar=bg[:, h:h + 1], in1=rb[:], op0=ALU.mult, op1=ALU.mult),
                                 reads=[Byb[h], Brb], pwrites=[ybb])
                        P.dma("pool", mixS[4:8, :, off + q0:off + q0 + 512].rearrange("c p t -> p c t"), ybt[:], ybd, reads=[ybb])
                P.emit_phase()
                if DEBUG_STOP == 2:
                    P.disabled = True

            with ExitStack() as ph:
                def sb(name, shape, dt):
                    return ph.enter_context(nc.sbuf_tensor(name, list(shape), dt))

                def pst(name, shape, dt):
                    return ph.enter_context(nc.psum_tensor(name, list(shape), dt))

                w_o_sb = sb("w_o_sb", [128, 8, D], BF16)
                w_dn_sb = sb("w_dn_sb", [128, NU, D], BF16)
                wup = [sb("wup%d" % i, [128, 8, 256], BF16) for i in range(4)]
                g1bc = sb("g1bc", [128, D], F32)
                g2bc = sb("g2bc", [128, D], F32)
                fgbc = sb("fgbc", [128, D], F32)
                mixT = sb("mixT", [128, 8, 512], BF16)
                xin = [sb("x3in%d" % i, [128, D], F32) for i in range(2)]
                x1 = sb("x1", [128, 4, D], F32)
                x1b = sb("x1b", [128, 4, D], F32)
                xn2 = sb("xn2", [128, 4, D], BF16)
                h2T = sb("h2T", [128, 8, 512], BF16)
                aT = sb("aT", [128, NU, 512], BF16)
                c1 = [sb("c1_%d" % i, [128, 512], F32) for i in range(3)]
                c2 = [sb("c2_%d" % i, [128, 512], F32) for i in range(4)]
                sg = [sb("sg%d" % i, [128, 512], F32) for i in range(2)]
                tmp = [sb("tmp%d" % i, [128, 512], F32) for i in range(2)]
                junk = sb("junk3", [128, D], BF16)
                ms = sb("ms3", [128, 8], F32)
                sd = sb("sd3", [128, 8], F32)
                rstd = sb("rstd3", [128, 8], F32)
                tp = [pst("tp3_%d" % i, [128, 2, 512], BF16) for i in range(2)]
                zp = [pst("zp3_%d" % i, [128, 512], F32) for i in range(6)]
                tp_ring = Ring(tp)
                zp_ring = Ring(zp)
                c1_ring = Ring(c1)
                c2_ring = Ring(c2)
                sg_ring = Ring(sg)
                tmp_ring = Ring(tmp)
                wup_ds = [P.dsem("wup%d" % i) for i in range(4)]
                wup_ring = Ring(list(zip(wup, wup_ds)))
                xin_ds = [P.dsem("x3in%d" % i) for i in range(2)]
                xin_ring = Ring(list(zip(xin, xin_ds)))
                ds_w = P.dsem("p3w")
                ds_g = P.dsem("p3g")
                ds_g2 = P.dsem("p3g2")
                ds_mix = P.dsem("mix")
                ds_out = [P.dsem("out%d" % g) for g in range(4)]
                ds_outb = [P.dsem("outb%d" % g) for g in range(4)]
                Bw = Buf()
                Bg = Buf()
                Bfg = Buf()
                Bmix = Buf()
                Bx1 = [Buf() for _ in range(4)]
                Bxn2 = Buf()
                Bh2 = Buf()
                BaT = Buf()
                Bst = [Buf() for _ in range(4)]
                Bst2 = [Buf() for _ in range(4)]

                class Stepper3:
                    def __init__(self, gen):
                        self.gen = gen
                        self.done = gen is None
                        self.val = None

                    def step(self, n=1):
                        for _ in range(n):
                            if self.done:
                                return
                            try:
                                next(self.gen)
                            except StopIteration as e_:
                                self.done = True
                                self.val = e_.value

                    def finish(self):
                        while not self.done:
                            self.step()
                        return self.val

                P.dma("pool", w_o_sb[:], w_o_d.rearrange("(kc p) n -> p kc n", p=128), ds_w, pwrites=[Bw])
                P.dma("pool", w_dn_sb[:, 0:11, :], w_dn_d[0:11 * 128, :].rearrange("(kc p) n -> p kc n", p=128), ds_w, pwrites=[Bw])
                P.dma("pool", w_dn_sb[:, 11:22, :], w_dn_d[11 * 128:22 * 128, :].rearrange("(kc p) n -> p kc n", p=128), ds_w,
                      pwrites=[Bw])
                P.dma("sp", fgbc[:], fg_d.to_broadcast([128, D]), ds_g, writes=[Bfg])
                P.op("pool", lambda e: e.memset(mixT[:], 0.0), writes=[Bmix])
                for i in range(2):
                    P.op("pool", lambda e, i=i: e.memset(xin[i][:], 0.0), writes=[xin_ring.bufs[i]])
                P.op("pool", lambda e: e.memset(aT[:], 0.0), writes=[BaT])

                Bg1 = Buf()
                Bg2 = Buf()
                x1s = [x1, x1b]
                Bx1s = [[Buf() for _ in range(4)] for _ in range(2)]
                x1_i = [0]

                def p3_A1(s, ti, t0, w, ntl):
                    off = offs[s]
                    C = w + 2
                    first = (ti == 0)
                    last = (ti == ntl - 1)
                    jlo = 1 if first else 0
                    jhi = C - 1 if last else C
                    G = -(-C // 128)
                    tokc = off + t0 - 1
                    x1t = x1s[x1_i[0] % 2]
                    Bx1 = Bx1s[x1_i[0] % 2]
                    x1_i[0] += 1
                    if first:
                        P.dma("sp", g1bc[:], modS[s:s + 1, 2 * D:3 * D].to_broadcast([128, D]), ds_g, writes=[Bg1])
                    P.dma("sp", mixT[:, :, jlo:jhi], mixS[:, :, tokc + jlo:tokc + jhi].rearrange("c p t -> p c t"), ds_mix,
                          writes=[Bmix])
                    yield
                    for g in range(G):
                        r = min(128, C - 128 * g)
                        lo = max(jlo, 128 * g) - 128 * g
                        hi = min(jhi, 128 * g + r) - 128 * g
                        (xt, xd), xb = xin_ring.next()
                        tok0 = tokc + 128 * g
                        P.dma("sp", xt[lo:hi, :], x_d[tok0 + lo:tok0 + hi, :], xd, writes=[xb])
                        yield
                        for hf in range(2):
                            zt, zb = zp_ring.next()
                            P.mm(zt[0:r, :], [(mixT[:, c, 128 * g:128 * g + r], w_o_sb[:, c, hf * 512:(hf + 1) * 512])
                                              for c in range(8)], reads=[Bmix, Bw], writes=[zb])
                            yield
                            tt, tb = tmp_ring.next()
                            P.op("dve", lambda e, zt=zt, tt=tt, r=r, hf=hf: e.tensor_tensor(
                                out=tt[0:r, :], in0=zt[0:r, :], in1=g1bc[0:r, hf * 512:(hf + 1) * 512], op=ALU.mult),
                                 reads=[zb, Bg1], writes=[tb])
                            yield
                            P.op("dve", lambda e, tt=tt, xt=xt, r=r, hf=hf, g=g: e.tensor_tensor(
                                out=x1t[0:r, g, hf * 512:(hf + 1) * 512], in0=tt[0:r, :], in1=xt[0:r, hf * 512:(hf + 1) * 512],
                                op=ALU.add), reads=[tb, xb], pwrites=[Bx1[g]])
                            yield
                        P.op("act", lambda e, g=g: e.activation(out=junk[:], in_=x1t[:, g, :], func=AF.Square, scale=1.0 / 32.0,
                                                                accum_out=ms[:, g:g + 1]), reads=[Bx1[g]], writes=[Bst[g]])
                        yield
                        P.op("act", lambda e, g=g: e.activation(out=sd[:, g:g + 1], in_=ms[:, g:g + 1], func=AF.Sqrt,
                                                                bias=epsT[:], scale=1.0), reads=[Bst[g]], writes=[Bst[g]])
                        yield
                        P.op("dve", lambda e, g=g: e.reciprocal(out=rstd[:, g:g + 1], in_=sd[:, g:g + 1]),
                             reads=[Bst[g]], writes=[Bst[g]])
                        yield
                        P.op("act", lambda e, g=g: e.activation(out=xn2[:, g, :], in_=x1t[:, g, :], func=AF.Identity,
                                                                scale=rstd[:, g:g + 1]), reads=[Bx1[g], Bst[g]], pwrites=[Bxn2])
                        yield
                    return (s, off, t0, w, C, first, last, jlo, jhi, G, tokc, x1t, Bx1)

                def p3_A2(cx):
                    (s, off, t0, w, C, first, last, jlo, jhi, G, tokc, x1t, Bx1) = cx
                    for cp in range(4):
                        tpt, tpb = tp_ring.next()
                        for c2 in range(2):
                            c = 2 * cp + c2
                            for g in range(G):
                                r = min(128, C - 128 * g)
                                P.op("pe", lambda e, tpt=tpt, g=g, r=r, c=c, c2=c2: e.transpose(
                                    tpt[:, c2, 128 * g:128 * g + r], xn2[0:r, g, c * 128:(c + 1) * 128], ident[0:r, 0:r]),
                                     reads=[Bxn2], pwrites=[tpb])
                            yield
                        for c2 in range(2):
                            c = 2 * cp + c2
                            P.op("act", lambda e, tpt=tpt, c=c, c2=c2, s=s, jlo=jlo, jhi=jhi: e.activation(
                                out=h2T[:, c, jlo:jhi], in_=tpt[:, c2, jlo:jhi], func=AF.Identity, bias=sh2(s, c),
                                scale=gm2[:, s, c:c + 1]), reads=[tpb], pwrites=[Bh2])
                            yield
                    if first:
                        P.op("pool", lambda e: e.memset(h2T[:, :, 0:1], 0.0), pwrites=[Bh2])
                    if last:
                        P.op("pool", lambda e, C=C: e.memset(h2T[:, :, C - 1:C], 0.0), pwrites=[Bh2])
                    return None

                def p3_B(cx, stp):
                    (s, off, t0, w, C, first, last, jlo, jhi, G, tokc, x1t, Bx1) = cx
                    if first:
                        P.dma("sp", g2bc[:], modS[s:s + 1, 5 * D:6 * D].to_broadcast([128, D]), ds_g2, writes=[Bg2])
                    for u in range(NU):
                        (wt, wd), wb = wup_ring.next()
                        P.dma("sp", wt[:], wupS[u], wd, writes=[wb])
                        zg, zgb = zp_ring.next()
                        P.mm(zg[:, 0:C], [(wt[:, kc, 0:128], h2T[:, kc, 0:C]) for kc in range(8)], reads=[wb, Bh2], writes=[zgb])
                        zv, zvb = zp_ring.next()
                        P.mm(zv[:, 0:C], [(wt[:, kc, 128:256], h2T[:, kc, 0:C]) for kc in range(8)], reads=[wb, Bh2], writes=[zvb])
                        outs = []
                        for (zt, zb, ch) in ((zg, zgb, u), (zv, zvb, NU + u)):
                            c1t, c1b = c1_ring.next()
                            P.op("act", lambda e, zt=zt, c1t=c1t, ch=ch, w=w: e.activation(
                                out=c1t[:, 0:w], in_=zt[:, 1:w + 1], func=AF.Identity, scale=convF[:, 1, ch:ch + 1]),
                                 reads=[zb], writes=[c1b])
                            c2t, c2b = c2_ring.next()
                            P.op("dve", lambda e, zt=zt, c1t=c1t, c2t=c2t, ch=ch, w=w: e.scalar_tensor_tensor(
                                out=c2t[:, 0:w], in0=zt[:, 0:w], scalar=convF[:, 0, ch:ch + 1], in1=c1t[:, 0:w],
                                op0=ALU.mult, op1=ALU.add), reads=[zb, c1b], writes=[c2b])
                            P.op("dve", lambda e, zt=zt, c2t=c2t, ch=ch, w=w: e.scalar_tensor_tensor(
                                out=c2t[:, 0:w], in0=zt[:, 2:w + 2], scalar=convF[:, 2, ch:ch + 1], in1=c2t[:, 0:w],
                                op0=ALU.mult, op1=ALU.add), reads=[zb, c2b], writes=[c2b])
                            outs.append((c2t, c2b))
                        sgt, sgb = sg_ring.next()
                        P.op("act", lambda e, sgt=sgt, c2t=outs[0][0], w=w: e.activation(out=sgt[:, 0:w], in_=c2t[:, 0:w],
                                                                                          func=AF.Silu),
                             reads=[outs[0][1]], writes=[sgb])
                        P.op("pool", lambda e, sgt=sgt, c2t=outs[1][0], u=u, w=w: e.tensor_tensor(
                            out=aT[:, u, 1:w + 1], in0=sgt[:, 0:w], in1=c2t[:, 0:w], op=ALU.mult),
                             reads=[sgb, outs[1][1]], pwrites=[BaT])
                        if u >= 2:
                            stp.step(3)

                def p3_C(cx, stp):
                    (s, off, t0, w, C, first, last, jlo, jhi, G, tokc, x1t, Bx1) = cx
                    for g in range(G):
                        r = min(128, C - 128 * g)
                        lo = max(1, 128 * g) - 128 * g
                        hi = min(w + 1, 128 * g + r) - 128 * g
                        for hf in range(2):
                            zt, zb = zp_ring.next()
                            P.mm(zt[0:r, :], [(aT[:, u, 128 * g:128 * g + r], w_dn_sb[:, u, hf * 512:(hf + 1) * 512])
                                              for u in range(NU)], reads=[BaT, Bw], writes=[zb])
                            stp.step(6)
                            tt, tb = tmp_ring.next()
                            P.op("dve", lambda e, zt=zt, tt=tt, r=r, hf=hf: e.tensor_tensor(
                                out=tt[0:r, :], in0=zt[0:r, :], in1=g2bc[0:r, hf * 512:(hf + 1) * 512], op=ALU.mult),
                                 reads=[zb, Bg2], writes=[tb])
                            P.op("dve", lambda e, tt=tt, r=r, hf=hf, g=g: e.tensor_tensor(
                                out=x1t[0:r, g, hf * 512:(hf + 1) * 512], in0=tt[0:r, :], in1=x1t[0:r, g, hf * 512:(hf + 1) * 512],
                                op=ALU.add), reads=[tb, Bx1[g]], writes=[Bx1[g]])
                        P.op("act", lambda e, g=g: e.activation(out=junk[:], in_=x1t[:, g, :], func=AF.Square, scale=1.0 / 32.0,
                                                                accum_out=ms[:, 4 + g:5 + g]), reads=[Bx1[g]], writes=[Bst2[g]])
                        P.op("act", lambda e, g=g: e.activation(out=sd[:, 4 + g:5 + g], in_=ms[:, 4 + g:5 + g], func=AF.Sqrt,
                                                                bias=epsT[:], scale=1.0), reads=[Bst2[g]], writes=[Bst2[g]])
                        P.op("dve", lambda e, g=g: e.reciprocal(out=rstd[:, 4 + g:5 + g], in_=sd[:, 4 + g:5 + g]),
                             reads=[Bst2[g]], writes=[Bst2[g]])
                        P.op("dve", lambda e, g=g: e.scalar_tensor_tensor(
                            out=x1t[:, g, :], in0=x1t[:, g, :], scalar=rstd[:, 4 + g:5 + g], in1=fgbc[:], op0=ALU.mult, op1=ALU.mult),
                             reads=[Bst2[g], Bfg, Bx1[g]], writes=[Bx1[g]])
                        if hi > lo:
                            tok0 = tokc + 128 * g
                            P.dma("pool", y_d[tok0 + lo:tok0 + hi, :], x1t[lo:hi, g, :],
                                  (ds_out if x1t is x1 else ds_outb)[g], reads=[Bx1[g]])

                tiles3 = []
                for s in range(NS):
                    tl = seq_tiles(seqs[s])
                    for ti, (t0, w) in enumerate(tl):
                        tiles3.append((s, ti, t0, w, len(tl)))
                cx = Stepper3(p3_A1(*tiles3[0])).finish()
                Stepper3(p3_A2(cx)).finish()
                for n_ in range(len(tiles3)):
                    stA1 = Stepper3(p3_A1(*tiles3[n_ + 1]) if n_ + 1 < len(tiles3) else None)
                    p3_B(cx, stA1)
                    ncx = stA1.finish()
                    stA2 = Stepper3(p3_A2(ncx) if ncx is not None else None)
                    p3_C(cx, stA2)
                    stA2.finish()
                    cx = ncx

                P.emit_phase()
                if DEBUG_STOP == 3:
                    P.disabled = True
        except _Stop:
            pass
        P.final_wait()
    return nc


def rope_tables_np(smax):
    inv = (1.0 / (np.float32(THETA) ** (np.arange(0, DR, 2, dtype=np.float32) / np.float32(DR)))).astype(np.float32)
    ang = (np.arange(-1, smax + 1, dtype=np.float32)[None, :] * inv[:, None]).astype(np.float32)
    cos = np.cos(ang).astype(np.float32)
    sin = np.sin(ang).astype(np.float32)
    return np.concatenate([cos, cos], 0), np.concatenate([sin, sin], 0)


_CACHE = {}


def run_cores(seqs, xs, cs, weights):
    key = tuple(seqs)
    if key not in _CACHE:
        _CACHE[key] = build_program(list(seqs))
    nc = _CACHE[key]
    cos2, sin2 = rope_tables_np(max(seqs))
    shared = dict(weights)
    shared["cos2"] = cos2
    shared["sin2"] = sin2
    in_maps = []
    for i in range(len(xs)):
        m = dict(shared)
        m["x"] = np.ascontiguousarray(xs[i], dtype=np.float32)
        m["c"] = np.ascontiguousarray(cs[i], dtype=np.float32)
        in_maps.append(m)
    res = run_bass_kernel_spmd(nc, in_maps, core_ids=list(range(len(xs))))
    return [r["y"] for r in res.results]


def prep_weights(w_ada, b_ada, norm1_g, w_in, conv_a_w, q_norm_g, w_uq, kv_norm_g, w_ukv, out_norm_a_g, out_norm_b_g,
                 w_o, norm2_g, w_up, ffn_conv_w, w_down, final_g):
    f = lambda a: np.ascontiguousarray(np.asarray(a, dtype=np.float32))
    return dict(
        w_ada=f(w_ada[0]), b_ada=f(b_ada[0]).reshape(1, -1), norm1_g=f(norm1_g[0]).reshape(1, -1), w_in=f(w_in[0]),
        conv_a_w=f(conv_a_w[0]), q_norm_g=f(q_norm_g[0]).reshape(1, -1), w_uq=f(w_uq[0]),
        kv_norm_g=f(kv_norm_g[0]).reshape(1, -1), w_ukv=f(w_ukv[0]), out_norm_a_g=f(out_norm_a_g[0]).reshape(1, -1),
        out_norm_b_g=f(out_norm_b_g[0]).reshape(1, -1), w_o=f(w_o[0]), norm2_g=f(norm2_g[0]).reshape(1, -1),
        w_up=f(w_up[0]), ffn_conv_w=f(ffn_conv_w[0]), w_down=f(w_down[0]), final_g=f(final_g).reshape(1, -1))


def kernel(x_prompt, x_sample, c_prompt, c_sample, w_ada, b_ada, norm1_g, w_in, conv_a_w, q_norm_g, w_uq, kv_norm_g,
           w_ukv, out_norm_a_g, out_norm_b_g, w_o, norm2_g, w_up, ffn_conv_w, w_down, final_g):
    x_prompt = np.asarray(x_prompt, dtype=np.float32)
    x_sample = np.asarray(x_sample, dtype=np.float32)
    c_prompt = np.asarray(c_prompt, dtype=np.float32)
    c_sample = np.asarray(c_sample, dtype=np.float32)
    weights = prep_weights(w_ada, b_ada, norm1_g, w_in, conv_a_w, q_norm_g, w_uq, kv_norm_g, w_ukv, out_norm_a_g,
                           out_norm_b_g, w_o, norm2_g, w_up, ffn_conv_w, w_down, final_g)
    SS = x_sample.shape[1]
    SP = x_prompt.shape[1]
    seqs = (SS, SP, SP)
    xs, cs = [], []
    for i in range(N_CORES):
        xs.append(np.concatenate([x_sample[i], x_prompt[2 * i], x_prompt[2 * i + 1]], axis=0))
        cs.append(np.stack([c_sample[i], c_prompt[2 * i], c_prompt[2 * i + 1]], axis=0))
    ys = run_cores(seqs, xs, cs, weights)
    y_prompt = np.empty_like(x_prompt)
    y_sample = np.empty_like(x_sample)
    for i in range(N_CORES):
        y = ys[i]
        y_sample[i] = y[0:SS]
        y_prompt[2 * i] = y[SS:SS + SP]
        y_prompt[2 * i + 1] = y[SS + SP:SS + 2 * SP]
    return (y_prompt, y_sample)
```

```python
import math
from contextlib import ExitStack

import numpy as np
import concourse.bass as bass
import concourse.mybir as mybir
from concourse.bass_utils import run_bass_kernel_spmd

F32 = mybir.dt.float32
BF16 = mybir.dt.bfloat16
AF = mybir.ActivationFunctionType
ALU = mybir.AluOpType

D = 1024
A_W = 512
NH = 4
DN = 128
DR = 64
DV = 128
QL = 384
KVL = 256
INC = 2240
DFF = 2816
NU = DFF // 128
ATTN_SCALE = 1.0 / math.sqrt(DN + DR)
EPS = 1e-6
THETA = 10000.0
N_CORES = 8
SAME_ENGINE_SYNC = True

ENGS = ["pe", "act", "dve", "pool", "sp"]


class Buf:
    __slots__ = ("w", "r", "war")

    def __init__(self):
        self.w = {}
        self.r = {}
        self.war = {}


def _mg(d, k, v):
    if d.get(k, -1) < v:
        d[k] = v


class DSem:
    __slots__ = ("h", "val", "val0")

    def __init__(self, h):
        self.h = h
        self.val = 0
        self.val0 = 0


class Op:
    __slots__ = ("eng", "idx", "fn", "deps", "sig", "dsem", "dval")


class Prog:
    def __init__(self, nc, es):
        self.nc = nc
        self.es = es
        self.eh = dict(pe=nc.tensor, act=nc.scalar, dve=nc.vector, pool=nc.gpsimd, sp=nc.sync)
        self.psem = {e: es.enter_context(nc.semaphore("pg_" + e)) for e in ENGS}
        self.cnt = {e: 0 for e in ENGS}
        self.waited = {e: {} for e in ENGS}
        self.ops = {e: [] for e in ENGS}
        self.dsems = []
        self.first_phase = True
        self.disabled = False

    def dsem(self, name):
        d = DSem(self.es.enter_context(self.nc.semaphore("d_" + name)))
        self.dsems.append(d)
        return d

    def op(self, eng, fn, reads=(), writes=(), pwrites=(), dsem=None):
        if self.disabled:
            return None
        self.nops = getattr(self, 'nops', 0) + 1
        if self.nops > DEBUG_MAXOPS:
            return None
        deps = {}
        for b in reads:
            for k, v in b.w.items():
                _mg(deps, k, v)
        for b in writes:
            war = {}
            for k, v in b.r.items():
                _mg(war, k, v)
            for k, v in b.w.items():
                _mg(war, k, v)
            b.war = war
            for k, v in war.items():
                _mg(deps, k, v)
        newver = []
        for b in pwrites:
            if b.r or not b.w:
                war = {}
                for k, v in b.r.items():
                    _mg(war, k, v)
                for k, v in b.w.items():
                    _mg(war, k, v)
                b.war = war
                newver.append(b)
            for k, v in b.war.items():
                _mg(deps, k, v)
        o = Op()
        o.eng = eng
        o.idx = len(self.ops[eng])
        o.fn = fn
        o.deps = deps
        o.sig = False
        o.dsem = dsem
        if dsem is not None:
            dsem.val += 16
            o.dval = dsem.val
            key, val = dsem, dsem.val
        else:
            o.dval = 0
            key, val = eng, o.idx
        self.ops[eng].append(o)
        for k, v in deps.items():
            if isinstance(k, str):
                self.ops[k][v].sig = True
        for b in reads:
            _mg(b.r, key, val)
        for b in writes:
            b.w = {key: val}
            b.r = {}
        for b in newver:
            b.w = {}
            b.r = {}
        for b in pwrites:
            _mg(b.w, key, val)
        return o

    def dma(self, q, out, in_, dsem, reads=(), writes=(), pwrites=(), nonc=False):
        nc = self.nc

        def fn(e):
            if nonc:
                with nc.allow_non_contiguous_dma("small strided setup load"):
                    return e.dma_start(out=out, in_=in_)
            return e.dma_start(out=out, in_=in_)

        return self.op(q, fn, reads, writes, pwrites, dsem=dsem)

    def mm(self, out, pairs, reads=(), writes=(), pwrites=(), first=True, last=True):
        def fn(e):
            n = len(pairs)
            ins = None
            for i, (l, r) in enumerate(pairs):
                ins = e.matmul(out, l, r, start=(first and i == 0), stop=(last and i == n - 1))
            return ins

        return self.op("pe", fn, reads, writes, pwrites)

    def emit_phase(self):
        if self.disabled:
            return
        nc = self.nc
        for e in ENGS:
            for o in reversed(self.ops[e]):
                if o.dsem is None:
                    o.sig = True
                    break
        signo = {}
        for e in ENGS:
            c = self.cnt[e]
            lst = []
            for o in self.ops[e]:
                if o.sig and o.dsem is None:
                    c += 1
                lst.append(c)
            signo[e] = lst
        fence = None
        if not self.first_phase:
            fence = ([(self.psem[k], self.cnt[k], k) for k in ENGS if self.cnt[k] > 0]
                     + [(d.h, d.val0, d) for d in self.dsems if getattr(d, "val0", 0) > 0])
        with nc.Block() as block:
            reg = dict(pe=block.tensor, act=block.scalar, dve=block.vector, pool=block.gpsimd, sp=block.sync)
            for e in ENGS:
                ops = self.ops[e]

                def body(engh, e=e, ops=ops):
                    wd = self.waited[e]
                    if fence is not None:
                        for h, v, k in fence:
                            if k == e:
                                continue
                            if wd.get(k, 0) >= v:
                                continue
                            engh.wait_ge(h, v)
                            wd[k] = v
                    for o in ops:
                        for k, v in o.deps.items():
                            if isinstance(k, str):
                                if k == e and (e == "pe" or not SAME_ENGINE_SYNC):
                                    continue
                                need = signo[k][v]
                                h = self.psem[k]
                            else:
                                need = v
                                h = k.h
                            if wd.get(k, 0) >= need:
                                continue
                            engh.wait_ge(h, need)
                            wd[k] = need
                        ins = o.fn(engh)
                        if o.dsem is not None:
                            ins.then_inc(o.dsem.h, 16)
                        elif o.sig:
                            ins.then_inc(self.psem[e], 1)

                reg[e](body)
        for e in ENGS:
            if self.ops[e]:
                self.cnt[e] = signo[e][-1]
            self.ops[e] = []
        for d in self.dsems:
            d.val0 = d.val
        self.first_phase = False

    def final_wait(self):
        nc = self.nc
        with nc.Block() as block:
            def body(engh):
                for k in ENGS:
                    if k != "sp" and self.cnt[k] > 0:
                        engh.wait_ge(self.psem[k], self.cnt[k])
                for d in self.dsems:
                    if d.val > 0:
                        engh.wait_ge(d.h, d.val)
            block.sync(body)


class Ring:
    def __init__(self, items):
        self.items = items
        self.bufs = [Buf() for _ in items]
        self.i = 0

    def next(self):
        k = self.i % len(self.items)
        self.i += 1
        return self.items[k], self.bufs[k]


def seq_tiles(S):
    n = -(-S // 510)
    W = -(-S // n)
    out = []
    t = 0
    while t < S:
        w = min(W, S - t)
        out.append((t, w))
        t += w
    return out


DEBUG_STOP = 99
DEBUG_MAXOPS = 10 ** 9


class _Stop(Exception):
    pass


def build_program(seqs):
    NT = sum(seqs)
    offs = [sum(seqs[:i]) for i in range(len(seqs))]
    NS = len(seqs)
    SMAX = max(seqs)
    nc = bass.Bass("TRN2", target_bir_lowering=False)

    def din(name, shape, dt=F32):
        return nc.dram_tensor(name, list(shape), dt, kind="ExternalInput").ap()

    def dscr(name, shape, dt=BF16):
        return nc.dram_tensor(name, list(shape), dt, kind="Internal").ap()

    x_d = din("x", [NT, D])
    c_d = din("c", [NS, D])
    w_ada_d = din("w_ada", [D, 6 * D])
    b_ada_d = din("b_ada", [1, 6 * D])
    n1g_d = din("norm1_g", [1, D])
    w_in_d = din("w_in", [D, INC])
    conva_d = din("conv_a_w", [3, A_W])
    qng_d = din("q_norm_g", [1, QL])
    w_uq_d = din("w_uq", [QL, NH * (DN + DR)])
    kvng_d = din("kv_norm_g", [1, KVL])
    w_ukv_d = din("w_ukv", [KVL, NH * (DN + DV)])
    ang_d = din("out_norm_a_g", [1, A_W])
    bng_d = din("out_norm_b_g", [1, A_W])
    w_o_d = din("w_o", [D, D])
    n2g_d = din("norm2_g", [1, D])
    w_up_d = din("w_up", [D, 2 * DFF])
    convf_d = din("ffn_conv_w", [3, 2 * DFF])
    w_dn_d = din("w_down", [DFF, D])
    fg_d = din("final_g", [1, D])
    cos_d = din("cos2", [64, SMAX + 2])
    sin_d = din("sin2", [64, SMAX + 2])
    y_d = nc.dram_tensor("y", [NT, D], F32, kind="ExternalOutput").ap()

    qS = dscr("qS", [NH, 128, NT])
    qrS = dscr("qrS", [NH, 64, NT])
    kS = dscr("kS", [NH, 128, NT])
    krS = dscr("krS", [64, NT])
    vS = dscr("vS", [NT, NH * DV])
    mixS = dscr("mixS", [8, 128, NT])
    modS = dscr("modS", [NS, 6 * D], F32)
    wupS = dscr("wupS", [NU, 128, 8, 256])

    with ExitStack() as es:
        P = Prog(nc, es)

        def sbg(name, shape, dt):
            return es.enter_context(nc.sbuf_tensor(name, list(shape), dt))

        ident = sbg("ident", [128, 128], BF16)
        ones = sbg("ones", [128, 128], BF16)
        epsT = sbg("epsT", [128, 1], F32)
        onesf = sbg("onesf", [128, 128], F32)
        n1g = sbg("n1g", [128, 8], F32)
        n2g = sbg("n2g", [128, 8], F32)
        qg = sbg("qg", [128, 3], F32)
        kvg = sbg("kvg", [128, 2], F32)
        ag = sbg("ag", [128, 4], F32)
        bg = sbg("bg", [128, 4], F32)
        convA = sbg("convA", [128, 3, 4], F32)
        convF = sbg("convF", [128, 3, 44], F32)
        modT = sbg("modT", [128, NS, 48], F32)
        gm1 = sbg("gm1", [128, NS, 8], F32)
        gm2 = sbg("gm2", [128, NS, 8], F32)

        def sh1(s, c):
            return modT[:, s, 0 * 8 + c:0 * 8 + c + 1]

        def sh2(s, c):
            return modT[:, s, 3 * 8 + c:3 * 8 + c + 1]

        try:
            with ExitStack() as ph:
                def sb(name, shape, dt):
                    return ph.enter_context(nc.sbuf_tensor(name, list(shape), dt))

                identf = sb("identf", [128, 128], F32)
                cT = sb("cT", [128, 8, NS], F32)
                cA = sb("cA", [128, 8, NS], F32)
                modsb = sb("modsb", [NS, 6 * D], F32)
                bsb = sb("bsb", [NS, 6 * D], F32)
                wa = [sb("wa%d" % i, [128, 8, 512], F32) for i in range(3)]
                pm = [ph.enter_context(nc.psum_tensor("p0m%d" % i, [128, 512], F32)) for i in range(2)]
                B = {k: Buf() for k in ["identf", "ident", "ones", "eps", "cT", "cA", "modsb", "bsb", "small", "modT", "gm",
                                         "modS"]}
                ds_small = P.dsem("small")
                ds_c = P.dsem("c")
                ds_b = P.dsem("b")
                ds_mod = P.dsem("mod")
                ds_modT = P.dsem("modT")
                ds_wa = [P.dsem("wa%d" % i) for i in range(3)]
                wa_ring = Ring(list(zip(wa, ds_wa)))
                pm_ring = Ring(pm)


                P.op("pool", lambda e: e.memset(identf[:], 0.0), writes=[B["identf"]])
                P.op("pool", lambda e: e.affine_select(out=identf[:], in_=identf[:], pattern=[[-1, 128]],
                                                       compare_op=ALU.not_equal, fill=1.0, base=0,
                                                       channel_multiplier=1), reads=[B["identf"]], writes=[B["identf"]])
                P.op("dve", lambda e: e.tensor_copy(out=ident[:], in_=identf[:]), reads=[B["identf"]], writes=[B["ident"]])
                P.op("dve", lambda e: e.memset(ones[:], 1.0), writes=[B["ones"]])
                P.op("dve", lambda e: e.memset(onesf[:], 1.0), writes=[B["ones"]])
                P.op("dve", lambda e: e.memset(epsT[:], EPS), writes=[B["eps"]])

                def fm(dst, src, n):
                    P.dma("sp", dst, src.rearrange("o (c p) -> p (o c)", p=128), ds_small, pwrites=[B["small"]], nonc=True)

                fm(n1g[:], n1g_d, 8)
                fm(n2g[:], n2g_d, 8)
                fm(qg[:], qng_d, 3)
                fm(kvg[:], kvng_d, 2)
                fm(ag[:], ang_d, 4)
                fm(bg[:], bng_d, 4)
                for j in range(3):
                    P.dma("sp", convA[:, j, :], conva_d[j:j + 1, :].rearrange("o (c p) -> p (o c)", p=128), ds_small,
                          pwrites=[B["small"]], nonc=True)
                    for q4 in range(4):
                        P.dma("sp", convF[:, j, q4 * 11:(q4 + 1) * 11],
                              convf_d[j:j + 1, q4 * 1408:(q4 + 1) * 1408].rearrange("o (c p) -> p (o c)", p=128), ds_small,
                              pwrites=[B["small"]], nonc=True)
                for s in range(NS):
                    P.dma("sp", cT[:, :, s], c_d[s:s + 1, :].rearrange("o (c p) -> p (o c)", p=128), ds_c, pwrites=[B["cT"]],
                          nonc=True)
                P.dma("sp", bsb[:], b_ada_d.to_broadcast([NS, 6 * D]), ds_b, writes=[B["bsb"]])
                P.op("act", lambda e: e.activation(out=cA[:], in_=cT[:], func=AF.Silu), reads=[B["cT"]], writes=[B["cA"]])
                wav = w_ada_d.rearrange("(kc p) n -> p kc n", p=128)
                for j in range(12):
                    (wt, dsw), wb = wa_ring.next()
                    P.dma("sp", wt[:], wav[:, :, j * 512:(j + 1) * 512], dsw, writes=[wb])
                    pmt, pb = pm_ring.next()
                    P.mm(pmt[0:NS, :], [(cA[:, kc, :], wt[:, kc, :]) for kc in range(8)], reads=[B["cA"], wb], writes=[pb])
                    P.op("dve", lambda e, pmt=pmt, j=j: e.tensor_tensor(out=modsb[:, j * 512:(j + 1) * 512], in0=pmt[0:NS, :],
                                                                         in1=bsb[:, j * 512:(j + 1) * 512], op=ALU.add),
                         reads=[pb, B["bsb"]], pwrites=[B["modsb"]])
                P.dma("pool", modS, modsb[:], ds_mod, reads=[B["modsb"]], writes=[B["modS"]])
                for s in range(NS):
                    for v6 in range(6):
                        P.dma("sp", modT[:, s, v6 * 8:(v6 + 1) * 8],
                              modS[s:s + 1, v6 * D:(v6 + 1) * D].rearrange("o (j p) -> p (o j)", p=128), ds_modT,
                              reads=[B["modS"]], pwrites=[B["modT"]], nonc=True)
                for s in range(NS):
                    P.op("dve", lambda e, s=s: e.scalar_tensor_tensor(out=gm1[:, s, :], in0=modT[:, s, 8:16], scalar=1.0,
                                                                      in1=n1g[:], op0=ALU.add, op1=ALU.mult),
                         reads=[B["modT"], B["small"]], pwrites=[B["gm"]])
                    P.op("dve", lambda e, s=s: e.scalar_tensor_tensor(out=gm2[:, s, :], in0=modT[:, s, 32:40], scalar=1.0,
                                                                      in1=n2g[:], op0=ALU.add, op1=ALU.mult),
                         reads=[B["modT"], B["small"]], pwrites=[B["gm"]])
                P.op("dve", lambda e: e.tensor_scalar(out=qg[:], in0=qg[:], scalar1=ATTN_SCALE, scalar2=None, op0=ALU.mult),
                     reads=[B["small"]], writes=[B["small"]])
                P.emit_phase()
                if DEBUG_STOP == 0:
                    P.disabled = True

            with ExitStack() as ph:
                def sb(name, shape, dt):
                    return ph.enter_context(nc.sbuf_tensor(name, list(shape), dt))

                def pst(name, shape, dt):
                    return ph.enter_context(nc.psum_tensor(name, list(shape), dt))

                w_in_sb = sb("w_in_sb", [128, 8, INC], BF16)
                w_krr = sb("w_krr", [128, 8, 64], BF16)
                w_uq_sb = sb("w_uq_sb", [128, 3, 768], BF16)
                w_uqr = sb("w_uqr", [128, 3, 256], BF16)
                w_uk_sb = sb("w_uk_sb", [128, 2, 512], BF16)
                w_uv_sb = sb("w_uv_sb", [128, 2, 512], BF16)
                xin = [sb("xin%d" % i, [128, D], F32) for i in range(3)]
                junk = sb("junk", [128, D], BF16)
                ms = sb("ms", [128, 8], F32)
                sd = sb("sd", [128, 8], F32)
                rstd = sb("rstd", [128, 8], F32)
                xn = [sb("xn%d" % i, [128, 4, D], BF16) for i in range(2)]
                hT = [sb("hT%d" % i, [128, 8, 512], BF16) for i in range(2)]
                hasb = sb("hasb", [128, 4, 512], F32)
                psb = sb("psb", [128, 4, 512], F32)
                basb = sb("basb", [128, 4, 512], F32)
                cqsb = sb("cqsb", [128, 3, 512], F32)
                ckvsb = sb("ckvsb", [128, 2, 512], F32)
                sq = [sb("sq%d" % i, [128, 512], BF16) for i in range(4)]
                rs = [sb("rs%d" % i, [128, 512], F32) for i in range(2)]
                rr = [sb("rr%d" % i, [128, 512], F32) for i in range(3)]
                cqn = sb("cqn", [128, 3, 512], BF16)
                ckvn = sb("ckvn", [128, 2, 512], BF16)
                yan = sb("yan", [128, 4, 512], BF16)
                qTo = sb("qTo", [128, 4, 512], BF16)
                qro = sb("qro", [64, 4, 512], BF16)
                t1 = [sb("t1_%d" % i, [64, 512], F32) for i in range(2)]
                t2 = [sb("t2_%d" % i, [64, 512], F32) for i in range(2)]
                kTo = sb("kTo", [128, 4, 512], BF16)
                kro = sb("kro", [64, 512], BF16)
                vo = sb("vo", [128, 4, 512], BF16)
                cs = [sb("cs%d" % i, [64, 512], F32) for i in range(2)]
                sn = [sb("sn%d" % i, [64, 512], F32) for i in range(2)]
                tp = [pst("tp%d" % i, [128, 2, 512], BF16) for i in range(2)]
                zp = [pst("zp%d" % i, [128, 512], F32) for i in range(6)]

                tp_ring = Ring(tp)
                zp_ring = Ring(zp)
                sq_ring = Ring(sq)
                rs_ring = Ring(rs)
                rr_ring = Ring(rr)
                t1_ring = Ring(t1)
                t2_ring = Ring(t2)
                xin_ds = [P.dsem("xin%d" % i) for i in range(3)]
                xin_ring = Ring(list(zip(xin, xin_ds)))
                cs_ds = [P.dsem("cs%d" % i) for i in range(2)]
                cs_ring = Ring(list(zip(cs, sn, cs_ds)))
                xn_ring = Ring(xn)
                hT_ring = Ring(hT)
                ds_w = P.dsem("p1w")
                ds_wup = P.dsem("wup")
                ds_st = {k: P.dsem("st_" + k) for k in ["ya", "q", "qr", "k", "kr", "v"]}
                Bw = Buf()
                Bst1 = [Buf() for _ in range(4)]
                Bha = [Buf() for _ in range(4)]
                Bp = [Buf() for _ in range(4)]
                Bba = [Buf() for _ in range(4)]
                Bx = {k: Buf() for k in ["cqsb", "ckvsb", "cqn", "ckvn", "yan", "qTo", "qro", "kTo", "kro", "vo"]}

                P.dma("pool", w_in_sb[:], w_in_d.rearrange("(kc p) n -> p kc n", p=128), ds_w, pwrites=[Bw])
                P.dma("pool", w_uq_sb[:], w_uq_d.rearrange("(kc p) n -> p kc n", p=128), ds_w, pwrites=[Bw])
                wkv = w_ukv_d.rearrange("(kc p) (h t d) -> p kc h t d", p=128, h=NH, t=2)
                for kc in range(2):
                    P.dma("pool", w_uk_sb[:, kc, :].rearrange("p (h d) -> p h d", h=NH), wkv[:, kc, :, 0, :], ds_w, pwrites=[Bw])
                    P.dma("pool", w_uv_sb[:, kc, :].rearrange("p (h d) -> p h d", h=NH), wkv[:, kc, :, 1, :], ds_w, pwrites=[Bw])
                wupv = w_up_d.rearrange("(kc p) n -> p kc n", p=128)
                for u in range(NU):
                    P.dma("pool", wupS[u, :, :, 0:128], wupv[:, :, u * 128:(u + 1) * 128], ds_wup)
                    P.dma("pool", wupS[u, :, :, 128:256], wupv[:, :, DFF + u * 128:DFF + (u + 1) * 128], ds_wup)
                Bw2 = Buf()
                P.op("dve", lambda e: e.tensor_scalar(out=w_krr[:, :, 0:32], in0=w_in_sb[:, :, 2208:2240], scalar1=-1.0,
                                                      scalar2=None, op0=ALU.mult), reads=[Bw], pwrites=[Bw2])
                P.op("dve", lambda e: e.tensor_copy(out=w_krr[:, :, 32:64], in_=w_in_sb[:, :, 2176:2208]), reads=[Bw],
                     pwrites=[Bw2])
                for h in range(NH):
                    b0 = h * 192 + 128
                    P.op("dve", lambda e, h=h, b0=b0: e.tensor_scalar(out=w_uqr[:, :, h * 64:h * 64 + 32],
                                                                      in0=w_uq_sb[:, :, b0 + 32:b0 + 64], scalar1=-1.0,
                                                                      scalar2=None, op0=ALU.mult), reads=[Bw], pwrites=[Bw2])
                    P.op("dve", lambda e, h=h, b0=b0: e.tensor_copy(out=w_uqr[:, :, h * 64 + 32:h * 64 + 64],
                                                                    in_=w_uq_sb[:, :, b0:b0 + 32]), reads=[Bw], pwrites=[Bw2])
                for i in range(3):
                    P.op("pool", lambda e, i=i: e.memset(xin[i][:], 0.0), writes=[xin_ring.bufs[i]])

                WR = [Bw, Bw2]

                def p1_front_a(s, ti, t0, w, ntl):
                    S = seqs[s]
                    off = offs[s]
                    C = w + 2
                    first = (ti == 0)
                    last = (ti == ntl - 1)
                    jlo = 1 if first else 0
                    jhi = C - 1 if last else C
                    G = -(-C // 128)
                    xnt, xnb = xn_ring.next()
                    hTt, hTb = hT_ring.next()
                    (cst, snt, csd), csb = cs_ring.next()
                    P.dma("sp", cst[:, 0:C], cos_d[:, t0:t0 + C], csd, writes=[csb])
                    P.dma("sp", snt[:, 0:C], sin_d[:, t0:t0 + C], csd, pwrites=[csb])
                    for g in range(G):
                        r = min(128, C - 128 * g)
                        lo = max(jlo, 128 * g) - 128 * g
                        hi = min(jhi, 128 * g + r) - 128 * g
                        (xt, xd), xb = xin_ring.next()
                        tok0 = off + t0 - 1 + 128 * g
                        P.dma("sp", xt[lo:hi, :], x_d[tok0 + lo:tok0 + hi, :], xd, writes=[xb])
                        yield
                        P.op("act", lambda e, xt=xt, g=g: e.activation(out=junk[:], in_=xt[:], func=AF.Square,
                                                                       scale=1.0 / 32.0, accum_out=ms[:, g:g + 1]),
                             reads=[xb], writes=[Bst1[g]])
                        yield
                        P.op("act", lambda e, g=g: e.activation(out=sd[:, g:g + 1], in_=ms[:, g:g + 1], func=AF.Ln,
                                                                bias=epsT[:], scale=1.0), reads=[Bst1[g]], writes=[Bst1[g]])
                        yield
                        P.op("act", lambda e, g=g: e.activation(out=rstd[:, g:g + 1], in_=sd[:, g:g + 1], func=AF.Exp,
                                                                scale=-0.5), reads=[Bst1[g]], writes=[Bst1[g]])
                        yield
                        P.op("dve", lambda e, xt=xt, g=g, xnt=xnt: e.tensor_scalar(out=xnt[:, g, :], in0=xt[:],
                                                                                   scalar1=rstd[:, g:g + 1], scalar2=None,
                                                                                   op0=ALU.mult),
                             reads=[xb, Bst1[g]], pwrites=[xnb])
                        yield
                    return (s, off, t0, w, C, first, last, jlo, jhi, G, xnt, xnb, hTt, hTb, cst, snt, csb)

                def p1_front_b(cx):
                    (s, off, t0, w, C, first, last, jlo, jhi, G, xnt, xnb, hTt, hTb, cst, snt, csb) = cx
                    for cp in range(4):
                        tpt, tpb = tp_ring.next()
                        for c2 in range(2):
                            c = 2 * cp + c2
                            for g in range(G):
                                r = min(128, C - 128 * g)
                                P.op("pe", lambda e, tpt=tpt, g=g, r=r, c=c, c2=c2, xnt=xnt: e.transpose(
                                    tpt[:, c2, 128 * g:128 * g + r], xnt[0:r, g, c * 128:(c + 1) * 128], ident[0:r, 0:r]),
                                     reads=[xnb], pwrites=[tpb])
                        for c2 in range(2):
                            c = 2 * cp + c2
                            P.op("act", lambda e, tpt=tpt, c=c, c2=c2, hTt=hTt, s=s, C=C: e.activation(
                                out=hTt[:, c, 0:C], in_=tpt[:, c2, 0:C], func=AF.Identity, bias=sh1(s, c),
                                scale=gm1[:, s, c:c + 1]), reads=[tpb], pwrites=[hTb])

                class Stepper:
                    def __init__(self, gen):
                        self.gen = gen
                        self.done = gen is None
                        self.val = None

                    def step(self, n=1):
                        for _ in range(n):
                            if self.done:
                                return
                            try:
                                next(self.gen)
                            except StopIteration as e_:
                                self.done = True
                                self.val = e_.value

                    def finish(self):
                        while not self.done:
                            self.step()
                        return self.val

                def rstd_bc(stt, stb, n, scale):
                    rst, rsb = rs_ring.next()
                    P.op("act", lambda e: e.activation(out=rst[:, 0:n], in_=stt[:, 0:n], func=AF.Ln, bias=epsT[:], scale=scale),
                         reads=[stb], writes=[rsb])
                    rqt, rqb = rr_ring.next()
                    P.op("act", lambda e: e.activation(out=rqt[:, 0:n], in_=rst[:, 0:n], func=AF.Exp, scale=-0.5),
                         reads=[rsb], writes=[rqb])
                    return rqt, rqb

                def p1_back(cx, stp):
                    (s, off, t0, w, C, first, last, jlo, jhi, G, xnt, xnb, hTt, hTb, cst, snt, csb) = cx
                    def zgroup(col0, M, wsb=None, rot=False):
                        zt, zb = zp_ring.next()
                        if rot:
                            pairs = [(w_krr[:, kc, 0:64], hTt[:, kc, 0:C]) for kc in range(8)]
                        else:
                            pairs = [(w_in_sb[:, kc, col0:col0 + M], hTt[:, kc, 0:C]) for kc in range(8)]
                        P.mm(zt[0:M, 0:C], pairs, reads=[hTb] + WR, writes=[zb])
                        stp.step(2)
                        return zt, zb

                    for c in range(4):
                        zt, zb = zgroup(c * 128, 128)
                        P.op("act", lambda e, zt=zt, c=c, C=C: e.activation(out=hasb[:, c, 0:C], in_=zt[:, 0:C], func=AF.Copy),
                             reads=[zb], writes=[Bha[c]])
                    for c in range(4):
                        zt, zb = zgroup(1024 + c * 128, 128)
                        P.op("dve", lambda e, zt=zt, c=c, jlo=jlo, jhi=jhi: e.tensor_tensor(
                            out=psb[:, c, jlo:jhi], in0=zt[:, jlo:jhi], in1=hasb[:, c, jlo:jhi], op=ALU.mult),
                             reads=[zb, Bha[c]], pwrites=[Bp[c]])
                    if first:
                        P.op("pool", lambda e: e.memset(psb[:, :, 0:1], 0.0), pwrites=Bp)
                    if last:
                        P.op("pool", lambda e, C=C: e.memset(psb[:, :, C - 1:C], 0.0), pwrites=Bp)
                    for c in range(4):
                        zt, zb = zgroup(512 + c * 128, 128)
                        P.op("act", lambda e, zt=zt, c=c, C=C: e.activation(out=basb[:, c, 0:C], in_=zt[:, 0:C], func=AF.Copy),
                             reads=[zb], writes=[Bba[c]])
                    sqs = []
                    for c in range(3):
                        zt, zb = zgroup(1536 + c * 128, 128)
                        P.op("act", lambda e, zt=zt, c=c, C=C: e.activation(out=cqsb[:, c, 0:C], in_=zt[:, 0:C], func=AF.Copy),
                             reads=[zb], pwrites=[Bx["cqsb"]])
                        sqt, sqb = sq_ring.next()
                        P.op("act", lambda e, zt=zt, sqt=sqt, C=C: e.activation(out=sqt[:, 0:C], in_=zt[:, 0:C], func=AF.Square),
                             reads=[zb], writes=[sqb])
                        sqs.append((sqt, sqb))
                    stt, stb = zp_ring.next()
                    P.mm(stt[:, 0:C], [(ones[:], q[0][:, 0:C]) for q in sqs], reads=[q[1] for q in sqs], writes=[stb])
                    rqt, rqb = rstd_bc(stt, stb, C, 1.0 / QL)
                    for c in range(3):
                        P.op("dve", lambda e, c=c, rqt=rqt, C=C: e.scalar_tensor_tensor(
                            out=cqn[:, c, 0:C], in0=cqsb[:, c, 0:C], scalar=qg[:, c:c + 1], in1=rqt[:, 0:C],
                            op0=ALU.mult, op1=ALU.mult), reads=[Bx["cqsb"], rqb], pwrites=[Bx["cqn"]])
                    sqs = []
                    for c in range(2):
                        zt, zb = zgroup(1920 + c * 128, 128)
                        P.op("act", lambda e, zt=zt, c=c, C=C: e.activation(out=ckvsb[:, c, 0:C], in_=zt[:, 0:C], func=AF.Copy),
                             reads=[zb], pwrites=[Bx["ckvsb"]])
                        sqt, sqb = sq_ring.next()
                        P.op("act", lambda e, zt=zt, sqt=sqt, C=C: e.activation(out=sqt[:, 0:C], in_=zt[:, 0:C], func=AF.Square),
                             reads=[zb], writes=[sqb])
                        sqs.append((sqt, sqb))
                    stt, stb = zp_ring.next()
                    P.mm(stt[:, 0:C], [(ones[:], q[0][:, 0:C]) for q in sqs], reads=[q[1] for q in sqs], writes=[stb])
                    rkt, rkb = rstd_bc(stt, stb, C, 1.0 / KVL)
                    for c in range(2):
                        P.op("dve", lambda e, c=c, rkt=rkt, C=C: e.scalar_tensor_tensor(
                            out=ckvn[:, c, 0:C], in0=ckvsb[:, c, 0:C], scalar=kvg[:, c:c + 1], in1=rkt[:, 0:C],
                            op0=ALU.mult, op1=ALU.mult), reads=[Bx["ckvsb"], rkb], pwrites=[Bx["ckvn"]])
                    za, zab = zgroup(2176, 64)
                    zr, zrb = zgroup(0, 64, rot=True)
                    t1t, t1b = t1_ring.next()
                    t2t, t2b = t2_ring.next()
                    P.op("dve", lambda e, za=za, t1t=t1t, cst=cst, C=C: e.tensor_tensor(
                        out=t1t[:, 0:C], in0=za[0:64, 0:C], in1=cst[:, 0:C], op=ALU.mult), reads=[zab, csb], writes=[t1b])
                    P.op("dve", lambda e, zr=zr, t2t=t2t, snt=snt, C=C: e.tensor_tensor(
                        out=t2t[:, 0:C], in0=zr[0:64, 0:C], in1=snt[:, 0:C], op=ALU.mult), reads=[zrb, csb], writes=[t2b])
                    P.op("dve", lambda e, t1t=t1t, t2t=t2t, C=C: e.tensor_tensor(
                        out=kro[:, 0:C], in0=t1t[:, 0:C], in1=t2t[:, 0:C], op=ALU.add), reads=[t1b, t2b], writes=[Bx["kro"]])
                    P.dma("pool", krS[:, off + t0:off + t0 + w], kro[:, 1:w + 1], ds_st["kr"], reads=[Bx["kro"]])
                    ncx = stp.finish()
                    if ncx is not None:
                        p1_front_b(ncx)
                    for h in range(NH):
                        zt, zb = zp_ring.next()
                        P.mm(zt[:, 0:C], [(w_uq_sb[:, kc, h * 192:h * 192 + 128], cqn[:, kc, 0:C]) for kc in range(3)],
                             reads=[Bx["cqn"]] + WR, writes=[zb])
                        P.op("act", lambda e, zt=zt, h=h, C=C: e.activation(out=qTo[:, h, 0:C], in_=zt[:, 0:C], func=AF.Copy),
                             reads=[zb], pwrites=[Bx["qTo"]])
                        za, zab = zp_ring.next()
                        P.mm(za[0:64, 0:C], [(w_uq_sb[:, kc, h * 192 + 128:h * 192 + 192], cqn[:, kc, 0:C]) for kc in range(3)],
                             reads=[Bx["cqn"]] + WR, writes=[zab])
                        zr, zrb = zp_ring.next()
                        P.mm(zr[0:64, 0:C], [(w_uqr[:, kc, h * 64:(h + 1) * 64], cqn[:, kc, 0:C]) for kc in range(3)],
                             reads=[Bx["cqn"]] + WR, writes=[zrb])
                        t1t, t1b = t1_ring.next()
                        t2t, t2b = t2_ring.next()
                        P.op("dve", lambda e, za=za, t1t=t1t, cst=cst, C=C: e.tensor_tensor(
                            out=t1t[:, 0:C], in0=za[0:64, 0:C], in1=cst[:, 0:C], op=ALU.mult), reads=[zab, csb], writes=[t1b])
                        P.op("dve", lambda e, zr=zr, t2t=t2t, snt=snt, C=C: e.tensor_tensor(
                            out=t2t[:, 0:C], in0=zr[0:64, 0:C], in1=snt[:, 0:C], op=ALU.mult), reads=[zrb, csb], writes=[t2b])
                        P.op("dve", lambda e, t1t=t1t, t2t=t2t, h=h, C=C: e.tensor_tensor(
                            out=qro[:, h, 0:C], in0=t1t[:, 0:C], in1=t2t[:, 0:C], op=ALU.add),
                             reads=[t1b, t2b], pwrites=[Bx["qro"]])
                    P.dma("pool", qS[:, :, off + t0:off + t0 + w].rearrange("h p t -> p h t"), qTo[:, :, 1:w + 1],
                          ds_st["q"], reads=[Bx["qTo"]])
                    P.dma("pool", qrS[:, :, off + t0:off + t0 + w].rearrange("h p t -> p h t"), qro[:, :, 1:w + 1],
                          ds_st["qr"], reads=[Bx["qro"]])
                    for h in range(NH):
                        zt, zb = zp_ring.next()
                        P.mm(zt[:, 0:C], [(w_uk_sb[:, kc, h * 128:(h + 1) * 128], ckvn[:, kc, 0:C]) for kc in range(2)],
                             reads=[Bx["ckvn"]] + WR, writes=[zb])
                        P.op("act", lambda e, zt=zt, h=h, C=C: e.activation(out=kTo[:, h, 0:C], in_=zt[:, 0:C], func=AF.Copy),
                             reads=[zb], pwrites=[Bx["kTo"]])
                    P.dma("pool", kS[:, :, off + t0:off + t0 + w].rearrange("h p t -> p h t"), kTo[:, :, 1:w + 1],
                          ds_st["k"], reads=[Bx["kTo"]])
                    for g in range(G):
                        r = min(128, C - 128 * g)
                        zt, zb = zp_ring.next()
                        P.mm(zt[0:r, :], [(ckvn[:, kc, 128 * g:128 * g + r], w_uv_sb[:, kc, :]) for kc in range(2)],
                             reads=[Bx["ckvn"]] + WR, writes=[zb])
                        P.op("dve", lambda e, zt=zt, g=g, r=r: e.tensor_copy(out=vo[0:r, g, :], in_=zt[0:r, :]),
                             reads=[zb], pwrites=[Bx["vo"]])
                    for g in range(G):
                        r = min(128, C - 128 * g)
                        lo = max(1, 128 * g) - 128 * g
                        hi = min(w + 1, 128 * g + r) - 128 * g
                        if hi <= lo:
                            continue
                        tok0 = off + t0 - 1 + 128 * g
                        P.dma("pool", vS[tok0 + lo:tok0 + hi, :], vo[lo:hi, g, :], ds_st["v"], reads=[Bx["vo"]])

                    ya = hasb
                    Bya = Bha
                    for c in range(4):
                        P.op("dve", lambda e, c=c, w=w: e.tensor_scalar(out=ya[:, c, 1:w + 1], in0=psb[:, c, 1:w + 1],
                                                                        scalar1=convA[:, 1, c:c + 1], scalar2=None,
                                                                        op0=ALU.mult), reads=[Bp[c]], writes=[Bya[c]])
                    for c in range(4):
                        P.op("dve", lambda e, c=c, w=w: e.scalar_tensor_tensor(
                            out=ya[:, c, 1:w + 1], in0=psb[:, c, 0:w], scalar=convA[:, 0, c:c + 1], in1=ya[:, c, 1:w + 1],
                            op0=ALU.mult, op1=ALU.add), reads=[Bp[c], Bya[c]], writes=[Bya[c]])
                    for c in range(4):
                        P.op("dve", lambda e, c=c, w=w: e.scalar_tensor_tensor(
                            out=ya[:, c, 1:w + 1], in0=psb[:, c, 2:w + 2], scalar=convA[:, 2, c:c + 1], in1=ya[:, c, 1:w + 1],
                            op0=ALU.mult, op1=ALU.add), reads=[Bp[c], Bya[c]], writes=[Bya[c]])
                    sqs = []
                    for c in range(4):
                        P.op("dve", lambda e, c=c, w=w: e.tensor_tensor(out=ya[:, c, 1:w + 1], in0=ya[:, c, 1:w + 1],
                                                                        in1=basb[:, c, 1:w + 1], op=ALU.mult),
                             reads=[Bba[c], Bya[c]], writes=[Bya[c]])
                    for c in range(4):
                        sqt, sqb = sq_ring.next()
                        P.op("act", lambda e, c=c, w=w, sqt=sqt: e.activation(out=sqt[:, 0:w], in_=ya[:, c, 1:w + 1],
                                                                              func=AF.Square), reads=[Bya[c]], writes=[sqb])
                        sqs.append((sqt, sqb))
                    stt, stb = zp_ring.next()
                    P.mm(stt[:, 0:w], [(ones[:], q[0][:, 0:w]) for q in sqs], reads=[q[1] for q in sqs], writes=[stb])
                    rat, rab = rstd_bc(stt, stb, w, 1.0 / A_W)
                    for c in range(4):
                        P.op("dve", lambda e, c=c, rat=rat, w=w: e.scalar_tensor_tensor(
                            out=yan[:, c, 0:w], in0=ya[:, c, 1:w + 1], scalar=ag[:, c:c + 1], in1=rat[:, 0:w],
                            op0=ALU.mult, op1=ALU.mult), reads=[Bya[c], rab], pwrites=[Bx["yan"]])
                    P.dma("pool", mixS[0:4, :, off + t0:off + t0 + w].rearrange("c p t -> p c t"), yan[:, :, 0:w],
                          ds_st["ya"], reads=[Bx["yan"]])
                    return ncx

                tiles_all = []
                for s in range(NS):
                    tl = seq_tiles(seqs[s])
                    for ti, (t0, w) in enumerate(tl):
                        tiles_all.append((s, ti, t0, w, len(tl)))
                st0 = Stepper(p1_front_a(*tiles_all[0]))
                cx = st0.finish()
                p1_front_b(cx)
                for n_ in range(len(tiles_all)):
                    stp = Stepper(p1_front_a(*tiles_all[n_ + 1]) if n_ + 1 < len(tiles_all) else None)
                    cx = p1_back(cx, stp)
                P.emit_phase()
                if DEBUG_STOP == 1:
                    P.disabled = True

            with ExitStack() as ph:
                def sb(name, shape, dt):
                    return ph.enter_context(nc.sbuf_tensor(name, list(shape), dt))

                def pst(name, shape, dt):
                    return ph.enter_context(nc.psum_tensor(name, list(shape), dt))

                kT = sb("kT", [128, NH, SMAX], BF16)
                krT = sb("krT", [128, SMAX], BF16)
                V = sb("V", [128, SMAX // 128, NH * DV], BF16)
                qT = [sb("qT%d" % i, [128, NH, 512], BF16) for i in range(2)]
                qrT = [sb("qrT%d" % i, [128, NH, 512], BF16) for i in range(2)]
                pT = [sb("pT%d" % i, [128, 1024], BF16) for i in range(4)]
                acc = [sb("acc%d" % i, [128, 1024], F32) for i in range(2)]
                yb = sb("yb", [128, NH, 512], F32)
                rec = [sb("rec%d" % i, [128, 512], F32) for i in range(2)]
                sq = [sb("sq2_%d" % i, [128, 512], BF16) for i in range(4)]
                rs2 = sb("rs2", [128, 512], F32)
                rb = sb("rb", [128, 512], F32)
                ybn = [sb("ybn%d" % i, [128, NH, 512], BF16) for i in range(1)]
                sps = [pst("sps%d" % i, [128, 512], F32) for i in range(6)]
                ops_ = [pst("ops%d" % i, [128, 512], F32) for i in range(2)]
                s_ring = Ring(sps)
                o_ring = Ring(ops_)
                acc_ring = Ring(acc)
                pT_ring = Ring(pT)
                pT_hb = [[Buf(), Buf()] for _ in range(4)]
                rec_ring = Ring(rec)
                sq_ring = Ring(sq)
                q_ds = [P.dsem("q2_%d" % i) for i in range(2)]
                q_ring = Ring(list(zip(qT, qrT, q_ds)))
                ybn_ds = [P.dsem("ybn%d" % i) for i in range(1)]
                ybn_ring = Ring(list(zip(ybn, ybn_ds)))
                ds_kv = P.dsem("kv")
                Bkv = Buf()
                P.op("pool", lambda e: e.memset(krT[64:128, :], 0.0), writes=[Bkv])
                for i in range(2):
                    P.op("pool", lambda e, i=i: e.memset(qrT[i][64:128, :, :], 0.0), writes=[q_ring.bufs[i]])
                Byb = [Buf() for _ in range(NH)]
                Brs = Buf()
                Brb = Buf()

                for s in range(NS):
                    S = seqs[s]
                    off = offs[s]
                    NKC = S // 128
                    P.dma("sp", kT[:, :, 0:S], kS[:, :, off:off + S].rearrange("h p t -> p h t"), ds_kv, pwrites=[Bkv])
                    P.dma("sp", krT[0:64, 0:S], krS[:, off:off + S], ds_kv, pwrites=[Bkv])
                    vsrc = vS[off:off + S, :].rearrange("(c p) d -> p c d", p=128)
                    step = 16
                    for c0 in range(0, NKC, step):
                        c1 = min(NKC, c0 + step)
                        P.dma("sp", V[:, c0:c1, :], vsrc[:, c0:c1, :], ds_kv, pwrites=[Bkv])
                    for qi in range(S // 512):
                        q0 = qi * 512
                        (qt, qrt, qd), qb = q_ring.next()
                        P.dma("sp", qt[:], qS[:, :, off + q0:off + q0 + 512].rearrange("h p t -> p h t"), qd, writes=[qb])
                        P.dma("sp", qrt[0:64, :, :], qrS[:, :, off + q0:off + q0 + 512].rearrange("h p t -> p h t"), qd,
                              pwrites=[qb])
                        sqs = []
                        for h in range(NH):
                            ot, ob = o_ring.next()
                            acc2, accb_ = acc_ring.next()
                            assert NKC % 2 == 0
                            pend = []
                            cur = [None]

                            def qk(kc):
                                st_, stb_ = s_ring.next()
                                P.mm(st_[:], [(kT[:, h, kc * 128:(kc + 1) * 128], qt[:, h, :]),
                                              (krT[:, kc * 128:(kc + 1) * 128], qrt[:, h, :])], reads=[Bkv, qb], writes=[stb_])
                                if kc % 2 == 0:
                                    slot = pT_ring.i % 4
                                    ptp, _ = pT_ring.next()
                                    cur[0] = (ptp, pT_hb[slot])
                                ptp, hb = cur[0]
                                half = kc % 2
                                P.op("act", lambda e, st_=st_, ptp=ptp, half=half: e.activation(out=ptp[:, half * 512:(half + 1) * 512], in_=st_[:],
                                                                                                func=AF.Exp),
                                     reads=[stb_], writes=[hb[half]])
                                pend.append((kc, ptp, hb))

                            def pv():
                                kc, ptp, hb = pend.pop(0)
                                half = kc % 2
                                f = (kc == 0)
                                l = (kc == NKC - 1)
                                if f:
                                    P.mm(ot[:], [(V[:, kc, h * DV:(h + 1) * DV], ptp[:, half * 512:(half + 1) * 512])], reads=[Bkv, hb[half]],
                                         writes=[ob], first=f, last=l)
                                else:
                                    P.mm(ot[:], [(V[:, kc, h * DV:(h + 1) * DV], ptp[:, half * 512:(half + 1) * 512])], reads=[Bkv, hb[half]],
                                         pwrites=[ob], first=f, last=l)
                                if half == 1:
                                    if kc == 1:
                                        P.op("dve", lambda e, ptp=ptp, acc2=acc2: e.tensor_copy(out=acc2[:], in_=ptp[:]),
                                             reads=[hb[0], hb[1]], writes=[accb_])
                                    else:
                                        P.op("dve", lambda e, ptp=ptp, acc2=acc2: e.tensor_tensor(out=acc2[:], in0=acc2[:], in1=ptp[:],
                                                                                      op=ALU.add),
                                             reads=[hb[0], hb[1], accb_], writes=[accb_])

                            LOOK = 2
                            for kc in range(min(LOOK, NKC)):
                                qk(kc)
                            for kc in range(NKC):
                                if kc + LOOK < NKC:
                                    qk(kc + LOOK)
                                pv()
                            P.op("dve", lambda e, acc2=acc2: e.tensor_tensor(out=acc2[:, 0:512], in0=acc2[:, 0:512], in1=acc2[:, 512:1024], op=ALU.add),
                                 reads=[accb_], writes=[accb_])
                            mt, mb = s_ring.next()
                            P.mm(mt[:], [(onesf[:], acc2[:, 0:512])], reads=[accb_], writes=[mb])
                            ret, reb = rec_ring.next()
                            P.op("act", lambda e, mt=mt, ret=ret: e.activation(out=ret[:], in_=mt[:], func=AF.Ln), reads=[mb],
                                 writes=[reb])
                            P.op("act", lambda e, ret=ret: e.activation(out=ret[:], in_=ret[:], func=AF.Exp, scale=-1.0),
                                 reads=[reb], writes=[reb])
                            P.op("dve", lambda e, ot=ot, ret=ret, h=h: e.tensor_tensor(out=yb[:, h, :], in0=ot[:], in1=ret[:],
                                                                                      op=ALU.mult),
                                 reads=[ob, reb], writes=[Byb[h]])
                            sqt, sqb = sq_ring.next()
                            P.op("act", lambda e, sqt=sqt, h=h: e.activation(out=sqt[:], in_=yb[:, h, :], func=AF.Square),
                                 reads=[Byb[h]], writes=[sqb])
                            sqs.append((sqt, sqb))
                        stt, stb = s_ring.next()
                        P.mm(stt[:], [(ones[:], q[0][:]) for q in sqs], reads=[q[1] for q in sqs], writes=[stb])
                        P.op("act", lambda e, stt=stt: e.activation(out=rs2[:], in_=stt[:], func=AF.Ln, bias=epsT[:],
                                                                   scale=1.0 / A_W), reads=[stb], writes=[Brs])
                        P.op("act", lambda e: e.activation(out=rb[:], in_=rs2[:], func=AF.Exp, scale=-0.5), reads=[Brs],
                             writes=[Brb])
                        (ybt, ybd), ybb = ybn_ring.next()
                        for h in range(NH):
                            P.op("dve", lambda e, h=h, ybt=ybt: e.scalar_tensor_tensor(
                                out=ybt[:, h, :], in0=yb[:, h, :], scalar=bg[:, h:h + 1], in1=rb[:], op0=ALU.mult, op1=ALU.mult),
                                 reads=[Byb[h], Brb], pwrites=[ybb])
                        P.dma("pool", mixS[4:8, :, off + q0:off + q0 + 512].rearrange("c p t -> p c t"), ybt[:], ybd, reads=[ybb])
                P.emit_phase()
                if DEBUG_STOP == 2:
                    P.disabled = True

            with ExitStack() as ph:
                def sb(name, shape, dt):
                    return ph.enter_context(nc.sbuf_tensor(name, list(shape), dt))

                def pst(name, shape, dt):
                    return ph.enter_context(nc.psum_tensor(name, list(shape), dt))

                w_o_sb = sb("w_o_sb", [128, 8, D], BF16)
                w_dn_sb = sb("w_dn_sb", [128, NU, D], BF16)
                wup = [sb("wup%d" % i, [128, 8, 256], BF16) for i in range(4)]
                g1bc = sb("g1bc", [128, D], F32)
                g2bc = sb("g2bc", [128, D], F32)
                fgbc = sb("fgbc", [128, D], F32)
                mixT = sb("mixT", [128, 8, 512], BF16)
                xin = [sb("x3in%d" % i, [128, D], F32) for i in range(2)]
                x1 = sb("x1", [128, 4, D], F32)
                x1b = sb("x1b", [128, 4, D], F32)
                xn2 = sb("xn2", [128, 4, D], BF16)
                h2T = sb("h2T", [128, 8, 512], BF16)
                aT = sb("aT", [128, NU, 512], BF16)
                c1 = [sb("c1_%d" % i, [128, 512], F32) for i in range(3)]
                c2 = [sb("c2_%d" % i, [128, 512], F32) for i in range(4)]
                sg = [sb("sg%d" % i, [128, 512], F32) for i in range(2)]
                tmp = [sb("tmp%d" % i, [128, 512], F32) for i in range(2)]
                junk = sb("junk3", [128, D], BF16)
                ms = sb("ms3", [128, 8], F32)
                sd = sb("sd3", [128, 8], F32)
                rstd = sb("rstd3", [128, 8], F32)
                tp = [pst("tp3_%d" % i, [128, 2, 512], BF16) for i in range(2)]
                zp = [pst("zp3_%d" % i, [128, 512], F32) for i in range(6)]
                tp_ring = Ring(tp)
                zp_ring = Ring(zp)
                c1_ring = Ring(c1)
                c2_ring = Ring(c2)
                sg_ring = Ring(sg)
                tmp_ring = Ring(tmp)
                wup_ds = [P.dsem("wup%d" % i) for i in range(4)]
                wup_ring = Ring(list(zip(wup, wup_ds)))
                xin_ds = [P.dsem("x3in%d" % i) for i in range(2)]
                xin_ring = Ring(list(zip(xin, xin_ds)))
                ds_w = P.dsem("p3w")
                ds_g = P.dsem("p3g")
                ds_g2 = P.dsem("p3g2")
                ds_mix = P.dsem("mix")
                ds_out = [P.dsem("out%d" % g) for g in range(4)]
                ds_outb = [P.dsem("outb%d" % g) for g in range(4)]
                Bw = Buf()
                Bg = Buf()
                Bfg = Buf()
                Bmix = Buf()
                Bx1 = [Buf() for _ in range(4)]
                Bxn2 = Buf()
                Bh2 = Buf()
                BaT = Buf()
                Bst = [Buf() for _ in range(4)]
                Bst2 = [Buf() for _ in range(4)]

                class Stepper3:
                    def __init__(self, gen):
                        self.gen = gen
                        self.done = gen is None
                        self.val = None

                    def step(self, n=1):
                        for _ in range(n):
                            if self.done:
                                return
                            try:
                                next(self.gen)
                            except StopIteration as e_:
                                self.done = True
                                self.val = e_.value

                    def finish(self):
                        while not self.done:
                            self.step()
                        return self.val

                P.dma("pool", w_o_sb[:], w_o_d.rearrange("(kc p) n -> p kc n", p=128), ds_w, pwrites=[Bw])
                P.dma("pool", w_dn_sb[:, 0:11, :], w_dn_d[0:11 * 128, :].rearrange("(kc p) n -> p kc n", p=128), ds_w, pwrites=[Bw])
                P.dma("pool", w_dn_sb[:, 11:22, :], w_dn_d[11 * 128:22 * 128, :].rearrange("(kc p) n -> p kc n", p=128), ds_w,
                      pwrites=[Bw])
                P.dma("sp", fgbc[:], fg_d.to_broadcast([128, D]), ds_g, writes=[Bfg])
                P.op("pool", lambda e: e.memset(mixT[:], 0.0), writes=[Bmix])
                for i in range(2):
                    P.op("pool", lambda e, i=i: e.memset(xin[i][:], 0.0), writes=[xin_ring.bufs[i]])
                P.op("pool", lambda e: e.memset(aT[:], 0.0), writes=[BaT])

                Bg1 = Buf()
                Bg2 = Buf()
                x1s = [x1, x1b]
                Bx1s = [[Buf() for _ in range(4)] for _ in range(2)]
                x1_i = [0]

                def p3_A1(s, ti, t0, w, ntl):
                    off = offs[s]
                    C = w + 2
                    first = (ti == 0)
                    last = (ti == ntl - 1)
                    jlo = 1 if first else 0
                    jhi = C - 1 if last else C
                    G = -(-C // 128)
                    tokc = off + t0 - 1
                    x1t = x1s[x1_i[0] % 2]
                    Bx1 = Bx1s[x1_i[0] % 2]
                    x1_i[0] += 1
                    if first:
                        P.dma("sp", g1bc[:], modS[s:s + 1, 2 * D:3 * D].to_broadcast([128, D]), ds_g, writes=[Bg1])
                    P.dma("sp", mixT[:, :, jlo:jhi], mixS[:, :, tokc + jlo:tokc + jhi].rearrange("c p t -> p c t"), ds_mix,
                          writes=[Bmix])
                    yield
                    for g in range(G):
                        r = min(128, C - 128 * g)
                        lo = max(jlo, 128 * g) - 128 * g
                        hi = min(jhi, 128 * g + r) - 128 * g
                        (xt, xd), xb = xin_ring.next()
                        tok0 = tokc + 128 * g
                        P.dma("sp", xt[lo:hi, :], x_d[tok0 + lo:tok0 + hi, :], xd, writes=[xb])
                        yield
                        for hf in range(2):
                            zt, zb = zp_ring.next()
                            P.mm(zt[0:r, :], [(mixT[:, c, 128 * g:128 * g + r], w_o_sb[:, c, hf * 512:(hf + 1) * 512])
                                              for c in range(8)], reads=[Bmix, Bw], writes=[zb])
                            yield
                            tt, tb = tmp_ring.next()
                            P.op("dve", lambda e, zt=zt, tt=tt, r=r, hf=hf: e.tensor_tensor(
                                out=tt[0:r, :], in0=zt[0:r, :], in1=g1bc[0:r, hf * 512:(hf + 1) * 512], op=ALU.mult),
                                 reads=[zb, Bg1], writes=[tb])
                            yield
                            P.op("dve", lambda e, tt=tt, xt=xt, r=r, hf=hf, g=g: e.tensor_tensor(
                                out=x1t[0:r, g, hf * 512:(hf + 1) * 512], in0=tt[0:r, :], in1=xt[0:r, hf * 512:(hf + 1) * 512],
                                op=ALU.add), reads=[tb, xb], pwrites=[Bx1[g]])
                            yield
                        P.op("act", lambda e, g=g: e.activation(out=junk[:], in_=x1t[:, g, :], func=AF.Square, scale=1.0 / 32.0,
                                                                accum_out=ms[:, g:g + 1]), reads=[Bx1[g]], writes=[Bst[g]])
                        yield
                        P.op("act", lambda e, g=g: e.activation(out=sd[:, g:g + 1], in_=ms[:, g:g + 1], func=AF.Sqrt,
                                                                bias=epsT[:], scale=1.0), reads=[Bst[g]], writes=[Bst[g]])
                        yield
                        P.op("dve", lambda e, g=g: e.reciprocal(out=rstd[:, g:g + 1], in_=sd[:, g:g + 1]),
                             reads=[Bst[g]], writes=[Bst[g]])
                        yield
                        P.op("act", lambda e, g=g: e.activation(out=xn2[:, g, :], in_=x1t[:, g, :], func=AF.Identity,
                                                                scale=rstd[:, g:g + 1]), reads=[Bx1[g], Bst[g]], pwrites=[Bxn2])
                        yield
                    return (s, off, t0, w, C, first, last, jlo, jhi, G, tokc, x1t, Bx1)

                def p3_A2(cx):
                    (s, off, t0, w, C, first, last, jlo, jhi, G, tokc, x1t, Bx1) = cx
                    for cp in range(4):
                        tpt, tpb = tp_ring.next()
                        for c2 in range(2):
                            c = 2 * cp + c2
                            for g in range(G):
                                r = min(128, C - 128 * g)
                                P.op("pe", lambda e, tpt=tpt, g=g, r=r, c=c, c2=c2: e.transpose(
                                    tpt[:, c2, 128 * g:128 * g + r], xn2[0:r, g, c * 128:(c + 1) * 128], ident[0:r, 0:r]),
                                     reads=[Bxn2], pwrites=[tpb])
                            yield
                        for c2 in range(2):
                            c = 2 * cp + c2
                            P.op("act", lambda e, tpt=tpt, c=c, c2=c2, s=s, jlo=jlo, jhi=jhi: e.activation(
                                out=h2T[:, c, jlo:jhi], in_=tpt[:, c2, jlo:jhi], func=AF.Identity, bias=sh2(s, c),
                                scale=gm2[:, s, c:c + 1]), reads=[tpb], pwrites=[Bh2])
                            yield
                    if first:
                        P.op("pool", lambda e: e.memset(h2T[:, :, 0:1], 0.0), pwrites=[Bh2])
                    if last:
                        P.op("pool", lambda e, C=C: e.memset(h2T[:, :, C - 1:C], 0.0), pwrites=[Bh2])
                    return None

                def p3_B(cx, stp):
                    (s, off, t0, w, C, first, last, jlo, jhi, G, tokc, x1t, Bx1) = cx
                    if first:
                        P.dma("sp", g2bc[:], modS[s:s + 1, 5 * D:6 * D].to_broadcast([128, D]), ds_g2, writes=[Bg2])
                    for u in range(NU):
                        (wt, wd), wb = wup_ring.next()
                        P.dma("sp", wt[:], wupS[u], wd, writes=[wb])
                        zg, zgb = zp_ring.next()
                        P.mm(zg[:, 0:C], [(wt[:, kc, 0:128], h2T[:, kc, 0:C]) for kc in range(8)], reads=[wb, Bh2], writes=[zgb])
                        zv, zvb = zp_ring.next()
                        P.mm(zv[:, 0:C], [(wt[:, kc, 128:256], h2T[:, kc, 0:C]) for kc in range(8)], reads=[wb, Bh2], writes=[zvb])
                        outs = []
                        for (zt, zb, ch) in ((zg, zgb, u), (zv, zvb, NU + u)):
                            c1t, c1b = c1_ring.next()
                            P.op("act", lambda e, zt=zt, c1t=c1t, ch=ch, w=w: e.activation(
                                out=c1t[:, 0:w], in_=zt[:, 1:w + 1], func=AF.Identity, scale=convF[:, 1, ch:ch + 1]),
                                 reads=[zb], writes=[c1b])
                            c2t, c2b = c2_ring.next()
                            P.op("dve", lambda e, zt=zt, c1t=c1t, c2t=c2t, ch=ch, w=w: e.scalar_tensor_tensor(
                                out=c2t[:, 0:w], in0=zt[:, 0:w], scalar=convF[:, 0, ch:ch + 1], in1=c1t[:, 0:w],
                                op0=ALU.mult, op1=ALU.add), reads=[zb, c1b], writes=[c2b])
                            P.op("dve", lambda e, zt=zt, c2t=c2t, ch=ch, w=w: e.scalar_tensor_tensor(
                                out=c2t[:, 0:w], in0=zt[:, 2:w + 2], scalar=convF[:, 2, ch:ch + 1], in1=c2t[:, 0:w],
                                op0=ALU.mult, op1=ALU.add), reads=[zb, c2b], writes=[c2b])
                            outs.append((c2t, c2b))
                        sgt, sgb = sg_ring.next()
                        P.op("act", lambda e, sgt=sgt, c2t=outs[0][0], w=w: e.activation(out=sgt[:, 0:w], in_=c2t[:, 0:w],
                                                                                          func=AF.Silu),
                             reads=[outs[0][1]], writes=[sgb])
                        P.op("pool", lambda e, sgt=sgt, c2t=outs[1][0], u=u, w=w: e.tensor_tensor(
                            out=aT[:, u, 1:w + 1], in0=sgt[:, 0:w], in1=c2t[:, 0:w], op=ALU.mult),
                             reads=[sgb, outs[1][1]], pwrites=[BaT])
                        if u >= 2:
                            stp.step(3)

                def p3_C(cx, stp):
                    (s, off, t0, w, C, first, last, jlo, jhi, G, tokc, x1t, Bx1) = cx
                    for g in range(G):
                        r = min(128, C - 128 * g)
                        lo = max(1, 128 * g) - 128 * g
                        hi = min(w + 1, 128 * g + r) - 128 * g
                        for hf in range(2):
                            zt, zb = zp_ring.next()
                            P.mm(zt[0:r, :], [(aT[:, u, 128 * g:128 * g + r], w_dn_sb[:, u, hf * 512:(hf + 1) * 512])
                                              for u in range(NU)], reads=[BaT, Bw], writes=[zb])
                            stp.step(6)
                            tt, tb = tmp_ring.next()
                            P.op("dve", lambda e, zt=zt, tt=tt, r=r, hf=hf: e.tensor_tensor(
                                out=tt[0:r, :], in0=zt[0:r, :], in1=g2bc[0:r, hf * 512:(hf + 1) * 512], op=ALU.mult),
                                 reads=[zb, Bg2], writes=[tb])
                            P.op("dve", lambda e, tt=tt, r=r, hf=hf, g=g: e.tensor_tensor(
                                out=x1t[0:r, g, hf * 512:(hf + 1) * 512], in0=tt[0:r, :], in1=x1t[0:r, g, hf * 512:(hf + 1) * 512],
                                op=ALU.add), reads=[tb, Bx1[g]], writes=[Bx1[g]])
                        P.op("act", lambda e, g=g: e.activation(out=junk[:], in_=x1t[:, g, :], func=AF.Square, scale=1.0 / 32.0,
                                                                accum_out=ms[:, 4 + g:5 + g]), reads=[Bx1[g]], writes=[Bst2[g]])
                        P.op("act", lambda e, g=g: e.activation(out=sd[:, 4 + g:5 + g], in_=ms[:, 4 + g:5 + g], func=AF.Sqrt,
                                                                bias=epsT[:], scale=1.0), reads=[Bst2[g]], writes=[Bst2[g]])
                        P.op("dve", lambda e, g=g: e.reciprocal(out=rstd[:, 4 + g:5 + g], in_=sd[:, 4 + g:5 + g]),
                             reads=[Bst2[g]], writes=[Bst2[g]])
                        P.op("dve", lambda e, g=g: e.scalar_tensor_tensor(
                            out=x1t[:, g, :], in0=x1t[:, g, :], scalar=rstd[:, 4 + g:5 + g], in1=fgbc[:], op0=ALU.mult, op1=ALU.mult),
                             reads=[Bst2[g], Bfg, Bx1[g]], writes=[Bx1[g]])
                        if hi > lo:
                            tok0 = tokc + 128 * g
                            P.dma("pool", y_d[tok0 + lo:tok0 + hi, :], x1t[lo:hi, g, :],
                                  (ds_out if x1t is x1 else ds_outb)[g], reads=[Bx1[g]])

                tiles3 = []
                for s in range(NS):
                    tl = seq_tiles(seqs[s])
                    for ti, (t0, w) in enumerate(tl):
                        tiles3.append((s, ti, t0, w, len(tl)))
                cx = Stepper3(p3_A1(*tiles3[0])).finish()
                Stepper3(p3_A2(cx)).finish()
                for n_ in range(len(tiles3)):
                    stA1 = Stepper3(p3_A1(*tiles3[n_ + 1]) if n_ + 1 < len(tiles3) else None)
                    p3_B(cx, stA1)
                    ncx = stA1.finish()
                    stA2 = Stepper3(p3_A2(ncx) if ncx is not None else None)
                    p3_C(cx, stA2)
                    stA2.finish()
                    cx = ncx

                P.emit_phase()
                if DEBUG_STOP == 3:
                    P.disabled = True
        except _Stop:
            pass
        P.final_wait()
    return nc


def rope_tables_np(smax):
    inv = (1.0 / (np.float32(THETA) ** (np.arange(0, DR, 2, dtype=np.float32) / np.float32(DR)))).astype(np.float32)
    ang = (np.arange(-1, smax + 1, dtype=np.float32)[None, :] * inv[:, None]).astype(np.float32)
    cos = np.cos(ang).astype(np.float32)
    sin = np.sin(ang).astype(np.float32)
    return np.concatenate([cos, cos], 0), np.concatenate([sin, sin], 0)


_CACHE = {}


def run_cores(seqs, xs, cs, weights):
    key = tuple(seqs)
    if key not in _CACHE:
        _CACHE[key] = build_program(list(seqs))
    nc = _CACHE[key]
    cos2, sin2 = rope_tables_np(max(seqs))
    shared = dict(weights)
    shared["cos2"] = cos2
    shared["sin2"] = sin2
    in_maps = []
    for i in range(len(xs)):
        m = dict(shared)
        m["x"] = np.ascontiguousarray(xs[i], dtype=np.float32)
        m["c"] = np.ascontiguousarray(cs[i], dtype=np.float32)
        in_maps.append(m)
    res = run_bass_kernel_spmd(nc, in_maps, core_ids=list(range(len(xs))))
    return [r["y"] for r in res.results]


def prep_weights(w_ada, b_ada, norm1_g, w_in, conv_a_w, q_norm_g, w_uq, kv_norm_g, w_ukv, out_norm_a_g, out_norm_b_g,
                 w_o, norm2_g, w_up, ffn_conv_w, w_down, final_g):
    f = lambda a: np.ascontiguousarray(np.asarray(a, dtype=np.float32))
    return dict(
        w_ada=f(w_ada[0]), b_ada=f(b_ada[0]).reshape(1, -1), norm1_g=f(norm1_g[0]).reshape(1, -1), w_in=f(w_in[0]),
        conv_a_w=f(conv_a_w[0]), q_norm_g=f(q_norm_g[0]).reshape(1, -1), w_uq=f(w_uq[0]),
        kv_norm_g=f(kv_norm_g[0]).reshape(1, -1), w_ukv=f(w_ukv[0]), out_norm_a_g=f(out_norm_a_g[0]).reshape(1, -1),
        out_norm_b_g=f(out_norm_b_g[0]).reshape(1, -1), w_o=f(w_o[0]), norm2_g=f(norm2_g[0]).reshape(1, -1),
        w_up=f(w_up[0]), ffn_conv_w=f(ffn_conv_w[0]), w_down=f(w_down[0]), final_g=f(final_g).reshape(1, -1))


def kernel(x_prompt, x_sample, c_prompt, c_sample, w_ada, b_ada, norm1_g, w_in, conv_a_w, q_norm_g, w_uq, kv_norm_g,
           w_ukv, out_norm_a_g, out_norm_b_g, w_o, norm2_g, w_up, ffn_conv_w, w_down, final_g):
    x_prompt = np.asarray(x_prompt, dtype=np.float32)
    x_sample = np.asarray(x_sample, dtype=np.float32)
    c_prompt = np.asarray(c_prompt, dtype=np.float32)
    c_sample = np.asarray(c_sample, dtype=np.float32)
    weights = prep_weights(w_ada, b_ada, norm1_g, w_in, conv_a_w, q_norm_g, w_uq, kv_norm_g, w_ukv, out_norm_a_g,
                           out_norm_b_g, w_o, norm2_g, w_up, ffn_conv_w, w_down, final_g)
    SS = x_sample.shape[1]
    SP = x_prompt.shape[1]
    seqs = (SS, SP, SP)
    xs, cs = [], []
    for i in range(N_CORES):
        xs.append(np.concatenate([x_sample[i], x_prompt[2 * i], x_prompt[2 * i + 1]], axis=0))
        cs.append(np.stack([c_sample[i], c_prompt[2 * i], c_prompt[2 * i + 1]], axis=0))
    ys = run_cores(seqs, xs, cs, weights)
    y_prompt = np.empty_like(x_prompt)
    y_sample = np.empty_like(x_sample)
    for i in range(N_CORES):
        y = ys[i]
        y_sample[i] = y[0:SS]
        y_prompt[2 * i] = y[SS:SS + SP]
        y_prompt[2 * i + 1] = y[SS + SP:SS + 2 * SP]
    return (y_prompt, y_sample)
```

```python
import math
from contextlib import ExitStack

import numpy as np
import concourse.bass as bass
import concourse.mybir as mybir
from concourse.bass_utils import run_bass_kernel_spmd

F32 = mybir.dt.float32
BF16 = mybir.dt.bfloat16
AF = mybir.ActivationFunctionType
ALU = mybir.AluOpType

D = 1024
A_W = 512
NH = 4
DN = 128
DR = 64
DV = 128
QL = 384
KVL = 256
INC = 2240
DFF = 2816
NU = DFF // 128
ATTN_SCALE = 1.0 / math.sqrt(DN + DR)
EPS = 1e-6
THETA = 10000.0
N_CORES = 8
SAME_ENGINE_SYNC = True

ENGS = ["pe", "act", "dve", "pool", "sp"]


class Buf:
    __slots__ = ("w", "r", "war")

    def __init__(self):
        self.w = {}
        self.r = {}
        self.war = {}


def _mg(d, k, v):
    if d.get(k, -1) < v:
        d[k] = v


class DSem:
    __slots__ = ("h", "val", "val0")

    def __init__(self, h):
        self.h = h
        self.val = 0
        self.val0 = 0


class Op:
    __slots__ = ("eng", "idx", "fn", "deps", "sig", "dsem", "dval")


class Prog:
    def __init__(self, nc, es):
        self.nc = nc
        self.es = es
        self.eh = dict(pe=nc.tensor, act=nc.scalar, dve=nc.vector, pool=nc.gpsimd, sp=nc.sync)
        self.psem = {e: es.enter_context(nc.semaphore("pg_" + e)) for e in ENGS}
        self.cnt = {e: 0 for e in ENGS}
        self.waited = {e: {} for e in ENGS}
        self.ops = {e: [] for e in ENGS}
        self.dsems = []
        self.first_phase = True
        self.disabled = False

    def dsem(self, name):
        d = DSem(self.es.enter_context(self.nc.semaphore("d_" + name)))
        self.dsems.append(d)
        return d

    def op(self, eng, fn, reads=(), writes=(), pwrites=(), dsem=None):
        if self.disabled:
            return None
        self.nops = getattr(self, 'nops', 0) + 1
        if self.nops > DEBUG_MAXOPS:
            return None
        deps = {}
        for b in reads:
            for k, v in b.w.items():
                _mg(deps, k, v)
        for b in writes:
            war = {}
            for k, v in b.r.items():
                _mg(war, k, v)
            for k, v in b.w.items():
                _mg(war, k, v)
            b.war = war
            for k, v in war.items():
                _mg(deps, k, v)
        newver = []
        for b in pwrites:
            if b.r or not b.w:
                war = {}
                for k, v in b.r.items():
                    _mg(war, k, v)
                for k, v in b.w.items():
                    _mg(war, k, v)
                b.war = war
                newver.append(b)
            for k, v in b.war.items():
                _mg(deps, k, v)
        o = Op()
        o.eng = eng
        o.idx = len(self.ops[eng])
        o.fn = fn
        o.deps = deps
        o.sig = False
        o.dsem = dsem
        if dsem is not None:
            dsem.val += 16
            o.dval = dsem.val
            key, val = dsem, dsem.val
        else:
            o.dval = 0
            key, val = eng, o.idx
        self.ops[eng].append(o)
        for k, v in deps.items():
            if isinstance(k, str):
                self.ops[k][v].sig = True
        for b in reads:
            _mg(b.r, key, val)
        for b in writes:
            b.w = {key: val}
            b.r = {}
        for b in newver:
            b.w = {}
            b.r = {}
        for b in pwrites:
            _mg(b.w, key, val)
        return o

    def dma(self, q, out, in_, dsem, reads=(), writes=(), pwrites=(), nonc=False):
        nc = self.nc

        def fn(e):
            if nonc:
                with nc.allow_non_contiguous_dma("small strided setup load"):
                    return e.dma_start(out=out, in_=in_)
            return e.dma_start(out=out, in_=in_)

        return self.op(q, fn, reads, writes, pwrites, dsem=dsem)

    def mm(self, out, pairs, reads=(), writes=(), pwrites=(), first=True, last=True):
        def fn(e):
            n = len(pairs)
            ins = None
            for i, (l, r) in enumerate(pairs):
                ins = e.matmul(out, l, r, start=(first and i == 0), stop=(last and i == n - 1))
            return ins

        return self.op("pe", fn, reads, writes, pwrites)

    def emit_phase(self):
        if self.disabled:
            return
        nc = self.nc
        for e in ENGS:
            for o in reversed(self.ops[e]):
                if o.dsem is None:
                    o.sig = True
                    break
        signo = {}
        for e in ENGS:
            c = self.cnt[e]
            lst = []
            for o in self.ops[e]:
                if o.sig and o.dsem is None:
                    c += 1
                lst.append(c)
            signo[e] = lst
        fence = None
        if not self.first_phase:
            fence = ([(self.psem[k], self.cnt[k], k) for k in ENGS if self.cnt[k] > 0]
                     + [(d.h, d.val0, d) for d in self.dsems if getattr(d, "val0", 0) > 0])
        with nc.Block() as block:
            reg = dict(pe=block.tensor, act=block.scalar, dve=block.vector, pool=block.gpsimd, sp=block.sync)
            for e in ENGS:
                ops = self.ops[e]

                def body(engh, e=e, ops=ops):
                    wd = self.waited[e]
                    if fence is not None:
                        for h, v, k in fence:
                            if k == e:
                                continue
                            if wd.get(k, 0) >= v:
                                continue
                            engh.wait_ge(h, v)
                            wd[k] = v
                    for o in ops:
                        for k, v in o.deps.items():
                            if isinstance(k, str):
                                if k == e and (e == "pe" or not SAME_ENGINE_SYNC):
                                    continue
                                need = signo[k][v]
                                h = self.psem[k]
                            else:
                                need = v
                                h = k.h
                            if wd.get(k, 0) >= need:
                                continue
                            engh.wait_ge(h, need)
                            wd[k] = need
                        ins = o.fn(engh)
                        if o.dsem is not None:
                            ins.then_inc(o.dsem.h, 16)
                        elif o.sig:
                            ins.then_inc(self.psem[e], 1)

                reg[e](body)
        for e in ENGS:
            if self.ops[e]:
                self.cnt[e] = signo[e][-1]
            self.ops[e] = []
        for d in self.dsems:
            d.val0 = d.val
        self.first_phase = False

    def final_wait(self):
        nc = self.nc
        with nc.Block() as block:
            def body(engh):
                for k in ENGS:
                    if k != "sp" and self.cnt[k] > 0:
                        engh.wait_ge(self.psem[k], self.cnt[k])
                for d in self.dsems:
                    if d.val > 0:
                        engh.wait_ge(d.h, d.val)
            block.sync(body)


class Ring:
    def __init__(self, items):
        self.items = items
        self.bufs = [Buf() for _ in items]
        self.i = 0

    def next(self):
        k = self.i % len(self.items)
        self.i += 1
        return self.items[k], self.bufs[k]


def seq_tiles(S):
    n = -(-S // 510)
    W = -(-S // n)
    out = []
    t = 0
    while t < S:
        w = min(W, S - t)
        out.append((t, w))
        t += w
    return out


DEBUG_STOP = 99
DEBUG_MAXOPS = 10 ** 9


class _Stop(Exception):
    pass


def build_program(seqs):
    NT = sum(seqs)
    offs = [sum(seqs[:i]) for i in range(len(seqs))]
    NS = len(seqs)
    SMAX = max(seqs)
    nc = bass.Bass("TRN2", target_bir_lowering=False)

    def din(name, shape, dt=F32):
        return nc.dram_tensor(name, list(shape), dt, kind="ExternalInput").ap()

    def dscr(name, shape, dt=BF16):
        return nc.dram_tensor(name, list(shape), dt, kind="Internal").ap()

    x_d = din("x", [NT, D])
    c_d = din("c", [NS, D])
    w_ada_d = din("w_ada", [D, 6 * D])
    b_ada_d = din("b_ada", [1, 6 * D])
    n1g_d = din("norm1_g", [1, D])
    w_in_d = din("w_in", [D, INC])
    conva_d = din("conv_a_w", [3, A_W])
    qng_d = din("q_norm_g", [1, QL])
    w_uq_d = din("w_uq", [QL, NH * (DN + DR)])
    kvng_d = din("kv_norm_g", [1, KVL])
    w_ukv_d = din("w_ukv", [KVL, NH * (DN + DV)])
    ang_d = din("out_norm_a_g", [1, A_W])
    bng_d = din("out_norm_b_g", [1, A_W])
    w_o_d = din("w_o", [D, D])
    n2g_d = din("norm2_g", [1, D])
    w_up_d = din("w_up", [D, 2 * DFF])
    convf_d = din("ffn_conv_w", [3, 2 * DFF])
    w_dn_d = din("w_down", [DFF, D])
    fg_d = din("final_g", [1, D])
    cos_d = din("cos2", [64, SMAX + 2])
    sin_d = din("sin2", [64, SMAX + 2])
    y_d = nc.dram_tensor("y", [NT, D], F32, kind="ExternalOutput").ap()

    qS = dscr("qS", [NH, 128, NT])
    qrS = dscr("qrS", [NH, 64, NT])
    kS = dscr("kS", [NH, 128, NT])
    krS = dscr("krS", [64, NT])
    vS = dscr("vS", [NT, NH * DV])
    mixS = dscr("mixS", [8, 128, NT])
    modS = dscr("modS", [NS, 6 * D], F32)
    wupS = dscr("wupS", [NU, 128, 8, 256])

    with ExitStack() as es:
        P = Prog(nc, es)

        def sbg(name, shape, dt):
            return es.enter_context(nc.sbuf_tensor(name, list(shape), dt))

        ident = sbg("ident", [128, 128], BF16)
        ones = sbg("ones", [128, 128], BF16)
        epsT = sbg("epsT", [128, 1], F32)
        onesf = sbg("onesf", [128, 128], F32)
        n1g = sbg("n1g", [128, 8], F32)
        n2g = sbg("n2g", [128, 8], F32)
        qg = sbg("qg", [128, 3], F32)
        kvg = sbg("kvg", [128, 2], F32)
        ag = sbg("ag", [128, 4], F32)
        bg = sbg("bg", [128, 4], F32)
        convA = sbg("convA", [128, 3, 4], F32)
        convF = sbg("convF", [128, 3, 44], F32)
        modT = sbg("modT", [128, NS, 48], F32)
        gm1 = sbg("gm1", [128, NS, 8], F32)
        gm2 = sbg("gm2", [128, NS, 8], F32)

        def sh1(s, c):
            return modT[:, s, 0 * 8 + c:0 * 8 + c + 1]

        def sh2(s, c):
            return modT[:, s, 3 * 8 + c:3 * 8 + c + 1]

        try:
            with ExitStack() as ph:
                def sb(name, shape, dt):
                    return ph.enter_context(nc.sbuf_tensor(name, list(shape), dt))

                identf = sb("identf", [128, 128], F32)
                cT = sb("cT", [128, 8, NS], F32)
                cA = sb("cA", [128, 8, NS], F32)
                modsb = sb("modsb", [NS, 6 * D], F32)
                bsb = sb("bsb", [NS, 6 * D], F32)
                wa = [sb("wa%d" % i, [128, 8, 512], F32) for i in range(3)]
                pm = [ph.enter_context(nc.psum_tensor("p0m%d" % i, [128, 512], F32)) for i in range(2)]
                B = {k: Buf() for k in ["identf", "ident", "ones", "eps", "cT", "cA", "modsb", "bsb", "small", "modT", "gm",
                                         "modS"]}
                ds_small = P.dsem("small")
                ds_c = P.dsem("c")
                ds_b = P.dsem("b")
                ds_mod = P.dsem("mod")
                ds_modT = P.dsem("modT")
                ds_wa = [P.dsem("wa%d" % i) for i in range(3)]
                wa_ring = Ring(list(zip(wa, ds_wa)))
                pm_ring = Ring(pm)


                P.op("pool", lambda e: e.memset(identf[:], 0.0), writes=[B["identf"]])
                P.op("pool", lambda e: e.affine_select(out=identf[:], in_=identf[:], pattern=[[-1, 128]],
                                                       compare_op=ALU.not_equal, fill=1.0, base=0,
                                                       channel_multiplier=1), reads=[B["identf"]], writes=[B["identf"]])
                P.op("dve", lambda e: e.tensor_copy(out=ident[:], in_=identf[:]), reads=[B["identf"]], writes=[B["ident"]])
                P.op("dve", lambda e: e.memset(ones[:], 1.0), writes=[B["ones"]])
                P.op("dve", lambda e: e.memset(onesf[:], 1.0), writes=[B["ones"]])
                P.op("dve", lambda e: e.memset(epsT[:], EPS), writes=[B["eps"]])

                def fm(dst, src, n):
                    P.dma("sp", dst, src.rearrange("o (c p) -> p (o c)", p=128), ds_small, pwrites=[B["small"]], nonc=True)

                fm(n1g[:], n1g_d, 8)
                fm(n2g[:], n2g_d, 8)
                fm(qg[:], qng_d, 3)
                fm(kvg[:], kvng_d, 2)
                fm(ag[:], ang_d, 4)
                fm(bg[:], bng_d, 4)
                for j in range(3):
                    P.dma("sp", convA[:, j, :], conva_d[j:j + 1, :].rearrange("o (c p) -> p (o c)", p=128), ds_small,
                          pwrites=[B["small"]], nonc=True)
                    for q4 in range(4):
                        P.dma("sp", convF[:, j, q4 * 11:(q4 + 1) * 11],
                              convf_d[j:j + 1, q4 * 1408:(q4 + 1) * 1408].rearrange("o (c p) -> p (o c)", p=128), ds_small,
                              pwrites=[B["small"]], nonc=True)
                for s in range(NS):
                    P.dma("sp", cT[:, :, s], c_d[s:s + 1, :].rearrange("o (c p) -> p (o c)", p=128), ds_c, pwrites=[B["cT"]],
                          nonc=True)
                P.dma("sp", bsb[:], b_ada_d.to_broadcast([NS, 6 * D]), ds_b, writes=[B["bsb"]])
                P.op("act", lambda e: e.activation(out=cA[:], in_=cT[:], func=AF.Silu), reads=[B["cT"]], writes=[B["cA"]])
                wav = w_ada_d.rearrange("(kc p) n -> p kc n", p=128)
                for j in range(12):
                    (wt, dsw), wb = wa_ring.next()
                    P.dma("sp", wt[:], wav[:, :, j * 512:(j + 1) * 512], dsw, writes=[wb])
                    pmt, pb = pm_ring.next()
                    P.mm(pmt[0:NS, :], [(cA[:, kc, :], wt[:, kc, :]) for kc in range(8)], reads=[B["cA"], wb], writes=[pb])
                    P.op("dve", lambda e, pmt=pmt, j=j: e.tensor_tensor(out=modsb[:, j * 512:(j + 1) * 512], in0=pmt[0:NS, :],
                                                                         in1=bsb[:, j * 512:(j + 1) * 512], op=ALU.add),
                         reads=[pb, B["bsb"]], pwrites=[B["modsb"]])
                P.dma("pool", modS, modsb[:], ds_mod, reads=[B["modsb"]], writes=[B["modS"]])
                for s in range(NS):
                    for v6 in range(6):
                        P.dma("sp", modT[:, s, v6 * 8:(v6 + 1) * 8],
                              modS[s:s + 1, v6 * D:(v6 + 1) * D].rearrange("o (j p) -> p (o j)", p=128), ds_modT,
                              reads=[B["modS"]], pwrites=[B["modT"]], nonc=True)
                for s in range(NS):
                    P.op("dve", lambda e, s=s: e.scalar_tensor_tensor(out=gm1[:, s, :], in0=modT[:, s, 8:16], scalar=1.0,
                                                                      in1=n1g[:], op0=ALU.add, op1=ALU.mult),
                         reads=[B["modT"], B["small"]], pwrites=[B["gm"]])
                    P.op("dve", lambda e, s=s: e.scalar_tensor_tensor(out=gm2[:, s, :], in0=modT[:, s, 32:40], scalar=1.0,
                                                                      in1=n2g[:], op0=ALU.add, op1=ALU.mult),
                         reads=[B["modT"], B["small"]], pwrites=[B["gm"]])
                P.op("dve", lambda e: e.tensor_scalar(out=qg[:], in0=qg[:], scalar1=ATTN_SCALE, scalar2=None, op0=ALU.mult),
                     reads=[B["small"]], writes=[B["small"]])
                P.emit_phase()
                if DEBUG_STOP == 0:
                    P.disabled = True

            with ExitStack() as ph:
                def sb(name, shape, dt):
                    return ph.enter_context(nc.sbuf_tensor(name, list(shape), dt))

                def pst(name, shape, dt):
                    return ph.enter_context(nc.psum_tensor(name, list(shape), dt))

                w_in_sb = sb("w_in_sb", [128, 8, INC], BF16)
                w_krr = sb("w_krr", [128, 8, 64], BF16)
                w_uq_sb = sb("w_uq_sb", [128, 3, 768], BF16)
                w_uqr = sb("w_uqr", [128, 3, 256], BF16)
                w_uk_sb = sb("w_uk_sb", [128, 2, 512], BF16)
                w_uv_sb = sb("w_uv_sb", [128, 2, 512], BF16)
                xin = [sb("xin%d" % i, [128, D], F32) for i in range(3)]
                junk = sb("junk", [128, D], BF16)
                ms = sb("ms", [128, 8], F32)
                sd = sb("sd", [128, 8], F32)
                rstd = sb("rstd", [128, 8], F32)
                xn = [sb("xn%d" % i, [128, 4, D], BF16) for i in range(2)]
                hT = [sb("hT%d" % i, [128, 8, 512], BF16) for i in range(2)]
                hasb = sb("hasb", [128, 4, 512], F32)
                psb = sb("psb", [128, 4, 512], F32)
                basb = sb("basb", [128, 4, 512], F32)
                cqsb = sb("cqsb", [128, 3, 512], F32)
                ckvsb = sb("ckvsb", [128, 2, 512], F32)
                sq = [sb("sq%d" % i, [128, 512], BF16) for i in range(4)]
                rs = [sb("rs%d" % i, [128, 512], F32) for i in range(2)]
                rr = [sb("rr%d" % i, [128, 512], F32) for i in range(3)]
                cqn = sb("cqn", [128, 3, 512], BF16)
                ckvn = sb("ckvn", [128, 2, 512], BF16)
                yan = sb("yan", [128, 4, 512], BF16)
                qTo = sb("qTo", [128, 4, 512], BF16)
                qro = sb("qro", [64, 4, 512], BF16)
                t1 = [sb("t1_%d" % i, [64, 512], F32) for i in range(2)]
                t2 = [sb("t2_%d" % i, [64, 512], F32) for i in range(2)]
                kTo = sb("kTo", [128, 4, 512], BF16)
                kro = sb("kro", [64, 512], BF16)
                vo = sb("vo", [128, 4, 512], BF16)
                cs = [sb("cs%d" % i, [64, 512], F32) for i in range(2)]
                sn = [sb("sn%d" % i, [64, 512], F32) for i in range(2)]
                tp = [pst("tp%d" % i, [128, 2, 512], BF16) for i in range(2)]
                zp = [pst("zp%d" % i, [128, 512], F32) for i in range(6)]

                tp_ring = Ring(tp)
                zp_ring = Ring(zp)
                sq_ring = Ring(sq)
                rs_ring = Ring(rs)
                rr_ring = Ring(rr)
                t1_ring = Ring(t1)
                t2_ring = Ring(t2)
                xin_ds = [P.dsem("xin%d" % i) for i in range(3)]
                xin_ring = Ring(list(zip(xin, xin_ds)))
                cs_ds = [P.dsem("cs%d" % i) for i in range(2)]
                cs_ring = Ring(list(zip(cs, sn, cs_ds)))
                xn_ring = Ring(xn)
                hT_ring = Ring(hT)
                ds_w = P.dsem("p1w")
                ds_wup = P.dsem("wup")
                ds_st = {k: P.dsem("st_" + k) for k in ["ya", "q", "qr", "k", "kr", "v"]}
                Bw = Buf()
                Bst1 = [Buf() for _ in range(4)]
                Bha = [Buf() for _ in range(4)]
                Bp = [Buf() for _ in range(4)]
                Bba = [Buf() for _ in range(4)]
                Bx = {k: Buf() for k in ["cqsb", "ckvsb", "cqn", "ckvn", "yan", "qTo", "qro", "kTo", "kro", "vo"]}

                P.dma("pool", w_in_sb[:], w_in_d.rearrange("(kc p) n -> p kc n", p=128), ds_w, pwrites=[Bw])
                P.dma("pool", w_uq_sb[:], w_uq_d.rearrange("(kc p) n -> p kc n", p=128), ds_w, pwrites=[Bw])
                wkv = w_ukv_d.rearrange("(kc p) (h t d) -> p kc h t d", p=128, h=NH, t=2)
                for kc in range(2):
                    P.dma("pool", w_uk_sb[:, kc, :].rearrange("p (h d) -> p h d", h=NH), wkv[:, kc, :, 0, :], ds_w, pwrites=[Bw])
                    P.dma("pool", w_uv_sb[:, kc, :].rearrange("p (h d) -> p h d", h=NH), wkv[:, kc, :, 1, :], ds_w, pwrites=[Bw])
                wupv = w_up_d.rearrange("(kc p) n -> p kc n", p=128)
                for u in range(NU):
                    P.dma("pool", wupS[u, :, :, 0:128], wupv[:, :, u * 128:(u + 1) * 128], ds_wup)
                    P.dma("pool", wupS[u, :, :, 128:256], wupv[:, :, DFF + u * 128:DFF + (u + 1) * 128], ds_wup)
                Bw2 = Buf()
                P.op("dve", lambda e: e.tensor_scalar(out=w_krr[:, :, 0:32], in0=w_in_sb[:, :, 2208:2240], scalar1=-1.0,
                                                      scalar2=None, op0=ALU.mult), reads=[Bw], pwrites=[Bw2])
                P.op("dve", lambda e: e.tensor_copy(out=w_krr[:, :, 32:64], in_=w_in_sb[:, :, 2176:2208]), reads=[Bw],
                     pwrites=[Bw2])
                for h in range(NH):
                    b0 = h * 192 + 128
                    P.op("dve", lambda e, h=h, b0=b0: e.tensor_scalar(out=w_uqr[:, :, h * 64:h * 64 + 32],
                                                                      in0=w_uq_sb[:, :, b0 + 32:b0 + 64], scalar1=-1.0,
                                                                      scalar2=None, op0=ALU.mult), reads=[Bw], pwrites=[Bw2])
                    P.op("dve", lambda e, h=h, b0=b0: e.tensor_copy(out=w_uqr[:, :, h * 64 + 32:h * 64 + 64],
                                                                    in_=w_uq_sb[:, :, b0:b0 + 32]), reads=[Bw], pwrites=[Bw2])
                for i in range(3):
                    P.op("pool", lambda e, i=i: e.memset(xin[i][:], 0.0), writes=[xin_ring.bufs[i]])

                WR = [Bw, Bw2]

                def p1_front_a(s, ti, t0, w, ntl):
                    S = seqs[s]
                    off = offs[s]
                    C = w + 2
                    first = (ti == 0)
                    last = (ti == ntl - 1)
                    jlo = 1 if first else 0
                    jhi = C - 1 if last else C
                    G = -(-C // 128)
                    xnt, xnb = xn_ring.next()
                    hTt, hTb = hT_ring.next()
                    (cst, snt, csd), csb = cs_ring.next()
                    P.dma("sp", cst[:, 0:C], cos_d[:, t0:t0 + C], csd, writes=[csb])
                    P.dma("sp", snt[:, 0:C], sin_d[:, t0:t0 + C], csd, pwrites=[csb])
                    for g in range(G):
                        r = min(128, C - 128 * g)
                        lo = max(jlo, 128 * g) - 128 * g
                        hi = min(jhi, 128 * g + r) - 128 * g
                        (xt, xd), xb = xin_ring.next()
                        tok0 = off + t0 - 1 + 128 * g
                        P.dma("sp", xt[lo:hi, :], x_d[tok0 + lo:tok0 + hi, :], xd, writes=[xb])
                        yield
                        P.op("act", lambda e, xt=xt, g=g: e.activation(out=junk[:], in_=xt[:], func=AF.Square,
                                                                       scale=1.0 / 32.0, accum_out=ms[:, g:g + 1]),
                             reads=[xb], writes=[Bst1[g]])
                        yield
                        P.op("act", lambda e, g=g: e.activation(out=sd[:, g:g + 1], in_=ms[:, g:g + 1], func=AF.Ln,
                                                                bias=epsT[:], scale=1.0), reads=[Bst1[g]], writes=[Bst1[g]])
                        yield
                        P.op("act", lambda e, g=g: e.activation(out=rstd[:, g:g + 1], in_=sd[:, g:g + 1], func=AF.Exp,
                                                                scale=-0.5), reads=[Bst1[g]], writes=[Bst1[g]])
                        yield
                        P.op("dve", lambda e, xt=xt, g=g, xnt=xnt: e.tensor_scalar(out=xnt[:, g, :], in0=xt[:],
                                                                                   scalar1=rstd[:, g:g + 1], scalar2=None,
                                                                                   op0=ALU.mult),
                             reads=[xb, Bst1[g]], pwrites=[xnb])
                        yield
                    return (s, off, t0, w, C, first, last, jlo, jhi, G, xnt, xnb, hTt, hTb, cst, snt, csb)

                def p1_front_b(cx):
                    (s, off, t0, w, C, first, last, jlo, jhi, G, xnt, xnb, hTt, hTb, cst, snt, csb) = cx
                    for cp in range(4):
                        tpt, tpb = tp_ring.next()
                        for c2 in range(2):
                            c = 2 * cp + c2
                            for g in range(G):
                                r = min(128, C - 128 * g)
                                P.op("pe", lambda e, tpt=tpt, g=g, r=r, c=c, c2=c2, xnt=xnt: e.transpose(
                                    tpt[:, c2, 128 * g:128 * g + r], xnt[0:r, g, c * 128:(c + 1) * 128], ident[0:r, 0:r]),
                                     reads=[xnb], pwrites=[tpb])
                        for c2 in range(2):
                            c = 2 * cp + c2
                            P.op("act", lambda e, tpt=tpt, c=c, c2=c2, hTt=hTt, s=s, C=C: e.activation(
                                out=hTt[:, c, 0:C], in_=tpt[:, c2, 0:C], func=AF.Identity, bias=sh1(s, c),
                                scale=gm1[:, s, c:c + 1]), reads=[tpb], pwrites=[hTb])

                class Stepper:
                    def __init__(self, gen):
                        self.gen = gen
                        self.done = gen is None
                        self.val = None

                    def step(self, n=1):
                        for _ in range(n):
                            if self.done:
                                return
                            try:
                                next(self.gen)
                            except StopIteration as e_:
                                self.done = True
                                self.val = e_.value

                    def finish(self):
                        while not self.done:
                            self.step()
                        return self.val

                def rstd_bc(stt, stb, n, scale):
                    rst, rsb = rs_ring.next()
                    P.op("act", lambda e: e.activation(out=rst[:, 0:n], in_=stt[:, 0:n], func=AF.Ln, bias=epsT[:], scale=scale),
                         reads=[stb], writes=[rsb])
                    rqt, rqb = rr_ring.next()
                    P.op("act", lambda e: e.activation(out=rqt[:, 0:n], in_=rst[:, 0:n], func=AF.Exp, scale=-0.5),
                         reads=[rsb], writes=[rqb])
                    return rqt, rqb

                def p1_back(cx, stp):
                    (s, off, t0, w, C, first, last, jlo, jhi, G, xnt, xnb, hTt, hTb, cst, snt, csb) = cx
                    def zgroup(col0, M, wsb=None, rot=False):
                        zt, zb = zp_ring.next()
                        if rot:
                            pairs = [(w_krr[:, kc, 0:64], hTt[:, kc, 0:C]) for kc in range(8)]
                        else:
                            pairs = [(w_in_sb[:, kc, col0:col0 + M], hTt[:, kc, 0:C]) for kc in range(8)]
                        P.mm(zt[0:M, 0:C], pairs, reads=[hTb] + WR, writes=[zb])
                        stp.step(2)
                        return zt, zb

                    for c in range(4):
                        zt, zb = zgroup(c * 128, 128)
                        P.op("act", lambda e, zt=zt, c=c, C=C: e.activation(out=hasb[:, c, 0:C], in_=zt[:, 0:C], func=AF.Copy),
                             reads=[zb], writes=[Bha[c]])
                    for c in range(4):
                        zt, zb = zgroup(1024 + c * 128, 128)
                        P.op("dve", lambda e, zt=zt, c=c, jlo=jlo, jhi=jhi: e.tensor_tensor(
                            out=psb[:, c, jlo:jhi], in0=zt[:, jlo:jhi], in1=hasb[:, c, jlo:jhi], op=ALU.mult),
                             reads=[zb, Bha[c]], pwrites=[Bp[c]])
                    if first:
                        P.op("pool", lambda e: e.memset(psb[:, :, 0:1], 0.0), pwrites=Bp)
                    if last:
                        P.op("pool", lambda e, C=C: e.memset(psb[:, :, C - 1:C], 0.0), pwrites=Bp)
                    for c in range(4):
                        zt, zb = zgroup(512 + c * 128, 128)
                        P.op("act", lambda e, zt=zt, c=c, C=C: e.activation(out=basb[:, c, 0:C], in_=zt[:, 0:C], func=AF.Copy),
                             reads=[zb], writes=[Bba[c]])
                    sqs = []
                    for c in range(3):
                        zt, zb = zgroup(1536 + c * 128, 128)
                        P.op("act", lambda e, zt=zt, c=c, C=C: e.activation(out=cqsb[:, c, 0:C], in_=zt[:, 0:C], func=AF.Copy),
                             reads=[zb], pwrites=[Bx["cqsb"]])
                        sqt, sqb = sq_ring.next()
                        P.op("act", lambda e, zt=zt, sqt=sqt, C=C: e.activation(out=sqt[:, 0:C], in_=zt[:, 0:C], func=AF.Square),
                             reads=[zb], writes=[sqb])
                        sqs.append((sqt, sqb))
                    stt, stb = zp_ring.next()
                    P.mm(stt[:, 0:C], [(ones[:], q[0][:, 0:C]) for q in sqs], reads=[q[1] for q in sqs], writes=[stb])
                    rqt, rqb = rstd_bc(stt, stb, C, 1.0 / QL)
                    for c in range(3):
                        P.op("dve", lambda e, c=c, rqt=rqt, C=C: e.scalar_tensor_tensor(
                            out=cqn[:, c, 0:C], in0=cqsb[:, c, 0:C], scalar=qg[:, c:c + 1], in1=rqt[:, 0:C],
                            op0=ALU.mult, op1=ALU.mult), reads=[Bx["cqsb"], rqb], pwrites=[Bx["cqn"]])
                    sqs = []
                    for c in range(2):
                        zt, zb = zgroup(1920 + c * 128, 128)
                        P.op("act", lambda e, zt=zt, c=c, C=C: e.activation(out=ckvsb[:, c, 0:C], in_=zt[:, 0:C], func=AF.Copy),
                             reads=[zb], pwrites=[Bx["ckvsb"]])
                        sqt, sqb = sq_ring.next()
                        P.op("act", lambda e, zt=zt, sqt=sqt, C=C: e.activation(out=sqt[:, 0:C], in_=zt[:, 0:C], func=AF.Square),
                             reads=[zb], writes=[sqb])
                        sqs.append((sqt, sqb))
                    stt, stb = zp_ring.next()
                    P.mm(stt[:, 0:C], [(ones[:], q[0][:, 0:C]) for q in sqs], reads=[q[1] for q in sqs], writes=[stb])
                    rkt, rkb = rstd_bc(stt, stb, C, 1.0 / KVL)
                    for c in range(2):
                        P.op("dve", lambda e, c=c, rkt=rkt, C=C: e.scalar_tensor_tensor(
                            out=ckvn[:, c, 0:C], in0=ckvsb[:, c, 0:C], scalar=kvg[:, c:c + 1], in1=rkt[:, 0:C],
                            op0=ALU.mult, op1=ALU.mult), reads=[Bx["ckvsb"], rkb], pwrites=[Bx["ckvn"]])
                    za, zab = zgroup(2176, 64)
                    zr, zrb = zgroup(0, 64, rot=True)
                    t1t, t1b = t1_ring.next()
                    t2t, t2b = t2_ring.next()
                    P.op("dve", lambda e, za=za, t1t=t1t, cst=cst, C=C: e.tensor_tensor(
                        out=t1t[:, 0:C], in0=za[0:64, 0:C], in1=cst[:, 0:C], op=ALU.mult), reads=[zab, csb], writes=[t1b])
                    P.op("dve", lambda e, zr=zr, t2t=t2t, snt=snt, C=C: e.tensor_tensor(
                        out=t2t[:, 0:C], in0=zr[0:64, 0:C], in1=snt[:, 0:C], op=ALU.mult), reads=[zrb, csb], writes=[t2b])
                    P.op("dve", lambda e, t1t=t1t, t2t=t2t, C=C: e.tensor_tensor(
                        out=kro[:, 0:C], in0=t1t[:, 0:C], in1=t2t[:, 0:C], op=ALU.add), reads=[t1b, t2b], writes=[Bx["kro"]])
                    P.dma("pool", krS[:, off + t0:off + t0 + w], kro[:, 1:w + 1], ds_st["kr"], reads=[Bx["kro"]])
                    ncx = stp.finish()
                    if ncx is not None:
                        p1_front_b(ncx)
                    for h in range(NH):
                        zt, zb = zp_ring.next()
                        P.mm(zt[:, 0:C], [(w_uq_sb[:, kc, h * 192:h * 192 + 128], cqn[:, kc, 0:C]) for kc in range(3)],
                             reads=[Bx["cqn"]] + WR, writes=[zb])
                        P.op("act", lambda e, zt=zt, h=h, C=C: e.activation(out=qTo[:, h, 0:C], in_=zt[:, 0:C], func=AF.Copy),
                             reads=[zb], pwrites=[Bx["qTo"]])
                        za, zab = zp_ring.next()
                        P.mm(za[0:64, 0:C], [(w_uq_sb[:, kc, h * 192 + 128:h * 192 + 192], cqn[:, kc, 0:C]) for kc in range(3)],
                             reads=[Bx["cqn"]] + WR, writes=[zab])
                        zr, zrb = zp_ring.next()
                        P.mm(zr[0:64, 0:C], [(w_uqr[:, kc, h * 64:(h + 1) * 64], cqn[:, kc, 0:C]) for kc in range(3)],
                             reads=[Bx["cqn"]] + WR, writes=[zrb])
                        t1t, t1b = t1_ring.next()
                        t2t, t2b = t2_ring.next()
                        P.op("dve", lambda e, za=za, t1t=t1t, cst=cst, C=C: e.tensor_tensor(
                            out=t1t[:, 0:C], in0=za[0:64, 0:C], in1=cst[:, 0:C], op=ALU.mult), reads=[zab, csb], writes=[t1b])
                        P.op("dve", lambda e, zr=zr, t2t=t2t, snt=snt, C=C: e.tensor_tensor(
                            out=t2t[:, 0:C], in0=zr[0:64, 0:C], in1=snt[:, 0:C], op=ALU.mult), reads=[zrb, csb], writes=[t2b])
                        P.op("dve", lambda e, t1t=t1t, t2t=t2t, h=h, C=C: e.tensor_tensor(
                            out=qro[:, h, 0:C], in0=t1t[:, 0:C], in1=t2t[:, 0:C], op=ALU.add),
                             reads=[t1b, t2b], pwrites=[Bx["qro"]])
                    P.dma("pool", qS[:, :, off + t0:off + t0 + w].rearrange("h p t -> p h t"), qTo[:, :, 1:w + 1],
                          ds_st["q"], reads=[Bx["qTo"]])
                    P.dma("pool", qrS[:, :, off + t0:off + t0 + w].rearrange("h p t -> p h t"), qro[:, :, 1:w + 1],
                          ds_st["qr"], reads=[Bx["qro"]])
                    for h in range(NH):
                        zt, zb = zp_ring.next()
                        P.mm(zt[:, 0:C], [(w_uk_sb[:, kc, h * 128:(h + 1) * 128], ckvn[:, kc, 0:C]) for kc in range(2)],
                             reads=[Bx["ckvn"]] + WR, writes=[zb])
                        P.op("act", lambda e, zt=zt, h=h, C=C: e.activation(out=kTo[:, h, 0:C], in_=zt[:, 0:C], func=AF.Copy),
                             reads=[zb], pwrites=[Bx["kTo"]])
                    P.dma("pool", kS[:, :, off + t0:off + t0 + w].rearrange("h p t -> p h t"), kTo[:, :, 1:w + 1],
                          ds_st["k"], reads=[Bx["kTo"]])
                    for g in range(G):
                        r = min(128, C - 128 * g)
                        zt, zb = zp_ring.next()
                        P.mm(zt[0:r, :], [(ckvn[:, kc, 128 * g:128 * g + r], w_uv_sb[:, kc, :]) for kc in range(2)],
                             reads=[Bx["ckvn"]] + WR, writes=[zb])
                        P.op("dve", lambda e, zt=zt, g=g, r=r: e.tensor_copy(out=vo[0:r, g, :], in_=zt[0:r, :]),
                             reads=[zb], pwrites=[Bx["vo"]])
                    for g in range(G):
                        r = min(128, C - 128 * g)
                        lo = max(1, 128 * g) - 128 * g
                        hi = min(w + 1, 128 * g + r) - 128 * g
                        if hi <= lo:
                            continue
                        tok0 = off + t0 - 1 + 128 * g
                        P.dma("pool", vS[tok0 + lo:tok0 + hi, :], vo[lo:hi, g, :], ds_st["v"], reads=[Bx["vo"]])

                    ya = hasb
                    Bya = Bha
                    for c in range(4):
                        P.op("dve", lambda e, c=c, w=w: e.tensor_scalar(out=ya[:, c, 1:w + 1], in0=psb[:, c, 1:w + 1],
                                                                        scalar1=convA[:, 1, c:c + 1], scalar2=None,
                                                                        op0=ALU.mult), reads=[Bp[c]], writes=[Bya[c]])
                    for c in range(4):
                        P.op("dve", lambda e, c=c, w=w: e.scalar_tensor_tensor(
                            out=ya[:, c, 1:w + 1], in0=psb[:, c, 0:w], scalar=convA[:, 0, c:c + 1], in1=ya[:, c, 1:w + 1],
                            op0=ALU.mult, op1=ALU.add), reads=[Bp[c], Bya[c]], writes=[Bya[c]])
                    for c in range(4):
                        P.op("dve", lambda e, c=c, w=w: e.scalar_tensor_tensor(
                            out=ya[:, c, 1:w + 1], in0=psb[:, c, 2:w + 2], scalar=convA[:, 2, c:c + 1], in1=ya[:, c, 1:w + 1],
                            op0=ALU.mult, op1=ALU.add), reads=[Bp[c], Bya[c]], writes=[Bya[c]])
                    sqs = []
                    for c in range(4):
                        P.op("dve", lambda e, c=c, w=w: e.tensor_tensor(out=ya[:, c, 1:w + 1], in0=ya[:, c, 1:w + 1],
                                                                        in1=basb[:, c, 1:w + 1], op=ALU.mult),
                             reads=[Bba[c], Bya[c]], writes=[Bya[c]])
                    for c in range(4):
                        sqt, sqb = sq_ring.next()
                        P.op("act", lambda e, c=c, w=w, sqt=sqt: e.activation(out=sqt[:, 0:w], in_=ya[:, c, 1:w + 1],
                                                                              func=AF.Square), reads=[Bya[c]], writes=[sqb])
                        sqs.append((sqt, sqb))
                    stt, stb = zp_ring.next()
                    P.mm(stt[:, 0:w], [(ones[:], q[0][:, 0:w]) for q in sqs], reads=[q[1] for q in sqs], writes=[stb])
                    rat, rab = rstd_bc(stt, stb, w, 1.0 / A_W)
                    for c in range(4):
                        P.op("dve", lambda e, c=c, rat=rat, w=w: e.scalar_tensor_tensor(
                            out=yan[:, c, 0:w], in0=ya[:, c, 1:w + 1], scalar=ag[:, c:c + 1], in1=rat[:, 0:w],
                            op0=ALU.mult, op1=ALU.mult), reads=[Bya[c], rab], pwrites=[Bx["yan"]])
                    P.dma("pool", mixS[0:4, :, off + t0:off + t0 + w].rearrange("c p t -> p c t"), yan[:, :, 0:w],
                          ds_st["ya"], reads=[Bx["yan"]])
                    return ncx

                tiles_all = []
                for s in range(NS):
                    tl = seq_tiles(seqs[s])
                    for ti, (t0, w) in enumerate(tl):
                        tiles_all.append((s, ti, t0, w, len(tl)))
                st0 = Stepper(p1_front_a(*tiles_all[0]))
                cx = st0.finish()
                p1_front_b(cx)
                for n_ in range(len(tiles_all)):
                    stp = Stepper(p1_front_a(*tiles_all[n_ + 1]) if n_ + 1 < len(tiles_all) else None)
                    cx = p1_back(cx, stp)
                P.emit_phase()
                if DEBUG_STOP == 1:
                    P.disabled = True

            with ExitStack() as ph:
                def sb(name, shape, dt):
                    return ph.enter_context(nc.sbuf_tensor(name, list(shape), dt))

                def pst(name, shape, dt):
                    return ph.enter_context(nc.psum_tensor(name, list(shape), dt))

                kT = sb("kT", [128, NH, SMAX], BF16)
                krT = sb("krT", [128, SMAX], BF16)
                V = sb("V", [128, SMAX // 128, NH * DV], BF16)
                qT = [sb("qT%d" % i, [128, NH, 512], BF16) for i in range(2)]
                qrT = [sb("qrT%d" % i, [128, NH, 512], BF16) for i in range(2)]
                pT = [sb("pT%d" % i, [128, 1024], BF16) for i in range(4)]
                acc = [sb("acc%d" % i, [128, 1024], F32) for i in range(2)]
                yb = sb("yb", [128, NH, 512], F32)
                rec = [sb("rec%d" % i, [128, 512], F32) for i in range(2)]
                sq = [sb("sq2_%d" % i, [128, 512], BF16) for i in range(4)]
                rs2 = sb("rs2", [128, 512], F32)
                rb = sb("rb", [128, 512], F32)
                ybn = [sb("ybn%d" % i, [128, NH, 512], BF16) for i in range(1)]
                sps = [pst("sps%d" % i, [128, 512], F32) for i in range(6)]
                ops_ = [pst("ops%d" % i, [128, 512], F32) for i in range(2)]
                s_ring = Ring(sps)
                o_ring = Ring(ops_)
                acc_ring = Ring(acc)
                pT_ring = Ring(pT)
                pT_hb = [[Buf(), Buf()] for _ in range(4)]
                rec_ring = Ring(rec)
                sq_ring = Ring(sq)
                q_ds = [P.dsem("q2_%d" % i) for i in range(2)]
                q_ring = Ring(list(zip(qT, qrT, q_ds)))
                ybn_ds = [P.dsem("ybn%d" % i) for i in range(1)]
                ybn_ring = Ring(list(zip(ybn, ybn_ds)))
                ds_kv = P.dsem("kv")
                Bkv = Buf()
                P.op("pool", lambda e: e.memset(krT[64:128, :], 0.0), writes=[Bkv])
                for i in range(2):
                    P.op("pool", lambda e, i=i: e.memset(qrT[i][64:128, :, :], 0.0), writes=[q_ring.bufs[i]])
                Byb = [Buf() for _ in range(NH)]
                Brs = Buf()
                Brb = Buf()

                for s in range(NS):
                    S = seqs[s]
                    off = offs[s]
                    NKC = S // 128
                    P.dma("sp", kT[:, :, 0:S], kS[:, :, off:off + S].rearrange("h p t -> p h t"), ds_kv, pwrites=[Bkv])
                    P.dma("sp", krT[0:64, 0:S], krS[:, off:off + S], ds_kv, pwrites=[Bkv])
                    vsrc = vS[off:off + S, :].rearrange("(c p) d -> p c d", p=128)
                    step = 16
                    for c0 in range(0, NKC, step):
                        c1 = min(NKC, c0 + step)
                        P.dma("sp", V[:, c0:c1, :], vsrc[:, c0:c1, :], ds_kv, pwrites=[Bkv])
                    NQ = S // 512
                    assert NKC % 2 == 0
                    LOOK = 2
                    DEFER = 3
                    items = [(qi, h, kc) for qi in range(NQ) for h in range(NH) for kc in range(NKC)]
                    qts = {}
                    hst = {}
                    sqs_q = {}
                    pend = []
                    deferred = {}

                    def load_q(qi):
                        if qi >= NQ or qi in qts:
                            return
                        q0 = qi * 512
                        (qt, qrt, qd), qb = q_ring.next()
                        P.dma("sp", qt[:], qS[:, :, off + q0:off + q0 + 512].rearrange("h p t -> p h t"), qd, writes=[qb])
                        P.dma("sp", qrt[0:64, :, :], qrS[:, :, off + q0:off + q0 + 512].rearrange("h p t -> p h t"), qd,
                              pwrites=[qb])
                        qts[qi] = (qt, qrt, qb)

                    def head_state(qi, h):
                        if (qi, h) not in hst:
                            ot, ob = o_ring.next()
                            acc2, accb_ = acc_ring.next()
                            hst[(qi, h)] = dict(ot=ot, ob=ob, acc2=acc2, accb=accb_, cur=None)
                        return hst[(qi, h)]

                    def qk(item):
                        qi, h, kc = item
                        qt, qrt, qb = qts[qi]
                        hs = head_state(qi, h)
                        st_, stb_ = s_ring.next()
                        P.mm(st_[:], [(kT[:, h, kc * 128:(kc + 1) * 128], qt[:, h, :]),
                                      (krT[:, kc * 128:(kc + 1) * 128], qrt[:, h, :])], reads=[Bkv, qb], writes=[stb_])
                        if kc % 2 == 0:
                            slot = pT_ring.i % 4
                            ptp, _ = pT_ring.next()
                            hs["cur"] = (ptp, pT_hb[slot])
                        ptp, hb = hs["cur"]
                        half = kc % 2
                        P.op("act", lambda e, st_=st_, ptp=ptp, half=half: e.activation(
                            out=ptp[:, half * 512:(half + 1) * 512], in_=st_[:], func=AF.Exp), reads=[stb_], writes=[hb[half]])
                        pend.append((item, ptp, hb))

                    def pv():
                        (qi, h, kc), ptp, hb = pend.pop(0)
                        hs = hst[(qi, h)]
                        ot, ob, acc2, accb_ = hs["ot"], hs["ob"], hs["acc2"], hs["accb"]
                        half = kc % 2
                        f = (kc == 0)
                        l = (kc == NKC - 1)
                        if f:
                            P.mm(ot[:], [(V[:, kc, h * DV:(h + 1) * DV], ptp[:, half * 512:(half + 1) * 512])],
                                 reads=[Bkv, hb[half]], writes=[ob], first=f, last=l)
                        else:
                            P.mm(ot[:], [(V[:, kc, h * DV:(h + 1) * DV], ptp[:, half * 512:(half + 1) * 512])],
                                 reads=[Bkv, hb[half]], pwrites=[ob], first=f, last=l)
                        if half == 1:
                            if kc == 1:
                                P.op("dve", lambda e, ptp=ptp, acc2=acc2: e.tensor_copy(out=acc2[:], in_=ptp[:]),
                                     reads=[hb[0], hb[1]], writes=[accb_])
                            else:
                                P.op("dve", lambda e, ptp=ptp, acc2=acc2: e.tensor_tensor(out=acc2[:], in0=acc2[:], in1=ptp[:],
                                                                                          op=ALU.add),
                                     reads=[hb[0], hb[1], accb_], writes=[accb_])

                    def tail(qi, h):
                        hs = hst.pop((qi, h))
                        ot, ob, acc2, accb_ = hs["ot"], hs["ob"], hs["acc2"], hs["accb"]
                        P.op("dve", lambda e, acc2=acc2: e.tensor_tensor(out=acc2[:, 0:512], in0=acc2[:, 0:512],
                                                                         in1=acc2[:, 512:1024], op=ALU.add),
                             reads=[accb_], writes=[accb_])
                        mt, mb = s_ring.next()
                        P.mm(mt[:], [(onesf[:], acc2[:, 0:512])], reads=[accb_], writes=[mb])
                        ret, reb = rec_ring.next()
                        P.op("act", lambda e, mt=mt, ret=ret: e.activation(out=ret[:], in_=mt[:], func=AF.Ln), reads=[mb],
                             writes=[reb])
                        P.op("act", lambda e, ret=ret: e.activation(out=ret[:], in_=ret[:], func=AF.Exp, scale=-1.0),
                             reads=[reb], writes=[reb])
                        P.op("dve", lambda e, ot=ot, ret=ret, h=h: e.tensor_tensor(out=yb[:, h, :], in0=ot[:], in1=ret[:],
                                                                                  op=ALU.mult),
                             reads=[ob, reb], writes=[Byb[h]])
                        sqt, sqb = sq_ring.next()
                        P.op("act", lambda e, sqt=sqt, h=h: e.activation(out=sqt[:], in_=yb[:, h, :], func=AF.Square),
                             reads=[Byb[h]], writes=[sqb])
                        sqs_q.setdefault(qi, []).append((sqt, sqb))
                        if h == NH - 1:
                            qtile_end(qi)

                    def qtile_end(qi):
                        q0 = qi * 512
                        sqs = sqs_q.pop(qi)
                        stt, stb = s_ring.next()
                        P.mm(stt[:], [(ones[:], q[0][:]) for q in sqs], reads=[q[1] for q in sqs], writes=[stb])
                        P.op("act", lambda e, stt=stt: e.activation(out=rs2[:], in_=stt[:], func=AF.Ln, bias=epsT[:],
                                                                   scale=1.0 / A_W), reads=[stb], writes=[Brs])
                        P.op("act", lambda e: e.activation(out=rb[:], in_=rs2[:], func=AF.Exp, scale=-0.5), reads=[Brs],
                             writes=[Brb])
                        (ybt, ybd), ybb = ybn_ring.next()
                        for h in range(NH):
                            P.op("dve", lambda e, h=h, ybt=ybt: e.scalar_tensor_tensor(
                                out=ybt[:, h, :], in0=yb[:, h, :], scalar=bg[:, h:h + 1], in1=rb[:], op0=ALU.mult, op1=ALU.mult),
                                 reads=[Byb[h], Brb], pwrites=[ybb])
                        P.dma("pool", mixS[4:8, :, off + q0:off + q0 + 512].rearrange("c p t -> p c t"), ybt[:], ybd, reads=[ybb])

                    load_q(0)
                    load_q(1)
                    for i in range(min(LOOK, len(items))):
                        qk(items[i])
                    for i, it in enumerate(items):
                        if i + LOOK < len(items):
                            nx = items[i + LOOK]
                            if nx[1] == 0 and nx[2] == 0:
                                load_q(nx[0] + 1)
                            qk(nx)
                        pv()
                        for fn_ in deferred.pop(i, []):
                            fn_()
                        if it[2] == NKC - 1:
                            deferred.setdefault(i + DEFER, []).append(lambda qi=it[0], h=it[1]: tail(qi, h))
                    for k_ in sorted(deferred):
                        for fn_ in deferred[k_]:
                            fn_()
                P.emit_phase()
                if DEBUG_STOP == 2:
                    P.disabled = True

            with ExitStack() as ph:
                def sb(name, shape, dt):
                    return ph.enter_context(nc.sbuf_tensor(name, list(shape), dt))

                def pst(name, shape, dt):
                    return ph.enter_context(nc.psum_tensor(name, list(shape), dt))

                w_o_sb = sb("w_o_sb", [128, 8, D], BF16)
                w_dn_sb = sb("w_dn_sb", [128, NU, D], BF16)
                wup = [sb("wup%d" % i, [128, 8, 256], BF16) for i in range(4)]
                g1bc = sb("g1bc", [128, D], F32)
                g2bc = sb("g2bc", [128, D], F32)
                fgbc = sb("fgbc", [128, D], F32)
                mixT = sb("mixT", [128, 8, 512], BF16)
                xin = [sb("x3in%d" % i, [128, D], F32) for i in range(2)]
                x1 = sb("x1", [128, 4, D], F32)
                x1b = sb("x1b", [128, 4, D], F32)
                xn2 = sb("xn2", [128, 4, D], BF16)
                h2T = sb("h2T", [128, 8, 512], BF16)
                aT = sb("aT", [128, NU, 512], BF16)
                c1 = [sb("c1_%d" % i, [128, 512], F32) for i in range(3)]
                c2 = [sb("c2_%d" % i, [128, 512], F32) for i in range(4)]
                sg = [sb("sg%d" % i, [128, 512], F32) for i in range(2)]
                tmp = [sb("tmp%d" % i, [128, 512], F32) for i in range(2)]
                junk = sb("junk3", [128, D], BF16)
                ms = sb("ms3", [128, 8], F32)
                sd = sb("sd3", [128, 8], F32)
                rstd = sb("rstd3", [128, 8], F32)
                tp = [pst("tp3_%d" % i, [128, 2, 512], BF16) for i in range(2)]
                zp = [pst("zp3_%d" % i, [128, 512], F32) for i in range(6)]
                tp_ring = Ring(tp)
                zp_ring = Ring(zp)
                c1_ring = Ring(c1)
                c2_ring = Ring(c2)
                sg_ring = Ring(sg)
                tmp_ring = Ring(tmp)
                wup_ds = [P.dsem("wup%d" % i) for i in range(4)]
                wup_ring = Ring(list(zip(wup, wup_ds)))
                xin_ds = [P.dsem("x3in%d" % i) for i in range(2)]
                xin_ring = Ring(list(zip(xin, xin_ds)))
                ds_w = P.dsem("p3w")
                ds_g = P.dsem("p3g")
                ds_g2 = P.dsem("p3g2")
                ds_mix = P.dsem("mix")
                ds_out = [P.dsem("out%d" % g) for g in range(4)]
                ds_outb = [P.dsem("outb%d" % g) for g in range(4)]
                Bw = Buf()
                Bg = Buf()
                Bfg = Buf()
                Bmix = Buf()
                Bx1 = [Buf() for _ in range(4)]
                Bxn2 = Buf()
                Bh2 = Buf()
                BaT = Buf()
                Bst = [Buf() for _ in range(4)]
                Bst2 = [Buf() for _ in range(4)]

                class Stepper3:
                    def __init__(self, gen):
                        self.gen = gen
                        self.done = gen is None
                        self.val = None

                    def step(self, n=1):
                        for _ in range(n):
                            if self.done:
                                return
                            try:
                                next(self.gen)
                            except StopIteration as e_:
                                self.done = True
                                self.val = e_.value

                    def finish(self):
                        while not self.done:
                            self.step()
                        return self.val

                P.dma("pool", w_o_sb[:], w_o_d.rearrange("(kc p) n -> p kc n", p=128), ds_w, pwrites=[Bw])
                P.dma("pool", w_dn_sb[:, 0:11, :], w_dn_d[0:11 * 128, :].rearrange("(kc p) n -> p kc n", p=128), ds_w, pwrites=[Bw])
                P.dma("pool", w_dn_sb[:, 11:22, :], w_dn_d[11 * 128:22 * 128, :].rearrange("(kc p) n -> p kc n", p=128), ds_w,
                      pwrites=[Bw])
                P.dma("sp", fgbc[:], fg_d.to_broadcast([128, D]), ds_g, writes=[Bfg])
                P.op("pool", lambda e: e.memset(mixT[:], 0.0), writes=[Bmix])
                for i in range(2):
                    P.op("pool", lambda e, i=i: e.memset(xin[i][:], 0.0), writes=[xin_ring.bufs[i]])
                P.op("pool", lambda e: e.memset(aT[:], 0.0), writes=[BaT])

                Bg1 = Buf()
                Bg2 = Buf()
                x1s = [x1, x1b]
                Bx1s = [[Buf() for _ in range(4)] for _ in range(2)]
                x1_i = [0]

                def p3_A1(s, ti, t0, w, ntl):
                    off = offs[s]
                    C = w + 2
                    first = (ti == 0)
                    last = (ti == ntl - 1)
                    jlo = 1 if first else 0
                    jhi = C - 1 if last else C
                    G = -(-C // 128)
                    tokc = off + t0 - 1
                    x1t = x1s[x1_i[0] % 2]
                    Bx1 = Bx1s[x1_i[0] % 2]
                    x1_i[0] += 1
                    if first:
                        P.dma("sp", g1bc[:], modS[s:s + 1, 2 * D:3 * D].to_broadcast([128, D]), ds_g, writes=[Bg1])
                    P.dma("sp", mixT[:, :, jlo:jhi], mixS[:, :, tokc + jlo:tokc + jhi].rearrange("c p t -> p c t"), ds_mix,
                          writes=[Bmix])
                    yield
                    for g in range(G):
                        r = min(128, C - 128 * g)
                        lo = max(jlo, 128 * g) - 128 * g
                        hi = min(jhi, 128 * g + r) - 128 * g
                        (xt, xd), xb = xin_ring.next()
                        tok0 = tokc + 128 * g
                        P.dma("sp", xt[lo:hi, :], x_d[tok0 + lo:tok0 + hi, :], xd, writes=[xb])
                        yield
                        for hf in range(2):
                            zt, zb = zp_ring.next()
                            P.mm(zt[0:r, :], [(mixT[:, c, 128 * g:128 * g + r], w_o_sb[:, c, hf * 512:(hf + 1) * 512])
                                              for c in range(8)], reads=[Bmix, Bw], writes=[zb])
                            yield
                            tt, tb = tmp_ring.next()
                            P.op("dve", lambda e, zt=zt, tt=tt, r=r, hf=hf: e.tensor_tensor(
                                out=tt[0:r, :], in0=zt[0:r, :], in1=g1bc[0:r, hf * 512:(hf + 1) * 512], op=ALU.mult),
                                 reads=[zb, Bg1], writes=[tb])
                            yield
                            P.op("dve", lambda e, tt=tt, xt=xt, r=r, hf=hf, g=g: e.tensor_tensor(
                                out=x1t[0:r, g, hf * 512:(hf + 1) * 512], in0=tt[0:r, :], in1=xt[0:r, hf * 512:(hf + 1) * 512],
                                op=ALU.add), reads=[tb, xb], pwrites=[Bx1[g]])
                            yield
                        P.op("act", lambda e, g=g: e.activation(out=junk[:], in_=x1t[:, g, :], func=AF.Square, scale=1.0 / 32.0,
                                                                accum_out=ms[:, g:g + 1]), reads=[Bx1[g]], writes=[Bst[g]])
                        yield
                        P.op("act", lambda e, g=g: e.activation(out=sd[:, g:g + 1], in_=ms[:, g:g + 1], func=AF.Sqrt,
                                                                bias=epsT[:], scale=1.0), reads=[Bst[g]], writes=[Bst[g]])
                        yield
                        P.op("dve", lambda e, g=g: e.reciprocal(out=rstd[:, g:g + 1], in_=sd[:, g:g + 1]),
                             reads=[Bst[g]], writes=[Bst[g]])
                        yield
                        P.op("act", lambda e, g=g: e.activation(out=xn2[:, g, :], in_=x1t[:, g, :], func=AF.Identity,
                                                                scale=rstd[:, g:g + 1]), reads=[Bx1[g], Bst[g]], pwrites=[Bxn2])
                        yield
                    return (s, off, t0, w, C, first, last, jlo, jhi, G, tokc, x1t, Bx1)

                def p3_A2(cx):
                    (s, off, t0, w, C, first, last, jlo, jhi, G, tokc, x1t, Bx1) = cx
                    for cp in range(4):
                        tpt, tpb = tp_ring.next()
                        for c2 in range(2):
                            c = 2 * cp + c2
                            for g in range(G):
                                r = min(128, C - 128 * g)
                                P.op("pe", lambda e, tpt=tpt, g=g, r=r, c=c, c2=c2: e.transpose(
                                    tpt[:, c2, 128 * g:128 * g + r], xn2[0:r, g, c * 128:(c + 1) * 128], ident[0:r, 0:r]),
                                     reads=[Bxn2], pwrites=[tpb])
                            yield
                        for c2 in range(2):
                            c = 2 * cp + c2
                            P.op("act", lambda e, tpt=tpt, c=c, c2=c2, s=s, jlo=jlo, jhi=jhi: e.activation(
                                out=h2T[:, c, jlo:jhi], in_=tpt[:, c2, jlo:jhi], func=AF.Identity, bias=sh2(s, c),
                                scale=gm2[:, s, c:c + 1]), reads=[tpb], pwrites=[Bh2])
                            yield
                    if first:
                        P.op("pool", lambda e: e.memset(h2T[:, :, 0:1], 0.0), pwrites=[Bh2])
                    if last:
                        P.op("pool", lambda e, C=C: e.memset(h2T[:, :, C - 1:C], 0.0), pwrites=[Bh2])
                    return None

                def p3_B(cx, stp):
                    (s, off, t0, w, C, first, last, jlo, jhi, G, tokc, x1t, Bx1) = cx
                    if first:
                        P.dma("sp", g2bc[:], modS[s:s + 1, 5 * D:6 * D].to_broadcast([128, D]), ds_g2, writes=[Bg2])
                    for u in range(NU):
                        (wt, wd), wb = wup_ring.next()
                        P.dma("sp", wt[:], wupS[u], wd, writes=[wb])
                        zg, zgb = zp_ring.next()
                        P.mm(zg[:, 0:C], [(wt[:, kc, 0:128], h2T[:, kc, 0:C]) for kc in range(8)], reads=[wb, Bh2], writes=[zgb])
                        zv, zvb = zp_ring.next()
                        P.mm(zv[:, 0:C], [(wt[:, kc, 128:256], h2T[:, kc, 0:C]) for kc in range(8)], reads=[wb, Bh2], writes=[zvb])
                        outs = []
                        for (zt, zb, ch) in ((zg, zgb, u), (zv, zvb, NU + u)):
                            c1t, c1b = c1_ring.next()
                            P.op("act", lambda e, zt=zt, c1t=c1t, ch=ch, w=w: e.activation(
                                out=c1t[:, 0:w], in_=zt[:, 1:w + 1], func=AF.Identity, scale=convF[:, 1, ch:ch + 1]),
                                 reads=[zb], writes=[c1b])
                            c2t, c2b = c2_ring.next()
                            P.op("dve", lambda e, zt=zt, c1t=c1t, c2t=c2t, ch=ch, w=w: e.scalar_tensor_tensor(
                                out=c2t[:, 0:w], in0=zt[:, 0:w], scalar=convF[:, 0, ch:ch + 1], in1=c1t[:, 0:w],
                                op0=ALU.mult, op1=ALU.add), reads=[zb, c1b], writes=[c2b])
                            P.op("dve", lambda e, zt=zt, c2t=c2t, ch=ch, w=w: e.scalar_tensor_tensor(
                                out=c2t[:, 0:w], in0=zt[:, 2:w + 2], scalar=convF[:, 2, ch:ch + 1], in1=c2t[:, 0:w],
                                op0=ALU.mult, op1=ALU.add), reads=[zb, c2b], writes=[c2b])
                            outs.append((c2t, c2b))
                        sgt, sgb = sg_ring.next()
                        P.op("act", lambda e, sgt=sgt, c2t=outs[0][0], w=w: e.activation(out=sgt[:, 0:w], in_=c2t[:, 0:w],
                                                                                          func=AF.Silu),
                             reads=[outs[0][1]], writes=[sgb])
                        P.op("pool", lambda e, sgt=sgt, c2t=outs[1][0], u=u, w=w: e.tensor_tensor(
                            out=aT[:, u, 1:w + 1], in0=sgt[:, 0:w], in1=c2t[:, 0:w], op=ALU.mult),
                             reads=[sgb, outs[1][1]], pwrites=[BaT])
                        if u >= 2:
                            stp.step(3)

                def p3_C(cx, stp):
                    (s, off, t0, w, C, first, last, jlo, jhi, G, tokc, x1t, Bx1) = cx
                    for g in range(G):
                        r = min(128, C - 128 * g)
                        lo = max(1, 128 * g) - 128 * g
                        hi = min(w + 1, 128 * g + r) - 128 * g
                        for hf in range(2):
                            zt, zb = zp_ring.next()
                            P.mm(zt[0:r, :], [(aT[:, u, 128 * g:128 * g + r], w_dn_sb[:, u, hf * 512:(hf + 1) * 512])
                                              for u in range(NU)], reads=[BaT, Bw], writes=[zb])
                            stp.step(6)
                            tt, tb = tmp_ring.next()
                            P.op("dve", lambda e, zt=zt, tt=tt, r=r, hf=hf: e.tensor_tensor(
                                out=tt[0:r, :], in0=zt[0:r, :], in1=g2bc[0:r, hf * 512:(hf + 1) * 512], op=ALU.mult),
                                 reads=[zb, Bg2], writes=[tb])
                            P.op("dve", lambda e, tt=tt, r=r, hf=hf, g=g: e.tensor_tensor(
                                out=x1t[0:r, g, hf * 512:(hf + 1) * 512], in0=tt[0:r, :], in1=x1t[0:r, g, hf * 512:(hf + 1) * 512],
                                op=ALU.add), reads=[tb, Bx1[g]], writes=[Bx1[g]])
                        P.op("act", lambda e, g=g: e.activation(out=junk[:], in_=x1t[:, g, :], func=AF.Square, scale=1.0 / 32.0,
                                                                accum_out=ms[:, 4 + g:5 + g]), reads=[Bx1[g]], writes=[Bst2[g]])
                        P.op("act", lambda e, g=g: e.activation(out=sd[:, 4 + g:5 + g], in_=ms[:, 4 + g:5 + g], func=AF.Sqrt,
                                                                bias=epsT[:], scale=1.0), reads=[Bst2[g]], writes=[Bst2[g]])
                        P.op("dve", lambda e, g=g: e.reciprocal(out=rstd[:, 4 + g:5 + g], in_=sd[:, 4 + g:5 + g]),
                             reads=[Bst2[g]], writes=[Bst2[g]])
                        P.op("dve", lambda e, g=g: e.scalar_tensor_tensor(
                            out=x1t[:, g, :], in0=x1t[:, g, :], scalar=rstd[:, 4 + g:5 + g], in1=fgbc[:], op0=ALU.mult, op1=ALU.mult),
                             reads=[Bst2[g], Bfg, Bx1[g]], writes=[Bx1[g]])
                        if hi > lo:
                            tok0 = tokc + 128 * g
                            P.dma("pool", y_d[tok0 + lo:tok0 + hi, :], x1t[lo:hi, g, :],
                                  (ds_out if x1t is x1 else ds_outb)[g], reads=[Bx1[g]])

                tiles3 = []
                for s in range(NS):
                    tl = seq_tiles(seqs[s])
                    for ti, (t0, w) in enumerate(tl):
                        tiles3.append((s, ti, t0, w, len(tl)))
                cx = Stepper3(p3_A1(*tiles3[0])).finish()
                Stepper3(p3_A2(cx)).finish()
                for n_ in range(len(tiles3)):
                    stA1 = Stepper3(p3_A1(*tiles3[n_ + 1]) if n_ + 1 < len(tiles3) else None)
                    p3_B(cx, stA1)
                    ncx = stA1.finish()
                    stA2 = Stepper3(p3_A2(ncx) if ncx is not None else None)
                    p3_C(cx, stA2)
                    stA2.finish()
                    cx = ncx

                P.emit_phase()
                if DEBUG_STOP == 3:
                    P.disabled = True
        except _Stop:
            pass
        P.final_wait()
    return nc


def rope_tables_np(smax):
    inv = (1.0 / (np.float32(THETA) ** (np.arange(0, DR, 2, dtype=np.float32) / np.float32(DR)))).astype(np.float32)
    ang = (np.arange(-1, smax + 1, dtype=np.float32)[None, :] * inv[:, None]).astype(np.float32)
    cos = np.cos(ang).astype(np.float32)
    sin = np.sin(ang).astype(np.float32)
    return np.concatenate([cos, cos], 0), np.concatenate([sin, sin], 0)


_CACHE = {}


def run_cores(seqs, xs, cs, weights):
    key = tuple(seqs)
    if key not in _CACHE:
        _CACHE[key] = build_program(list(seqs))
    nc = _CACHE[key]
    cos2, sin2 = rope_tables_np(max(seqs))
    shared = dict(weights)
    shared["cos2"] = cos2
    shared["sin2"] = sin2
    in_maps = []
    for i in range(len(xs)):
        m = dict(shared)
        m["x"] = np.ascontiguousarray(xs[i], dtype=np.float32)
        m["c"] = np.ascontiguousarray(cs[i], dtype=np.float32)
        in_maps.append(m)
    res = run_bass_kernel_spmd(nc, in_maps, core_ids=list(range(len(xs))))
    return [r["y"] for r in res.results]


def prep_weights(w_ada, b_ada, norm1_g, w_in, conv_a_w, q_norm_g, w_uq, kv_norm_g, w_ukv, out_norm_a_g, out_norm_b_g,
                 w_o, norm2_g, w_up, ffn_conv_w, w_down, final_g):
    f = lambda a: np.ascontiguousarray(np.asarray(a, dtype=np.float32))
    return dict(
        w_ada=f(w_ada[0]), b_ada=f(b_ada[0]).reshape(1, -1), norm1_g=f(norm1_g[0]).reshape(1, -1), w_in=f(w_in[0]),
        conv_a_w=f(conv_a_w[0]), q_norm_g=f(q_norm_g[0]).reshape(1, -1), w_uq=f(w_uq[0]),
        kv_norm_g=f(kv_norm_g[0]).reshape(1, -1), w_ukv=f(w_ukv[0]), out_norm_a_g=f(out_norm_a_g[0]).reshape(1, -1),
        out_norm_b_g=f(out_norm_b_g[0]).reshape(1, -1), w_o=f(w_o[0]), norm2_g=f(norm2_g[0]).reshape(1, -1),
        w_up=f(w_up[0]), ffn_conv_w=f(ffn_conv_w[0]), w_down=f(w_down[0]), final_g=f(final_g).reshape(1, -1))


def kernel(x_prompt, x_sample, c_prompt, c_sample, w_ada, b_ada, norm1_g, w_in, conv_a_w, q_norm_g, w_uq, kv_norm_g,
           w_ukv, out_norm_a_g, out_norm_b_g, w_o, norm2_g, w_up, ffn_conv_w, w_down, final_g):
    x_prompt = np.asarray(x_prompt, dtype=np.float32)
    x_sample = np.asarray(x_sample, dtype=np.float32)
    c_prompt = np.asarray(c_prompt, dtype=np.float32)
    c_sample = np.asarray(c_sample, dtype=np.float32)
    weights = prep_weights(w_ada, b_ada, norm1_g, w_in, conv_a_w, q_norm_g, w_uq, kv_norm_g, w_ukv, out_norm_a_g,
                           out_norm_b_g, w_o, norm2_g, w_up, ffn_conv_w, w_down, final_g)
    SS = x_sample.shape[1]
    SP = x_prompt.shape[1]
    seqs = (SS, SP, SP)
    xs, cs = [], []
    for i in range(N_CORES):
        xs.append(np.concatenate([x_sample[i], x_prompt[2 * i], x_prompt[2 * i + 1]], axis=0))
        cs.append(np.stack([c_sample[i], c_prompt[2 * i], c_prompt[2 * i + 1]], axis=0))
    ys = run_cores(seqs, xs, cs, weights)
    y_prompt = np.empty_like(x_prompt)
    y_sample = np.empty_like(x_sample)
    for i in range(N_CORES):
        y = ys[i]
        y_sample[i] = y[0:SS]
        y_prompt[2 * i] = y[SS:SS + SP]
        y_prompt[2 * i + 1] = y[SS + SP:SS + 2 * SP]
    return (y_prompt, y_sample)
```

```python
import math
from contextlib import ExitStack

import numpy as np
import concourse.bass as bass
import concourse.mybir as mybir
from concourse.bass_utils import run_bass_kernel_spmd

F32 = mybir.dt.float32
BF16 = mybir.dt.bfloat16
AF = mybir.ActivationFunctionType
ALU = mybir.AluOpType

D = 1024
A_W = 512
NH = 4
DN = 128
DR = 64
DV = 128
QL = 384
KVL = 256
INC = 2240
DFF = 2816
NU = DFF // 128
ATTN_SCALE = 1.0 / math.sqrt(DN + DR)
EPS = 1e-6
THETA = 10000.0
N_CORES = 8
SAME_ENGINE_SYNC = True

ENGS = ["pe", "act", "dve", "pool", "sp"]


class Buf:
    __slots__ = ("w", "r", "war")

    def __init__(self):
        self.w = {}
        self.r = {}
        self.war = {}


def _mg(d, k, v):
    if d.get(k, -1) < v:
        d[k] = v


class DSem:
    __slots__ = ("h", "val", "val0")

    def __init__(self, h):
        self.h = h
        self.val = 0
        self.val0 = 0


class Op:
    __slots__ = ("eng", "idx", "fn", "deps", "sig", "dsem", "dval")


class Prog:
    def __init__(self, nc, es):
        self.nc = nc
        self.es = es
        self.eh = dict(pe=nc.tensor, act=nc.scalar, dve=nc.vector, pool=nc.gpsimd, sp=nc.sync)
        self.psem = {e: es.enter_context(nc.semaphore("pg_" + e)) for e in ENGS}
        self.cnt = {e: 0 for e in ENGS}
        self.waited = {e: {} for e in ENGS}
        self.ops = {e: [] for e in ENGS}
        self.dsems = []
        self.first_phase = True
        self.disabled = False

    def dsem(self, name):
        d = DSem(self.es.enter_context(self.nc.semaphore("d_" + name)))
        self.dsems.append(d)
        return d

    def op(self, eng, fn, reads=(), writes=(), pwrites=(), dsem=None):
        if self.disabled:
            return None
        self.nops = getattr(self, 'nops', 0) + 1
        if self.nops > DEBUG_MAXOPS:
            return None
        deps = {}
        for b in reads:
            for k, v in b.w.items():
                _mg(deps, k, v)
        for b in writes:
            war = {}
            for k, v in b.r.items():
                _mg(war, k, v)
            for k, v in b.w.items():
                _mg(war, k, v)
            b.war = war
            for k, v in war.items():
                _mg(deps, k, v)
        newver = []
        for b in pwrites:
            if b.r or not b.w:
                war = {}
                for k, v in b.r.items():
                    _mg(war, k, v)
                for k, v in b.w.items():
                    _mg(war, k, v)
                b.war = war
                newver.append(b)
            for k, v in b.war.items():
                _mg(deps, k, v)
        o = Op()
        o.eng = eng
        o.idx = len(self.ops[eng])
        o.fn = fn
        o.deps = deps
        o.sig = False
        o.dsem = dsem
        if dsem is not None:
            dsem.val += 16
            o.dval = dsem.val
            key, val = dsem, dsem.val
        else:
            o.dval = 0
            key, val = eng, o.idx
        self.ops[eng].append(o)
        for k, v in deps.items():
            if isinstance(k, str):
                self.ops[k][v].sig = True
        for b in reads:
            _mg(b.r, key, val)
        for b in writes:
            b.w = {key: val}
            b.r = {}
        for b in newver:
            b.w = {}
            b.r = {}
        for b in pwrites:
            _mg(b.w, key, val)
        return o

    def dma(self, q, out, in_, dsem, reads=(), writes=(), pwrites=(), nonc=False):
        nc = self.nc

        def fn(e):
            if nonc:
                with nc.allow_non_contiguous_dma("small strided setup load"):
                    return e.dma_start(out=out, in_=in_)
            return e.dma_start(out=out, in_=in_)

        return self.op(q, fn, reads, writes, pwrites, dsem=dsem)

    def mm(self, out, pairs, reads=(), writes=(), pwrites=(), first=True, last=True):
        def fn(e):
            n = len(pairs)
            ins = None
            for i, (l, r) in enumerate(pairs):
                ins = e.matmul(out, l, r, start=(first and i == 0), stop=(last and i == n - 1))
            return ins

        return self.op("pe", fn, reads, writes, pwrites)

    def emit_phase(self):
        if self.disabled:
            return
        nc = self.nc
        for e in ENGS:
            for o in reversed(self.ops[e]):
                if o.dsem is None:
                    o.sig = True
                    break
        signo = {}
        for e in ENGS:
            c = self.cnt[e]
            lst = []
            for o in self.ops[e]:
                if o.sig and o.dsem is None:
                    c += 1
                lst.append(c)
            signo[e] = lst
        fence = None
        if not self.first_phase:
            fence = ([(self.psem[k], self.cnt[k], k) for k in ENGS if self.cnt[k] > 0]
                     + [(d.h, d.val0, d) for d in self.dsems if getattr(d, "val0", 0) > 0])
        with nc.Block() as block:
            reg = dict(pe=block.tensor, act=block.scalar, dve=block.vector, pool=block.gpsimd, sp=block.sync)
            for e in ENGS:
                ops = self.ops[e]

                def body(engh, e=e, ops=ops):
                    wd = self.waited[e]
                    if fence is not None:
                        for h, v, k in fence:
                            if k == e:
                                continue
                            if wd.get(k, 0) >= v:
                                continue
                            engh.wait_ge(h, v)
                            wd[k] = v
                    for o in ops:
                        for k, v in o.deps.items():
                            if isinstance(k, str):
                                if k == e and (e == "pe" or not SAME_ENGINE_SYNC):
                                    continue
                                need = signo[k][v]
                                h = self.psem[k]
                            else:
                                need = v
                                h = k.h
                            if wd.get(k, 0) >= need:
                                continue
                            engh.wait_ge(h, need)
                            wd[k] = need
                        ins = o.fn(engh)
                        if o.dsem is not None:
                            ins.then_inc(o.dsem.h, 16)
                        elif o.sig:
                            ins.then_inc(self.psem[e], 1)

                reg[e](body)
        for e in ENGS:
            if self.ops[e]:
                self.cnt[e] = signo[e][-1]
            self.ops[e] = []
        for d in self.dsems:
            d.val0 = d.val
        self.first_phase = False

    def final_wait(self):
        nc = self.nc
        with nc.Block() as block:
            def body(engh):
                for k in ENGS:
                    if k != "sp" and self.cnt[k] > 0:
                        engh.wait_ge(self.psem[k], self.cnt[k])
                for d in self.dsems:
                    if d.val > 0:
                        engh.wait_ge(d.h, d.val)
            block.sync(body)


class Ring:
    def __init__(self, items):
        self.items = items
        self.bufs = [Buf() for _ in items]
        self.i = 0

    def next(self):
        k = self.i % len(self.items)
        self.i += 1
        return self.items[k], self.bufs[k]


def seq_tiles(S):
    n = -(-S // 510)
    W = -(-S // n)
    out = []
    t = 0
    while t < S:
        w = min(W, S - t)
        out.append((t, w))
        t += w
    return out


DEBUG_STOP = 99
DEBUG_MAXOPS = 10 ** 9


class _Stop(Exception):
    pass


def build_program(seqs):
    NT = sum(seqs)
    offs = [sum(seqs[:i]) for i in range(len(seqs))]
    NS = len(seqs)
    SMAX = max(seqs)
    nc = bass.Bass("TRN2", target_bir_lowering=False)

    def din(name, shape, dt=F32):
        return nc.dram_tensor(name, list(shape), dt, kind="ExternalInput").ap()

    def dscr(name, shape, dt=BF16):
        return nc.dram_tensor(name, list(shape), dt, kind="Internal").ap()

    x_d = din("x", [NT, D])
    c_d = din("c", [NS, D])
    w_ada_d = din("w_ada", [D, 6 * D])
    b_ada_d = din("b_ada", [1, 6 * D])
    n1g_d = din("norm1_g", [1, D])
    w_in_d = din("w_in", [D, INC])
    conva_d = din("conv_a_w", [3, A_W])
    qng_d = din("q_norm_g", [1, QL])
    w_uq_d = din("w_uq", [QL, NH * (DN + DR)])
    kvng_d = din("kv_norm_g", [1, KVL])
    w_ukv_d = din("w_ukv", [KVL, NH * (DN + DV)])
    ang_d = din("out_norm_a_g", [1, A_W])
    bng_d = din("out_norm_b_g", [1, A_W])
    w_o_d = din("w_o", [D, D])
    n2g_d = din("norm2_g", [1, D])
    w_up_d = din("w_up", [D, 2 * DFF])
    convf_d = din("ffn_conv_w", [3, 2 * DFF])
    w_dn_d = din("w_down", [DFF, D])
    fg_d = din("final_g", [1, D])
    cos_d = din("cos2", [64, SMAX + 2])
    sin_d = din("sin2", [64, SMAX + 2])
    y_d = nc.dram_tensor("y", [NT, D], F32, kind="ExternalOutput").ap()

    qS = dscr("qS", [NH, 128, NT])
    qrS = dscr("qrS", [NH, 64, NT])
    kS = dscr("kS", [NH, 128, NT])
    krS = dscr("krS", [64, NT])
    vS = dscr("vS", [NT, NH * DV])
    mixS = dscr("mixS", [8, 128, NT])
    modS = dscr("modS", [NS, 6 * D], F32)
    wupS = dscr("wupS", [NU, 128, 8, 256])

    with ExitStack() as es:
        P = Prog(nc, es)

        def sbg(name, shape, dt):
            return es.enter_context(nc.sbuf_tensor(name, list(shape), dt))

        ident = sbg("ident", [128, 128], BF16)
        ones = sbg("ones", [128, 128], BF16)
        epsT = sbg("epsT", [128, 1], F32)
        onesf = sbg("onesf", [128, 128], F32)
        n1g = sbg("n1g", [128, 8], F32)
        n2g = sbg("n2g", [128, 8], F32)
        qg = sbg("qg", [128, 3], F32)
        kvg = sbg("kvg", [128, 2], F32)
        ag = sbg("ag", [128, 4], F32)
        bg = sbg("bg", [128, 4], F32)
        convA = sbg("convA", [128, 3, 4], F32)
        convF = sbg("convF", [128, 3, 44], F32)
        modT = sbg("modT", [128, NS, 48], F32)
        gm1 = sbg("gm1", [128, NS, 8], F32)
        gm2 = sbg("gm2", [128, NS, 8], F32)

        def sh1(s, c):
            return modT[:, s, 0 * 8 + c:0 * 8 + c + 1]

        def sh2(s, c):
            return modT[:, s, 3 * 8 + c:3 * 8 + c + 1]

        try:
            with ExitStack() as ph:
                def sb(name, shape, dt):
                    return ph.enter_context(nc.sbuf_tensor(name, list(shape), dt))

                identf = sb("identf", [128, 128], F32)
                cT = sb("cT", [128, 8, NS], F32)
                cA = sb("cA", [128, 8, NS], F32)
                modsb = sb("modsb", [NS, 6 * D], F32)
                bsb = sb("bsb", [NS, 6 * D], F32)
                wa = [sb("wa%d" % i, [128, 8, 512], F32) for i in range(3)]
                pm = [ph.enter_context(nc.psum_tensor("p0m%d" % i, [128, 512], F32)) for i in range(2)]
                B = {k: Buf() for k in ["identf", "ident", "ones", "eps", "cT", "cA", "modsb", "bsb", "small", "modT", "gm",
                                         "modS"]}
                ds_small = P.dsem("small")
                ds_c = P.dsem("c")
                ds_b = P.dsem("b")
                ds_mod = P.dsem("mod")
                ds_modT = P.dsem("modT")
                ds_wa = [P.dsem("wa%d" % i) for i in range(3)]
                wa_ring = Ring(list(zip(wa, ds_wa)))
                pm_ring = Ring(pm)


                P.op("pool", lambda e: e.memset(identf[:], 0.0), writes=[B["identf"]])
                P.op("pool", lambda e: e.affine_select(out=identf[:], in_=identf[:], pattern=[[-1, 128]],
                                                       compare_op=ALU.not_equal, fill=1.0, base=0,
                                                       channel_multiplier=1), reads=[B["identf"]], writes=[B["identf"]])
                P.op("dve", lambda e: e.tensor_copy(out=ident[:], in_=identf[:]), reads=[B["identf"]], writes=[B["ident"]])
                P.op("dve", lambda e: e.memset(ones[:], 1.0), writes=[B["ones"]])
                P.op("dve", lambda e: e.memset(onesf[:], 1.0), writes=[B["ones"]])
                P.op("dve", lambda e: e.memset(epsT[:], EPS), writes=[B["eps"]])

                def fm(dst, src, n):
                    P.dma("sp", dst, src.rearrange("o (c p) -> p (o c)", p=128), ds_small, pwrites=[B["small"]], nonc=True)

                fm(n1g[:], n1g_d, 8)
                fm(n2g[:], n2g_d, 8)
                fm(qg[:], qng_d, 3)
                fm(kvg[:], kvng_d, 2)
                fm(ag[:], ang_d, 4)
                fm(bg[:], bng_d, 4)
                for j in range(3):
                    P.dma("sp", convA[:, j, :], conva_d[j:j + 1, :].rearrange("o (c p) -> p (o c)", p=128), ds_small,
                          pwrites=[B["small"]], nonc=True)
                    for q4 in range(4):
                        P.dma("sp", convF[:, j, q4 * 11:(q4 + 1) * 11],
                              convf_d[j:j + 1, q4 * 1408:(q4 + 1) * 1408].rearrange("o (c p) -> p (o c)", p=128), ds_small,
                              pwrites=[B["small"]], nonc=True)
                for s in range(NS):
                    P.dma("sp", cT[:, :, s], c_d[s:s + 1, :].rearrange("o (c p) -> p (o c)", p=128), ds_c, pwrites=[B["cT"]],
                          nonc=True)
                P.dma("sp", bsb[:], b_ada_d.to_broadcast([NS, 6 * D]), ds_b, writes=[B["bsb"]])
                P.op("act", lambda e: e.activation(out=cA[:], in_=cT[:], func=AF.Silu), reads=[B["cT"]], writes=[B["cA"]])
                wav = w_ada_d.rearrange("(kc p) n -> p kc n", p=128)
                for j in range(12):
                    (wt, dsw), wb = wa_ring.next()
                    P.dma("sp", wt[:], wav[:, :, j * 512:(j + 1) * 512], dsw, writes=[wb])
                    pmt, pb = pm_ring.next()
                    P.mm(pmt[0:NS, :], [(cA[:, kc, :], wt[:, kc, :]) for kc in range(8)], reads=[B["cA"], wb], writes=[pb])
                    P.op("dve", lambda e, pmt=pmt, j=j: e.tensor_tensor(out=modsb[:, j * 512:(j + 1) * 512], in0=pmt[0:NS, :],
                                                                         in1=bsb[:, j * 512:(j + 1) * 512], op=ALU.add),
                         reads=[pb, B["bsb"]], pwrites=[B["modsb"]])
                P.dma("pool", modS, modsb[:], ds_mod, reads=[B["modsb"]], writes=[B["modS"]])
                for s in range(NS):
                    for v6 in range(6):
                        P.dma("sp", modT[:, s, v6 * 8:(v6 + 1) * 8],
                              modS[s:s + 1, v6 * D:(v6 + 1) * D].rearrange("o (j p) -> p (o j)", p=128), ds_modT,
                              reads=[B["modS"]], pwrites=[B["modT"]], nonc=True)
                for s in range(NS):
                    P.op("dve", lambda e, s=s: e.scalar_tensor_tensor(out=gm1[:, s, :], in0=modT[:, s, 8:16], scalar=1.0,
                                                                      in1=n1g[:], op0=ALU.add, op1=ALU.mult),
                         reads=[B["modT"], B["small"]], pwrites=[B["gm"]])
                    P.op("dve", lambda e, s=s: e.scalar_tensor_tensor(out=gm2[:, s, :], in0=modT[:, s, 32:40], scalar=1.0,
                                                                      in1=n2g[:], op0=ALU.add, op1=ALU.mult),
                         reads=[B["modT"], B["small"]], pwrites=[B["gm"]])
                P.op("dve", lambda e: e.tensor_scalar(out=qg[:], in0=qg[:], scalar1=ATTN_SCALE, scalar2=None, op0=ALU.mult),
                     reads=[B["small"]], writes=[B["small"]])
                P.emit_phase()
                if DEBUG_STOP == 0:
                    P.disabled = True

            with ExitStack() as ph:
                def sb(name, shape, dt):
                    return ph.enter_context(nc.sbuf_tensor(name, list(shape), dt))

                def pst(name, shape, dt):
                    return ph.enter_context(nc.psum_tensor(name, list(shape), dt))

                w_in_sb = sb("w_in_sb", [128, 8, INC], BF16)
                w_krr = sb("w_krr", [128, 8, 64], BF16)
                w_uq_sb = sb("w_uq_sb", [128, 3, 768], BF16)
                w_uqr = sb("w_uqr", [128, 3, 256], BF16)
                w_uk_sb = sb("w_uk_sb", [128, 2, 512], BF16)
                w_uv_sb = sb("w_uv_sb", [128, 2, 512], BF16)
                xin = [sb("xin%d" % i, [128, D], F32) for i in range(3)]
                junk = sb("junk", [128, D], BF16)
                ms = sb("ms", [128, 8], F32)
                sd = sb("sd", [128, 8], F32)
                rstd = sb("rstd", [128, 8], F32)
                xn = [sb("xn%d" % i, [128, 4, D], BF16) for i in range(2)]
                hT = [sb("hT%d" % i, [128, 8, 512], BF16) for i in range(2)]
                hasb = sb("hasb", [128, 4, 512], F32)
                psb = sb("psb", [128, 4, 512], F32)
                basb = sb("basb", [128, 4, 512], F32)
                cqsb = sb("cqsb", [128, 3, 512], F32)
                ckvsb = sb("ckvsb", [128, 2, 512], F32)
                sq = [sb("sq%d" % i, [128, 512], BF16) for i in range(4)]
                rs = [sb("rs%d" % i, [128, 512], F32) for i in range(2)]
                rr = [sb("rr%d" % i, [128, 512], F32) for i in range(3)]
                cqn = sb("cqn", [128, 3, 512], BF16)
                ckvn = sb("ckvn", [128, 2, 512], BF16)
                yan = sb("yan", [128, 4, 512], BF16)
                qTo = sb("qTo", [128, 4, 512], BF16)
                qro = sb("qro", [64, 4, 512], BF16)
                t1 = [sb("t1_%d" % i, [64, 512], F32) for i in range(2)]
                t2 = [sb("t2_%d" % i, [64, 512], F32) for i in range(2)]
                kTo = sb("kTo", [128, 4, 512], BF16)
                kro = sb("kro", [64, 512], BF16)
                vo = sb("vo", [128, 4, 512], BF16)
                cs = [sb("cs%d" % i, [64, 512], F32) for i in range(2)]
                sn = [sb("sn%d" % i, [64, 512], F32) for i in range(2)]
                tp = [pst("tp%d" % i, [128, 2, 512], BF16) for i in range(2)]
                zp = [pst("zp%d" % i, [128, 512], F32) for i in range(6)]

                tp_ring = Ring(tp)
                zp_ring = Ring(zp)
                sq_ring = Ring(sq)
                rs_ring = Ring(rs)
                rr_ring = Ring(rr)
                t1_ring = Ring(t1)
                t2_ring = Ring(t2)
                xin_ds = [P.dsem("xin%d" % i) for i in range(3)]
                xin_ring = Ring(list(zip(xin, xin_ds)))
                cs_ds = [P.dsem("cs%d" % i) for i in range(2)]
                cs_ring = Ring(list(zip(cs, sn, cs_ds)))
                xn_ring = Ring(xn)
                hT_ring = Ring(hT)
                ds_w = P.dsem("p1w")
                ds_wup = P.dsem("wup")
                ds_st = {k: P.dsem("st_" + k) for k in ["ya", "q", "qr", "k", "kr", "v"]}
                Bw = Buf()
                Bst1 = [Buf() for _ in range(4)]
                Bha = [Buf() for _ in range(4)]
                Bp = [Buf() for _ in range(4)]
                Bba = [Buf() for _ in range(4)]
                Bx = {k: Buf() for k in ["cqsb", "ckvsb", "cqn", "ckvn", "yan", "qTo", "qro", "kTo", "kro", "vo"]}

                P.dma("pool", w_in_sb[:], w_in_d.rearrange("(kc p) n -> p kc n", p=128), ds_w, pwrites=[Bw])
                P.dma("pool", w_uq_sb[:], w_uq_d.rearrange("(kc p) n -> p kc n", p=128), ds_w, pwrites=[Bw])
                wkv = w_ukv_d.rearrange("(kc p) (h t d) -> p kc h t d", p=128, h=NH, t=2)
                for kc in range(2):
                    P.dma("pool", w_uk_sb[:, kc, :].rearrange("p (h d) -> p h d", h=NH), wkv[:, kc, :, 0, :], ds_w, pwrites=[Bw])
                    P.dma("pool", w_uv_sb[:, kc, :].rearrange("p (h d) -> p h d", h=NH), wkv[:, kc, :, 1, :], ds_w, pwrites=[Bw])
                wupv = w_up_d.rearrange("(kc p) n -> p kc n", p=128)
                for u in range(NU):
                    P.dma("pool", wupS[u, :, :, 0:128], wupv[:, :, u * 128:(u + 1) * 128], ds_wup)
                    P.dma("pool", wupS[u, :, :, 128:256], wupv[:, :, DFF + u * 128:DFF + (u + 1) * 128], ds_wup)
                Bw2 = Buf()
                P.op("dve", lambda e: e.tensor_scalar(out=w_krr[:, :, 0:32], in0=w_in_sb[:, :, 2208:2240], scalar1=-1.0,
                                                      scalar2=None, op0=ALU.mult), reads=[Bw], pwrites=[Bw2])
                P.op("dve", lambda e: e.tensor_copy(out=w_krr[:, :, 32:64], in_=w_in_sb[:, :, 2176:2208]), reads=[Bw],
                     pwrites=[Bw2])
                for h in range(NH):
                    b0 = h * 192 + 128
                    P.op("dve", lambda e, h=h, b0=b0: e.tensor_scalar(out=w_uqr[:, :, h * 64:h * 64 + 32],
                                                                      in0=w_uq_sb[:, :, b0 + 32:b0 + 64], scalar1=-1.0,
                                                                      scalar2=None, op0=ALU.mult), reads=[Bw], pwrites=[Bw2])
                    P.op("dve", lambda e, h=h, b0=b0: e.tensor_copy(out=w_uqr[:, :, h * 64 + 32:h * 64 + 64],
                                                                    in_=w_uq_sb[:, :, b0:b0 + 32]), reads=[Bw], pwrites=[Bw2])
                for i in range(3):
                    P.op("pool", lambda e, i=i: e.memset(xin[i][:], 0.0), writes=[xin_ring.bufs[i]])

                WR = [Bw, Bw2]

                def p1_front_a(s, ti, t0, w, ntl):
                    S = seqs[s]
                    off = offs[s]
                    C = w + 2
                    first = (ti == 0)
                    last = (ti == ntl - 1)
                    jlo = 1 if first else 0
                    jhi = C - 1 if last else C
                    G = -(-C // 128)
                    xnt, xnb = xn_ring.next()
                    hTt, hTb = hT_ring.next()
                    (cst, snt, csd), csb = cs_ring.next()
                    P.dma("sp", cst[:, 0:C], cos_d[:, t0:t0 + C], csd, writes=[csb])
                    P.dma("sp", snt[:, 0:C], sin_d[:, t0:t0 + C], csd, pwrites=[csb])
                    for g in range(G):
                        r = min(128, C - 128 * g)
                        lo = max(jlo, 128 * g) - 128 * g
                        hi = min(jhi, 128 * g + r) - 128 * g
                        (xt, xd), xb = xin_ring.next()
                        tok0 = off + t0 - 1 + 128 * g
                        P.dma("sp", xt[lo:hi, :], x_d[tok0 + lo:tok0 + hi, :], xd, writes=[xb])
                        yield
                        P.op("act", lambda e, xt=xt, g=g: e.activation(out=junk[:], in_=xt[:], func=AF.Square,
                                                                       scale=1.0 / 32.0, accum_out=ms[:, g:g + 1]),
                             reads=[xb], writes=[Bst1[g]])
                        yield
                        P.op("act", lambda e, g=g: e.activation(out=sd[:, g:g + 1], in_=ms[:, g:g + 1], func=AF.Ln,
                                                                bias=epsT[:], scale=1.0), reads=[Bst1[g]], writes=[Bst1[g]])
                        yield
                        P.op("act", lambda e, g=g: e.activation(out=rstd[:, g:g + 1], in_=sd[:, g:g + 1], func=AF.Exp,
                                                                scale=-0.5), reads=[Bst1[g]], writes=[Bst1[g]])
                        yield
                        P.op("dve", lambda e, xt=xt, g=g, xnt=xnt: e.tensor_scalar(out=xnt[:, g, :], in0=xt[:],
                                                                                   scalar1=rstd[:, g:g + 1], scalar2=None,
                                                                                   op0=ALU.mult),
                             reads=[xb, Bst1[g]], pwrites=[xnb])
                        yield
                    return (s, off, t0, w, C, first, last, jlo, jhi, G, xnt, xnb, hTt, hTb, cst, snt, csb)

                def p1_front_b(cx):
                    (s, off, t0, w, C, first, last, jlo, jhi, G, xnt, xnb, hTt, hTb, cst, snt, csb) = cx
                    for cp in range(4):
                        tpt, tpb = tp_ring.next()
                        for c2 in range(2):
                            c = 2 * cp + c2
                            for g in range(G):
                                r = min(128, C - 128 * g)
                                P.op("pe", lambda e, tpt=tpt, g=g, r=r, c=c, c2=c2, xnt=xnt: e.transpose(
                                    tpt[:, c2, 128 * g:128 * g + r], xnt[0:r, g, c * 128:(c + 1) * 128], ident[0:r, 0:r]),
                                     reads=[xnb], pwrites=[tpb])
                        for c2 in range(2):
                            c = 2 * cp + c2
                            P.op("act", lambda e, tpt=tpt, c=c, c2=c2, hTt=hTt, s=s, C=C: e.activation(
                                out=hTt[:, c, 0:C], in_=tpt[:, c2, 0:C], func=AF.Identity, bias=sh1(s, c),
                                scale=gm1[:, s, c:c + 1]), reads=[tpb], pwrites=[hTb])

                class Stepper:
                    def __init__(self, gen):
                        self.gen = gen
                        self.done = gen is None
                        self.val = None

                    def step(self, n=1):
                        for _ in range(n):
                            if self.done:
                                return
                            try:
                                next(self.gen)
                            except StopIteration as e_:
                                self.done = True
                                self.val = e_.value

                    def finish(self):
                        while not self.done:
                            self.step()
                        return self.val

                def rstd_bc(stt, stb, n, scale):
                    rst, rsb = rs_ring.next()
                    P.op("act", lambda e: e.activation(out=rst[:, 0:n], in_=stt[:, 0:n], func=AF.Ln, bias=epsT[:], scale=scale),
                         reads=[stb], writes=[rsb])
                    rqt, rqb = rr_ring.next()
                    P.op("act", lambda e: e.activation(out=rqt[:, 0:n], in_=rst[:, 0:n], func=AF.Exp, scale=-0.5),
                         reads=[rsb], writes=[rqb])
                    return rqt, rqb

                def p1_back(cx, stp):
                    (s, off, t0, w, C, first, last, jlo, jhi, G, xnt, xnb, hTt, hTb, cst, snt, csb) = cx
                    def zgroup(col0, M, wsb=None, rot=False):
                        zt, zb = zp_ring.next()
                        if rot:
                            pairs = [(w_krr[:, kc, 0:64], hTt[:, kc, 0:C]) for kc in range(8)]
                        else:
                            pairs = [(w_in_sb[:, kc, col0:col0 + M], hTt[:, kc, 0:C]) for kc in range(8)]
                        P.mm(zt[0:M, 0:C], pairs, reads=[hTb] + WR, writes=[zb])
                        stp.step(2)
                        return zt, zb

                    for c in range(4):
                        zt, zb = zgroup(c * 128, 128)
                        P.op("act", lambda e, zt=zt, c=c, C=C: e.activation(out=hasb[:, c, 0:C], in_=zt[:, 0:C], func=AF.Copy),
                             reads=[zb], writes=[Bha[c]])
                    for c in range(4):
                        zt, zb = zgroup(1024 + c * 128, 128)
                        P.op("dve", lambda e, zt=zt, c=c, jlo=jlo, jhi=jhi: e.tensor_tensor(
                            out=psb[:, c, jlo:jhi], in0=zt[:, jlo:jhi], in1=hasb[:, c, jlo:jhi], op=ALU.mult),
                             reads=[zb, Bha[c]], pwrites=[Bp[c]])
                    if first:
                        P.op("pool", lambda e: e.memset(psb[:, :, 0:1], 0.0), pwrites=Bp)
                    if last:
                        P.op("pool", lambda e, C=C: e.memset(psb[:, :, C - 1:C], 0.0), pwrites=Bp)
                    for c in range(4):
                        zt, zb = zgroup(512 + c * 128, 128)
                        P.op("act", lambda e, zt=zt, c=c, C=C: e.activation(out=basb[:, c, 0:C], in_=zt[:, 0:C], func=AF.Copy),
                             reads=[zb], writes=[Bba[c]])
                    sqs = []
                    for c in range(3):
                        zt, zb = zgroup(1536 + c * 128, 128)
                        P.op("act", lambda e, zt=zt, c=c, C=C: e.activation(out=cqsb[:, c, 0:C], in_=zt[:, 0:C], func=AF.Copy),
                             reads=[zb], pwrites=[Bx["cqsb"]])
                        sqt, sqb = sq_ring.next()
                        P.op("act", lambda e, zt=zt, sqt=sqt, C=C: e.activation(out=sqt[:, 0:C], in_=zt[:, 0:C], func=AF.Square),
                             reads=[zb], writes=[sqb])
                        sqs.append((sqt, sqb))
                    stt, stb = zp_ring.next()
                    P.mm(stt[:, 0:C], [(ones[:], q[0][:, 0:C]) for q in sqs], reads=[q[1] for q in sqs], writes=[stb])
                    rqt, rqb = rstd_bc(stt, stb, C, 1.0 / QL)
                    for c in range(3):
                        P.op("dve", lambda e, c=c, rqt=rqt, C=C: e.scalar_tensor_tensor(
                            out=cqn[:, c, 0:C], in0=cqsb[:, c, 0:C], scalar=qg[:, c:c + 1], in1=rqt[:, 0:C],
                            op0=ALU.mult, op1=ALU.mult), reads=[Bx["cqsb"], rqb], pwrites=[Bx["cqn"]])
                    sqs = []
                    for c in range(2):
                        zt, zb = zgroup(1920 + c * 128, 128)
                        P.op("act", lambda e, zt=zt, c=c, C=C: e.activation(out=ckvsb[:, c, 0:C], in_=zt[:, 0:C], func=AF.Copy),
                             reads=[zb], pwrites=[Bx["ckvsb"]])
                        sqt, sqb = sq_ring.next()
                        P.op("act", lambda e, zt=zt, sqt=sqt, C=C: e.activation(out=sqt[:, 0:C], in_=zt[:, 0:C], func=AF.Square),
                             reads=[zb], writes=[sqb])
                        sqs.append((sqt, sqb))
                    stt, stb = zp_ring.next()
                    P.mm(stt[:, 0:C], [(ones[:], q[0][:, 0:C]) for q in sqs], reads=[q[1] for q in sqs], writes=[stb])
                    rkt, rkb = rstd_bc(stt, stb, C, 1.0 / KVL)
                    for c in range(2):
                        P.op("dve", lambda e, c=c, rkt=rkt, C=C: e.scalar_tensor_tensor(
                            out=ckvn[:, c, 0:C], in0=ckvsb[:, c, 0:C], scalar=kvg[:, c:c + 1], in1=rkt[:, 0:C],
                            op0=ALU.mult, op1=ALU.mult), reads=[Bx["ckvsb"], rkb], pwrites=[Bx["ckvn"]])
                    za, zab = zgroup(2176, 64)
                    zr, zrb = zgroup(0, 64, rot=True)
                    t1t, t1b = t1_ring.next()
                    t2t, t2b = t2_ring.next()
                    P.op("dve", lambda e, za=za, t1t=t1t, cst=cst, C=C: e.tensor_tensor(
                        out=t1t[:, 0:C], in0=za[0:64, 0:C], in1=cst[:, 0:C], op=ALU.mult), reads=[zab, csb], writes=[t1b])
                    P.op("dve", lambda e, zr=zr, t2t=t2t, snt=snt, C=C: e.tensor_tensor(
                        out=t2t[:, 0:C], in0=zr[0:64, 0:C], in1=snt[:, 0:C], op=ALU.mult), reads=[zrb, csb], writes=[t2b])
                    P.op("dve", lambda e, t1t=t1t, t2t=t2t, C=C: e.tensor_tensor(
                        out=kro[:, 0:C], in0=t1t[:, 0:C], in1=t2t[:, 0:C], op=ALU.add), reads=[t1b, t2b], writes=[Bx["kro"]])
                    P.dma("pool", krS[:, off + t0:off + t0 + w], kro[:, 1:w + 1], ds_st["kr"], reads=[Bx["kro"]])
                    ncx = stp.finish()
                    if ncx is not None:
                        p1_front_b(ncx)
                    for h in range(NH):
                        zt, zb = zp_ring.next()
                        P.mm(zt[:, 0:C], [(w_uq_sb[:, kc, h * 192:h * 192 + 128], cqn[:, kc, 0:C]) for kc in range(3)],
                             reads=[Bx["cqn"]] + WR, writes=[zb])
                        P.op("act", lambda e, zt=zt, h=h, C=C: e.activation(out=qTo[:, h, 0:C], in_=zt[:, 0:C], func=AF.Copy),
                             reads=[zb], pwrites=[Bx["qTo"]])
                        za, zab = zp_ring.next()
                        P.mm(za[0:64, 0:C], [(w_uq_sb[:, kc, h * 192 + 128:h * 192 + 192], cqn[:, kc, 0:C]) for kc in range(3)],
                             reads=[Bx["cqn"]] + WR, writes=[zab])
                        zr, zrb = zp_ring.next()
                        P.mm(zr[0:64, 0:C], [(w_uqr[:, kc, h * 64:(h + 1) * 64], cqn[:, kc, 0:C]) for kc in range(3)],
                             reads=[Bx["cqn"]] + WR, writes=[zrb])
                        t1t, t1b = t1_ring.next()
                        t2t, t2b = t2_ring.next()
                        P.op("dve", lambda e, za=za, t1t=t1t, cst=cst, C=C: e.tensor_tensor(
                            out=t1t[:, 0:C], in0=za[0:64, 0:C], in1=cst[:, 0:C], op=ALU.mult), reads=[zab, csb], writes=[t1b])
                        P.op("dve", lambda e, zr=zr, t2t=t2t, snt=snt, C=C: e.tensor_tensor(
                            out=t2t[:, 0:C], in0=zr[0:64, 0:C], in1=snt[:, 0:C], op=ALU.mult), reads=[zrb, csb], writes=[t2b])
                        P.op("dve", lambda e, t1t=t1t, t2t=t2t, h=h, C=C: e.tensor_tensor(
                            out=qro[:, h, 0:C], in0=t1t[:, 0:C], in1=t2t[:, 0:C], op=ALU.add),
                             reads=[t1b, t2b], pwrites=[Bx["qro"]])
                    P.dma("pool", qS[:, :, off + t0:off + t0 + w].rearrange("h p t -> p h t"), qTo[:, :, 1:w + 1],
                          ds_st["q"], reads=[Bx["qTo"]])
                    P.dma("pool", qrS[:, :, off + t0:off + t0 + w].rearrange("h p t -> p h t"), qro[:, :, 1:w + 1],
                          ds_st["qr"], reads=[Bx["qro"]])
                    for h in range(NH):
                        zt, zb = zp_ring.next()
                        P.mm(zt[:, 0:C], [(w_uk_sb[:, kc, h * 128:(h + 1) * 128], ckvn[:, kc, 0:C]) for kc in range(2)],
                             reads=[Bx["ckvn"]] + WR, writes=[zb])
                        P.op("act", lambda e, zt=zt, h=h, C=C: e.activation(out=kTo[:, h, 0:C], in_=zt[:, 0:C], func=AF.Copy),
                             reads=[zb], pwrites=[Bx["kTo"]])
                    P.dma("pool", kS[:, :, off + t0:off + t0 + w].rearrange("h p t -> p h t"), kTo[:, :, 1:w + 1],
                          ds_st["k"], reads=[Bx["kTo"]])
                    for g in range(G):
                        r = min(128, C - 128 * g)
                        zt, zb = zp_ring.next()
                        P.mm(zt[0:r, :], [(ckvn[:, kc, 128 * g:128 * g + r], w_uv_sb[:, kc, :]) for kc in range(2)],
                             reads=[Bx["ckvn"]] + WR, writes=[zb])
                        P.op("dve", lambda e, zt=zt, g=g, r=r: e.tensor_copy(out=vo[0:r, g, :], in_=zt[0:r, :]),
                             reads=[zb], pwrites=[Bx["vo"]])
                    for g in range(G):
                        r = min(128, C - 128 * g)
                        lo = max(1, 128 * g) - 128 * g
                        hi = min(w + 1, 128 * g + r) - 128 * g
                        if hi <= lo:
                            continue
                        tok0 = off + t0 - 1 + 128 * g
                        P.dma("pool", vS[tok0 + lo:tok0 + hi, :], vo[lo:hi, g, :], ds_st["v"], reads=[Bx["vo"]])

                    ya = hasb
                    Bya = Bha
                    for c in range(4):
                        P.op("dve", lambda e, c=c, w=w: e.tensor_scalar(out=ya[:, c, 1:w + 1], in0=psb[:, c, 1:w + 1],
                                                                        scalar1=convA[:, 1, c:c + 1], scalar2=None,
                                                                        op0=ALU.mult), reads=[Bp[c]], writes=[Bya[c]])
                    for c in range(4):
                        P.op("dve", lambda e, c=c, w=w: e.scalar_tensor_tensor(
                            out=ya[:, c, 1:w + 1], in0=psb[:, c, 0:w], scalar=convA[:, 0, c:c + 1], in1=ya[:, c, 1:w + 1],
                            op0=ALU.mult, op1=ALU.add), reads=[Bp[c], Bya[c]], writes=[Bya[c]])
                    for c in range(4):
                        P.op("dve", lambda e, c=c, w=w: e.scalar_tensor_tensor(
                            out=ya[:, c, 1:w + 1], in0=psb[:, c, 2:w + 2], scalar=convA[:, 2, c:c + 1], in1=ya[:, c, 1:w + 1],
                            op0=ALU.mult, op1=ALU.add), reads=[Bp[c], Bya[c]], writes=[Bya[c]])
                    sqs = []
                    for c in range(4):
                        P.op("dve", lambda e, c=c, w=w: e.tensor_tensor(out=ya[:, c, 1:w + 1], in0=ya[:, c, 1:w + 1],
                                                                        in1=basb[:, c, 1:w + 1], op=ALU.mult),
                             reads=[Bba[c], Bya[c]], writes=[Bya[c]])
                    for c in range(4):
                        sqt, sqb = sq_ring.next()
                        P.op("act", lambda e, c=c, w=w, sqt=sqt: e.activation(out=sqt[:, 0:w], in_=ya[:, c, 1:w + 1],
                                                                              func=AF.Square), reads=[Bya[c]], writes=[sqb])
                        sqs.append((sqt, sqb))
                    stt, stb = zp_ring.next()
                    P.mm(stt[:, 0:w], [(ones[:], q[0][:, 0:w]) for q in sqs], reads=[q[1] for q in sqs], writes=[stb])
                    rat, rab = rstd_bc(stt, stb, w, 1.0 / A_W)
                    for c in range(4):
                        P.op("dve", lambda e, c=c, rat=rat, w=w: e.scalar_tensor_tensor(
                            out=yan[:, c, 0:w], in0=ya[:, c, 1:w + 1], scalar=ag[:, c:c + 1], in1=rat[:, 0:w],
                            op0=ALU.mult, op1=ALU.mult), reads=[Bya[c], rab], pwrites=[Bx["yan"]])
                    P.dma("pool", mixS[0:4, :, off + t0:off + t0 + w].rearrange("c p t -> p c t"), yan[:, :, 0:w],
                          ds_st["ya"], reads=[Bx["yan"]])
                    return ncx

                tiles_all = []
                for s in range(NS):
                    tl = seq_tiles(seqs[s])
                    for ti, (t0, w) in enumerate(tl):
                        tiles_all.append((s, ti, t0, w, len(tl)))
                st0 = Stepper(p1_front_a(*tiles_all[0]))
                cx = st0.finish()
                p1_front_b(cx)
                for n_ in range(len(tiles_all)):
                    stp = Stepper(p1_front_a(*tiles_all[n_ + 1]) if n_ + 1 < len(tiles_all) else None)
                    cx = p1_back(cx, stp)
                P.emit_phase()
                if DEBUG_STOP == 1:
                    P.disabled = True

            with ExitStack() as ph:
                def sb(name, shape, dt):
                    return ph.enter_context(nc.sbuf_tensor(name, list(shape), dt))

                def pst(name, shape, dt):
                    return ph.enter_context(nc.psum_tensor(name, list(shape), dt))

                kT = sb("kT", [128, NH, SMAX], BF16)
                krT = sb("krT", [128, SMAX], BF16)
                V = sb("V", [128, SMAX // 128, NH * DV], BF16)
                qT = [sb("qT%d" % i, [128, NH, 512], BF16) for i in range(2)]
                qrT = [sb("qrT%d" % i, [128, NH, 512], BF16) for i in range(2)]
                pT = [sb("pT%d" % i, [128, 1024], BF16) for i in range(4)]
                acc = [sb("acc%d" % i, [128, 1024], F32) for i in range(2)]
                yb = sb("yb", [128, NH, 512], F32)
                rec = [sb("rec%d" % i, [128, 512], F32) for i in range(2)]
                sq = [sb("sq2_%d" % i, [128, 512], BF16) for i in range(4)]
                rs2 = sb("rs2", [128, 512], F32)
                rb = sb("rb", [128, 512], F32)
                ybn = [sb("ybn%d" % i, [128, NH, 512], BF16) for i in range(1)]
                sps = [pst("sps%d" % i, [128, 512], F32) for i in range(6)]
                ops_ = [pst("ops%d" % i, [128, 512], F32) for i in range(2)]
                s_ring = Ring(sps)
                o_ring = Ring(ops_)
                acc_ring = Ring(acc)
                pT_ring = Ring(pT)
                pT_hb = [[Buf(), Buf()] for _ in range(4)]
                rec_ring = Ring(rec)
                sq_ring = Ring(sq)
                q_ds = [P.dsem("q2_%d" % i) for i in range(2)]
                q_ring = Ring(list(zip(qT, qrT, q_ds)))
                ybn_ds = [P.dsem("ybn%d" % i) for i in range(1)]
                ybn_ring = Ring(list(zip(ybn, ybn_ds)))
                ds_kv = P.dsem("kv")
                Bkv = Buf()
                P.op("pool", lambda e: e.memset(krT[64:128, :], 0.0), writes=[Bkv])
                for i in range(2):
                    P.op("pool", lambda e, i=i: e.memset(qrT[i][64:128, :, :], 0.0), writes=[q_ring.bufs[i]])
                Byb = [Buf() for _ in range(NH)]
                Brs = Buf()
                Brb = Buf()

                for s in range(NS):
                    S = seqs[s]
                    off = offs[s]
                    NKC = S // 128
                    P.dma("sp", kT[:, :, 0:S], kS[:, :, off:off + S].rearrange("h p t -> p h t"), ds_kv, pwrites=[Bkv])
                    P.dma("sp", krT[0:64, 0:S], krS[:, off:off + S], ds_kv, pwrites=[Bkv])
                    vsrc = vS[off:off + S, :].rearrange("(c p) d -> p c d", p=128)
                    step = 16
                    for c0 in range(0, NKC, step):
                        c1 = min(NKC, c0 + step)
                        P.dma("sp", V[:, c0:c1, :], vsrc[:, c0:c1, :], ds_kv, pwrites=[Bkv])
                    NQ = S // 512
                    assert NKC % 2 == 0
                    LOOK = 2
                    DEFER = 3
                    items = [(qi, h, kc) for qi in range(NQ) for h in range(NH) for kc in range(NKC)]
                    qts = {}
                    hst = {}
                    sqs_q = {}
                    pend = []
                    deferred = {}

                    def load_q(qi):
                        if qi >= NQ or qi in qts:
                            return
                        q0 = qi * 512
                        (qt, qrt, qd), qb = q_ring.next()
                        P.dma("sp", qt[:], qS[:, :, off + q0:off + q0 + 512].rearrange("h p t -> p h t"), qd, writes=[qb])
                        P.dma("sp", qrt[0:64, :, :], qrS[:, :, off + q0:off + q0 + 512].rearrange("h p t -> p h t"), qd,
                              pwrites=[qb])
                        qts[qi] = (qt, qrt, qb)

                    def head_state(qi, h):
                        if (qi, h) not in hst:
                            ot, ob = o_ring.next()
                            acc2, accb_ = acc_ring.next()
                            hst[(qi, h)] = dict(ot=ot, ob=ob, acc2=acc2, accb=accb_, cur=None)
                        return hst[(qi, h)]

                    def qk(item):
                        qi, h, kc = item
                        qt, qrt, qb = qts[qi]
                        hs = head_state(qi, h)
                        st_, stb_ = s_ring.next()
                        P.mm(st_[:], [(kT[:, h, kc * 128:(kc + 1) * 128], qt[:, h, :]),
                                      (krT[:, kc * 128:(kc + 1) * 128], qrt[:, h, :])], reads=[Bkv, qb], writes=[stb_])
                        if kc % 2 == 0:
                            slot = pT_ring.i % 4
                            ptp, _ = pT_ring.next()
                            hs["cur"] = (ptp, pT_hb[slot])
                        ptp, hb = hs["cur"]
                        half = kc % 2
                        P.op("act", lambda e, st_=st_, ptp=ptp, half=half: e.activation(
                            out=ptp[:, half * 512:(half + 1) * 512], in_=st_[:], func=AF.Exp), reads=[stb_], writes=[hb[half]])
                        pend.append((item, ptp, hb))

                    def pv():
                        (qi, h, kc), ptp, hb = pend.pop(0)
                        hs = hst[(qi, h)]
                        ot, ob, acc2, accb_ = hs["ot"], hs["ob"], hs["acc2"], hs["accb"]
                        half = kc % 2
                        f = (kc == 0)
                        l = (kc == NKC - 1)
                        if f:
                            P.mm(ot[:], [(V[:, kc, h * DV:(h + 1) * DV], ptp[:, half * 512:(half + 1) * 512])],
                                 reads=[Bkv, hb[half]], writes=[ob], first=f, last=l)
                        else:
                            P.mm(ot[:], [(V[:, kc, h * DV:(h + 1) * DV], ptp[:, half * 512:(half + 1) * 512])],
                                 reads=[Bkv, hb[half]], pwrites=[ob], first=f, last=l)
                        if half == 1:
                            if kc == 1:
                                P.op("dve", lambda e, ptp=ptp, acc2=acc2: e.tensor_copy(out=acc2[:], in_=ptp[:]),
                                     reads=[hb[0], hb[1]], writes=[accb_])
                            else:
                                P.op("dve", lambda e, ptp=ptp, acc2=acc2: e.tensor_tensor(out=acc2[:], in0=acc2[:], in1=ptp[:],
                                                                                          op=ALU.add),
                                     reads=[hb[0], hb[1], accb_], writes=[accb_])

                    def tail(qi, h):
                        hs = hst.pop((qi, h))
                        ot, ob, acc2, accb_ = hs["ot"], hs["ob"], hs["acc2"], hs["accb"]
                        P.op("dve", lambda e, acc2=acc2: e.tensor_tensor(out=acc2[:, 0:512], in0=acc2[:, 0:512],
                                                                         in1=acc2[:, 512:1024], op=ALU.add),
                             reads=[accb_], writes=[accb_])
                        mt, mb = s_ring.next()
                        P.mm(mt[:], [(onesf[:], acc2[:, 0:512])], reads=[accb_], writes=[mb])
                        ret, reb = rec_ring.next()
                        P.op("act", lambda e, mt=mt, ret=ret: e.activation(out=ret[:], in_=mt[:], func=AF.Ln), reads=[mb],
                             writes=[reb])
                        P.op("act", lambda e, ret=ret: e.activation(out=ret[:], in_=ret[:], func=AF.Exp, scale=-1.0),
                             reads=[reb], writes=[reb])
                        P.op("dve", lambda e, ot=ot, ret=ret, h=h: e.tensor_tensor(out=yb[:, h, :], in0=ot[:], in1=ret[:],
                                                                                  op=ALU.mult),
                             reads=[ob, reb], writes=[Byb[h]])
                        sqt, sqb = sq_ring.next()
                        P.op("act", lambda e, sqt=sqt, h=h: e.activation(out=sqt[:], in_=yb[:, h, :], func=AF.Square),
                             reads=[Byb[h]], writes=[sqb])
                        sqs_q.setdefault(qi, []).append((sqt, sqb))
                        if h == NH - 1:
                            qtile_end(qi)

                    def qtile_end(qi):
                        q0 = qi * 512
                        sqs = sqs_q.pop(qi)
                        stt, stb = s_ring.next()
                        P.mm(stt[:], [(ones[:], q[0][:]) for q in sqs], reads=[q[1] for q in sqs], writes=[stb])
                        P.op("act", lambda e, stt=stt: e.activation(out=rs2[:], in_=stt[:], func=AF.Ln, bias=epsT[:],
                                                                   scale=1.0 / A_W), reads=[stb], writes=[Brs])
                        P.op("act", lambda e: e.activation(out=rb[:], in_=rs2[:], func=AF.Exp, scale=-0.5), reads=[Brs],
                             writes=[Brb])
                        (ybt, ybd), ybb = ybn_ring.next()
                        for h in range(NH):
                            P.op("dve", lambda e, h=h, ybt=ybt: e.scalar_tensor_tensor(
                                out=ybt[:, h, :], in0=yb[:, h, :], scalar=bg[:, h:h + 1], in1=rb[:], op0=ALU.mult, op1=ALU.mult),
                                 reads=[Byb[h], Brb], pwrites=[ybb])
                        P.dma("pool", mixS[4:8, :, off + q0:off + q0 + 512].rearrange("c p t -> p c t"), ybt[:], ybd, reads=[ybb])

                    load_q(0)
                    load_q(1)
                    for i in range(min(LOOK, len(items))):
                        qk(items[i])
                    for i, it in enumerate(items):
                        if i + LOOK < len(items):
                            nx = items[i + LOOK]
                            if nx[1] == 0 and nx[2] == 0:
                                load_q(nx[0] + 1)
                            qk(nx)
                        pv()
                        for fn_ in deferred.pop(i, []):
                            fn_()
                        if it[2] == NKC - 1:
                            deferred.setdefault(i + DEFER, []).append(lambda qi=it[0], h=it[1]: tail(qi, h))
                    for k_ in sorted(deferred):
                        for fn_ in deferred[k_]:
                            fn_()
                P.emit_phase()
                if DEBUG_STOP == 2:
                    P.disabled = True

            with ExitStack() as ph:
                def sb(name, shape, dt):
                    return ph.enter_context(nc.sbuf_tensor(name, list(shape), dt))

                def pst(name, shape, dt):
                    return ph.enter_context(nc.psum_tensor(name, list(shape), dt))

                w_o_sb = sb("w_o_sb", [128, 8, D], BF16)
                w_dn_sb = sb("w_dn_sb", [128, NU, D], BF16)
                wup = [sb("wup%d" % i, [128, 8, 256], BF16) for i in range(4)]
                g1bc = sb("g1bc", [128, D], F32)
                g2bc = sb("g2bc", [128, D], F32)
                fgbc = sb("fgbc", [128, D], F32)
                mixT = sb("mixT", [128, 8, 512], BF16)
                xin = [sb("x3in%d" % i, [128, D], F32) for i in range(2)]
                x1 = sb("x1", [128, 4, D], F32)
                x1b = sb("x1b", [128, 4, D], F32)
                xn2 = sb("xn2", [128, 4, D], BF16)
                h2T = sb("h2T", [128, 8, 512], BF16)
                aT = sb("aT", [128, NU, 512], BF16)
                c1 = [sb("c1_%d" % i, [128, 512], F32) for i in range(3)]
                c2 = [sb("c2_%d" % i, [128, 512], F32) for i in range(4)]
                sg = [sb("sg%d" % i, [128, 512], F32) for i in range(2)]
                tmp = [sb("tmp%d" % i, [128, 512], F32) for i in range(2)]
                junk = sb("junk3", [128, D], BF16)
                ms = sb("ms3", [128, 8], F32)
                sd = sb("sd3", [128, 8], F32)
                rstd = sb("rstd3", [128, 8], F32)
                tp = [pst("tp3_%d" % i, [128, 2, 512], BF16) for i in range(2)]
                zp = [pst("zp3_%d" % i, [128, 512], F32) for i in range(6)]
                tp_ring = Ring(tp)
                zp_ring = Ring(zp)
                c1_ring = Ring(c1)
                c2_ring = Ring(c2)
                sg_ring = Ring(sg)
                tmp_ring = Ring(tmp)
                wup_ds = [P.dsem("wup%d" % i) for i in range(4)]
                wup_ring = Ring(list(zip(wup, wup_ds)))
                xin_ds = [P.dsem("x3in%d" % i) for i in range(2)]
                xin_ring = Ring(list(zip(xin, xin_ds)))
                ds_w = P.dsem("p3w")
                ds_g = P.dsem("p3g")
                ds_g2 = P.dsem("p3g2")
                ds_mix = P.dsem("mix")
                ds_out = [P.dsem("out%d" % g) for g in range(4)]
                ds_outb = [P.dsem("outb%d" % g) for g in range(4)]
                Bw = Buf()
                Bg = Buf()
                Bfg = Buf()
                Bmix = Buf()
                Bx1 = [Buf() for _ in range(4)]
                Bxn2 = Buf()
                Bh2 = Buf()
                BaT = Buf()
                Bst = [Buf() for _ in range(4)]
                Bst2 = [Buf() for _ in range(4)]

                class Stepper3:
                    def __init__(self, gen):
                        self.gen = gen
                        self.done = gen is None
                        self.val = None

                    def step(self, n=1):
                        for _ in range(n):
                            if self.done:
                                return
                            try:
                                next(self.gen)
                            except StopIteration as e_:
                                self.done = True
                                self.val = e_.value

                    def finish(self):
                        while not self.done:
                            self.step()
                        return self.val

                P.dma("pool", w_o_sb[:], w_o_d.rearrange("(kc p) n -> p kc n", p=128), ds_w, pwrites=[Bw])
                P.dma("pool", w_dn_sb[:, 0:11, :], w_dn_d[0:11 * 128, :].rearrange("(kc p) n -> p kc n", p=128), ds_w, pwrites=[Bw])
                P.dma("pool", w_dn_sb[:, 11:22, :], w_dn_d[11 * 128:22 * 128, :].rearrange("(kc p) n -> p kc n", p=128), ds_w,
                      pwrites=[Bw])
                P.dma("sp", fgbc[:], fg_d.to_broadcast([128, D]), ds_g, writes=[Bfg])
                P.op("pool", lambda e: e.memset(mixT[:], 0.0), writes=[Bmix])
                for i in range(2):
                    P.op("pool", lambda e, i=i: e.memset(xin[i][:], 0.0), writes=[xin_ring.bufs[i]])
                P.op("pool", lambda e: e.memset(aT[:], 0.0), writes=[BaT])

                Bg1 = Buf()
                Bg2 = Buf()
                x1s = [x1, x1b]
                Bx1s = [[Buf() for _ in range(4)] for _ in range(2)]
                x1_i = [0]

                def p3_A1(s, ti, t0, w, ntl):
                    off = offs[s]
                    C = w + 2
                    first = (ti == 0)
                    last = (ti == ntl - 1)
                    jlo = 1 if first else 0
                    jhi = C - 1 if last else C
                    G = -(-C // 128)
                    tokc = off + t0 - 1
                    x1t = x1s[x1_i[0] % 2]
                    Bx1 = Bx1s[x1_i[0] % 2]
                    x1_i[0] += 1
                    if first:
                        P.dma("sp", g1bc[:], modS[s:s + 1, 2 * D:3 * D].to_broadcast([128, D]), ds_g, writes=[Bg1])
                    P.dma("sp", mixT[:, :, jlo:jhi], mixS[:, :, tokc + jlo:tokc + jhi].rearrange("c p t -> p c t"), ds_mix,
                          writes=[Bmix])
                    yield
                    for g in range(G):
                        r = min(128, C - 128 * g)
                        lo = max(jlo, 128 * g) - 128 * g
                        hi = min(jhi, 128 * g + r) - 128 * g
                        (xt, xd), xb = xin_ring.next()
                        tok0 = tokc + 128 * g
                        P.dma("sp", xt[lo:hi, :], x_d[tok0 + lo:tok0 + hi, :], xd, writes=[xb])
                        yield
                        for hf in range(2):
                            zt, zb = zp_ring.next()
                            P.mm(zt[0:r, :], [(mixT[:, c, 128 * g:128 * g + r], w_o_sb[:, c, hf * 512:(hf + 1) * 512])
                                              for c in range(8)], reads=[Bmix, Bw], writes=[zb])
                            yield
                            tt, tb = tmp_ring.next()
                            P.op("dve", lambda e, zt=zt, tt=tt, r=r, hf=hf: e.tensor_tensor(
                                out=tt[0:r, :], in0=zt[0:r, :], in1=g1bc[0:r, hf * 512:(hf + 1) * 512], op=ALU.mult),
                                 reads=[zb, Bg1], writes=[tb])
                            yield
                            P.op("dve", lambda e, tt=tt, xt=xt, r=r, hf=hf, g=g: e.tensor_tensor(
                                out=x1t[0:r, g, hf * 512:(hf + 1) * 512], in0=tt[0:r, :], in1=xt[0:r, hf * 512:(hf + 1) * 512],
                                op=ALU.add), reads=[tb, xb], pwrites=[Bx1[g]])
                            yield
                        P.op("act", lambda e, g=g: e.activation(out=junk[:], in_=x1t[:, g, :], func=AF.Square, scale=1.0 / 32.0,
                                                                accum_out=ms[:, g:g + 1]), reads=[Bx1[g]], writes=[Bst[g]])
                        yield
                        P.op("act", lambda e, g=g: e.activation(out=sd[:, g:g + 1], in_=ms[:, g:g + 1], func=AF.Sqrt,
                                                                bias=epsT[:], scale=1.0), reads=[Bst[g]], writes=[Bst[g]])
                        yield
                        P.op("dve", lambda e, g=g: e.reciprocal(out=rstd[:, g:g + 1], in_=sd[:, g:g + 1]),
                             reads=[Bst[g]], writes=[Bst[g]])
                        yield
                        P.op("act", lambda e, g=g: e.activation(out=xn2[:, g, :], in_=x1t[:, g, :], func=AF.Identity,
                                                                scale=rstd[:, g:g + 1]), reads=[Bx1[g], Bst[g]], pwrites=[Bxn2])
                        yield
                    return (s, off, t0, w, C, first, last, jlo, jhi, G, tokc, x1t, Bx1)

                def p3_A2(cx):
                    (s, off, t0, w, C, first, last, jlo, jhi, G, tokc, x1t, Bx1) = cx
                    for cp in range(4):
                        tpt, tpb = tp_ring.next()
                        for c2 in range(2):
                            c = 2 * cp + c2
                            for g in range(G):
                                r = min(128, C - 128 * g)
                                P.op("pe", lambda e, tpt=tpt, g=g, r=r, c=c, c2=c2: e.transpose(
                                    tpt[:, c2, 128 * g:128 * g + r], xn2[0:r, g, c * 128:(c + 1) * 128], ident[0:r, 0:r]),
                                     reads=[Bxn2], pwrites=[tpb])
                            yield
                        for c2 in range(2):
                            c = 2 * cp + c2
                            P.op("act", lambda e, tpt=tpt, c=c, c2=c2, s=s, jlo=jlo, jhi=jhi: e.activation(
                                out=h2T[:, c, jlo:jhi], in_=tpt[:, c2, jlo:jhi], func=AF.Identity, bias=sh2(s, c),
                                scale=gm2[:, s, c:c + 1]), reads=[tpb], pwrites=[Bh2])
                            yield
                    if first:
                        P.op("pool", lambda e: e.memset(h2T[:, :, 0:1], 0.0), pwrites=[Bh2])
                    if last:
                        P.op("pool", lambda e, C=C: e.memset(h2T[:, :, C - 1:C], 0.0), pwrites=[Bh2])
                    return None

                def p3_B(cx, stp):
                    (s, off, t0, w, C, first, last, jlo, jhi, G, tokc, x1t, Bx1) = cx
                    if first:
                        P.dma("sp", g2bc[:], modS[s:s + 1, 5 * D:6 * D].to_broadcast([128, D]), ds_g2, writes=[Bg2])
                    for u in range(NU):
                        (wt, wd), wb = wup_ring.next()
                        P.dma("sp", wt[:], wupS[u], wd, writes=[wb])
                        zg, zgb = zp_ring.next()
                        P.mm(zg[:, 0:C], [(wt[:, kc, 0:128], h2T[:, kc, 0:C]) for kc in range(8)], reads=[wb, Bh2], writes=[zgb])
                        zv, zvb = zp_ring.next()
                        P.mm(zv[:, 0:C], [(wt[:, kc, 128:256], h2T[:, kc, 0:C]) for kc in range(8)], reads=[wb, Bh2], writes=[zvb])
                        outs = []
                        for (zt, zb, ch) in ((zg, zgb, u), (zv, zvb, NU + u)):
                            c1t, c1b = c1_ring.next()
                            P.op("act", lambda e, zt=zt, c1t=c1t, ch=ch, w=w: e.activation(
                                out=c1t[:, 0:w], in_=zt[:, 1:w + 1], func=AF.Identity, scale=convF[:, 1, ch:ch + 1]),
                                 reads=[zb], writes=[c1b])
                            c2t, c2b = c2_ring.next()
                            P.op("dve", lambda e, zt=zt, c1t=c1t, c2t=c2t, ch=ch, w=w: e.scalar_tensor_tensor(
                                out=c2t[:, 0:w], in0=zt[:, 0:w], scalar=convF[:, 0, ch:ch + 1], in1=c1t[:, 0:w],
                                op0=ALU.mult, op1=ALU.add), reads=[zb, c1b], writes=[c2b])
                            P.op("dve", lambda e, zt=zt, c2t=c2t, ch=ch, w=w: e.scalar_tensor_tensor(
                                out=c2t[:, 0:w], in0=zt[:, 2:w + 2], scalar=convF[:, 2, ch:ch + 1], in1=c2t[:, 0:w],
                                op0=ALU.mult, op1=ALU.add), reads=[zb, c2b], writes=[c2b])
                            outs.append((c2t, c2b))
                        sgt, sgb = sg_ring.next()
                        P.op("act", lambda e, sgt=sgt, c2t=outs[0][0], w=w: e.activation(out=sgt[:, 0:w], in_=c2t[:, 0:w],
                                                                                          func=AF.Silu),
                             reads=[outs[0][1]], writes=[sgb])
                        P.op("pool", lambda e, sgt=sgt, c2t=outs[1][0], u=u, w=w: e.tensor_tensor(
                            out=aT[:, u, 1:w + 1], in0=sgt[:, 0:w], in1=c2t[:, 0:w], op=ALU.mult),
                             reads=[sgb, outs[1][1]], pwrites=[BaT])
                        if u == 2:
                            stp.step(2)

                def p3_C(cx, stp):
                    (s, off, t0, w, C, first, last, jlo, jhi, G, tokc, x1t, Bx1) = cx
                    for g in range(G):
                        r = min(128, C - 128 * g)
                        lo = max(1, 128 * g) - 128 * g
                        hi = min(w + 1, 128 * g + r) - 128 * g
                        for hf in range(2):
                            zt, zb = zp_ring.next()
                            P.mm(zt[0:r, :], [(aT[:, u, 128 * g:128 * g + r], w_dn_sb[:, u, hf * 512:(hf + 1) * 512])
                                              for u in range(NU)], reads=[BaT, Bw], writes=[zb])
                            if 2 * g + hf >= 2:
                                stp.step(4)
                            tt, tb = tmp_ring.next()
                            P.op("dve", lambda e, zt=zt, tt=tt, r=r, hf=hf: e.tensor_tensor(
                                out=tt[0:r, :], in0=zt[0:r, :], in1=g2bc[0:r, hf * 512:(hf + 1) * 512], op=ALU.mult),
                                 reads=[zb, Bg2], writes=[tb])
                            P.op("dve", lambda e, tt=tt, r=r, hf=hf, g=g: e.tensor_tensor(
                                out=x1t[0:r, g, hf * 512:(hf + 1) * 512], in0=tt[0:r, :], in1=x1t[0:r, g, hf * 512:(hf + 1) * 512],
                                op=ALU.add), reads=[tb, Bx1[g]], writes=[Bx1[g]])
                        P.op("act", lambda e, g=g: e.activation(out=junk[:], in_=x1t[:, g, :], func=AF.Square, scale=1.0 / 32.0,
                                                                accum_out=ms[:, 4 + g:5 + g]), reads=[Bx1[g]], writes=[Bst2[g]])
                        P.op("act", lambda e, g=g: e.activation(out=sd[:, 4 + g:5 + g], in_=ms[:, 4 + g:5 + g], func=AF.Sqrt,
                                                                bias=epsT[:], scale=1.0), reads=[Bst2[g]], writes=[Bst2[g]])
                        P.op("dve", lambda e, g=g: e.reciprocal(out=rstd[:, 4 + g:5 + g], in_=sd[:, 4 + g:5 + g]),
                             reads=[Bst2[g]], writes=[Bst2[g]])
                        P.op("dve", lambda e, g=g: e.scalar_tensor_tensor(
                            out=x1t[:, g, :], in0=x1t[:, g, :], scalar=rstd[:, 4 + g:5 + g], in1=fgbc[:], op0=ALU.mult, op1=ALU.mult),
                             reads=[Bst2[g], Bfg, Bx1[g]], writes=[Bx1[g]])
                        if hi > lo:
                            tok0 = tokc + 128 * g
                            P.dma("pool", y_d[tok0 + lo:tok0 + hi, :], x1t[lo:hi, g, :],
                                  (ds_out if x1t is x1 else ds_outb)[g], reads=[Bx1[g]])

                tiles3 = []
                for s in range(NS):
                    tl = seq_tiles(seqs[s])
                    for ti, (t0, w) in enumerate(tl):
                        tiles3.append((s, ti, t0, w, len(tl)))
                cx = Stepper3(p3_A1(*tiles3[0])).finish()
                Stepper3(p3_A2(cx)).finish()
                for n_ in range(len(tiles3)):
                    stA1 = Stepper3(p3_A1(*tiles3[n_ + 1]) if n_ + 1 < len(tiles3) else None)
                    p3_B(cx, stA1)
                    ncx = stA1.finish()
                    stA2 = Stepper3(p3_A2(ncx) if ncx is not None else None)
                    p3_C(cx, stA2)
                    stA2.finish()
                    cx = ncx

                P.emit_phase()
                if DEBUG_STOP == 3:
                    P.disabled = True
        except _Stop:
            pass
        P.final_wait()
    return nc


def rope_tables_np(smax):
    inv = (1.0 / (np.float32(THETA) ** (np.arange(0, DR, 2, dtype=np.float32) / np.float32(DR)))).astype(np.float32)
    ang = (np.arange(-1, smax + 1, dtype=np.float32)[None, :] * inv[:, None]).astype(np.float32)
    cos = np.cos(ang).astype(np.float32)
    sin = np.sin(ang).astype(np.float32)
    return np.concatenate([cos, cos], 0), np.concatenate([sin, sin], 0)


_CACHE = {}


def run_cores(seqs, xs, cs, weights):
    key = tuple(seqs)
    if key not in _CACHE:
        _CACHE[key] = build_program(list(seqs))
    nc = _CACHE[key]
    cos2, sin2 = rope_tables_np(max(seqs))
    shared = dict(weights)
    shared["cos2"] = cos2
    shared["sin2"] = sin2
    in_maps = []
    for i in range(len(xs)):
        m = dict(shared)
        m["x"] = np.ascontiguousarray(xs[i], dtype=np.float32)
        m["c"] = np.ascontiguousarray(cs[i], dtype=np.float32)
        in_maps.append(m)
    res = run_bass_kernel_spmd(nc, in_maps, core_ids=list(range(len(xs))))
    return [r["y"] for r in res.results]


def prep_weights(w_ada, b_ada, norm1_g, w_in, conv_a_w, q_norm_g, w_uq, kv_norm_g, w_ukv, out_norm_a_g, out_norm_b_g,
                 w_o, norm2_g, w_up, ffn_conv_w, w_down, final_g):
    f = lambda a: np.ascontiguousarray(np.asarray(a, dtype=np.float32))
    return dict(
        w_ada=f(w_ada[0]), b_ada=f(b_ada[0]).reshape(1, -1), norm1_g=f(norm1_g[0]).reshape(1, -1), w_in=f(w_in[0]),
        conv_a_w=f(conv_a_w[0]), q_norm_g=f(q_norm_g[0]).reshape(1, -1), w_uq=f(w_uq[0]),
        kv_norm_g=f(kv_norm_g[0]).reshape(1, -1), w_ukv=f(w_ukv[0]), out_norm_a_g=f(out_norm_a_g[0]).reshape(1, -1),
        out_norm_b_g=f(out_norm_b_g[0]).reshape(1, -1), w_o=f(w_o[0]), norm2_g=f(norm2_g[0]).reshape(1, -1),
        w_up=f(w_up[0]), ffn_conv_w=f(ffn_conv_w[0]), w_down=f(w_down[0]), final_g=f(final_g).reshape(1, -1))


def kernel(x_prompt, x_sample, c_prompt, c_sample, w_ada, b_ada, norm1_g, w_in, conv_a_w, q_norm_g, w_uq, kv_norm_g,
           w_ukv, out_norm_a_g, out_norm_b_g, w_o, norm2_g, w_up, ffn_conv_w, w_down, final_g):
    x_prompt = np.asarray(x_prompt, dtype=np.float32)
    x_sample = np.asarray(x_sample, dtype=np.float32)
    c_prompt = np.asarray(c_prompt, dtype=np.float32)
    c_sample = np.asarray(c_sample, dtype=np.float32)
    weights = prep_weights(w_ada, b_ada, norm1_g, w_in, conv_a_w, q_norm_g, w_uq, kv_norm_g, w_ukv, out_norm_a_g,
                           out_norm_b_g, w_o, norm2_g, w_up, ffn_conv_w, w_down, final_g)
    SS = x_sample.shape[1]
    SP = x_prompt.shape[1]
    seqs = (SS, SP, SP)
    xs, cs = [], []
    for i in range(N_CORES):
        xs.append(np.concatenate([x_sample[i], x_prompt[2 * i], x_prompt[2 * i + 1]], axis=0))
        cs.append(np.stack([c_sample[i], c_prompt[2 * i], c_prompt[2 * i + 1]], axis=0))
    ys = run_cores(seqs, xs, cs, weights)
    y_prompt = np.empty_like(x_prompt)
    y_sample = np.empty_like(x_sample)
    for i in range(N_CORES):
        y = ys[i]
        y_sample[i] = y[0:SS]
        y_prompt[2 * i] = y[SS:SS + SP]
        y_prompt[2 * i + 1] = y[SS + SP:SS + 2 * SP]
    return (y_prompt, y_sample)
```

```python
import math
from contextlib import ExitStack

import numpy as np
import concourse.bass as bass
import concourse.mybir as mybir
from concourse.bass_utils import run_bass_kernel_spmd

F32 = mybir.dt.float32
BF16 = mybir.dt.bfloat16
AF = mybir.ActivationFunctionType
ALU = mybir.AluOpType

D = 1024
A_W = 512
NH = 4
DN = 128
DR = 64
DV = 128
QL = 384
KVL = 256
INC = 2240
DFF = 2816
NU = DFF // 128
ATTN_SCALE = 1.0 / math.sqrt(DN + DR)
EPS = 1e-6
THETA = 10000.0
N_CORES = 8
SAME_ENGINE_SYNC = True

ENGS = ["pe", "act", "dve", "pool", "sp"]


class Buf:
    __slots__ = ("w", "r", "war")

    def __init__(self):
        self.w = {}
        self.r = {}
        self.war = {}


def _mg(d, k, v):
    if d.get(k, -1) < v:
        d[k] = v


class DSem:
    __slots__ = ("h", "val", "val0")

    def __init__(self, h):
        self.h = h
        self.val = 0
        self.val0 = 0


class Op:
    __slots__ = ("eng", "idx", "fn", "deps", "sig", "dsem", "dval")


class Prog:
    def __init__(self, nc, es):
        self.nc = nc
        self.es = es
        self.eh = dict(pe=nc.tensor, act=nc.scalar, dve=nc.vector, pool=nc.gpsimd, sp=nc.sync)
        self.psem = {e: es.enter_context(nc.semaphore("pg_" + e)) for e in ENGS}
        self.cnt = {e: 0 for e in ENGS}
        self.waited = {e: {} for e in ENGS}
        self.ops = {e: [] for e in ENGS}
        self.dsems = []
        self.first_phase = True
        self.disabled = False

    def dsem(self, name):
        d = DSem(self.es.enter_context(self.nc.semaphore("d_" + name)))
        self.dsems.append(d)
        return d

    def op(self, eng, fn, reads=(), writes=(), pwrites=(), dsem=None):
        if self.disabled:
            return None
        self.nops = getattr(self, 'nops', 0) + 1
        if self.nops > DEBUG_MAXOPS:
            return None
        deps = {}
        for b in reads:
            for k, v in b.w.items():
                _mg(deps, k, v)
        for b in writes:
            war = {}
            for k, v in b.r.items():
                _mg(war, k, v)
            for k, v in b.w.items():
                _mg(war, k, v)
            b.war = war
            for k, v in war.items():
                _mg(deps, k, v)
        newver = []
        for b in pwrites:
            if b.r or not b.w:
                war = {}
                for k, v in b.r.items():
                    _mg(war, k, v)
                for k, v in b.w.items():
                    _mg(war, k, v)
                b.war = war
                newver.append(b)
            for k, v in b.war.items():
                _mg(deps, k, v)
        o = Op()
        o.eng = eng
        o.idx = len(self.ops[eng])
        o.fn = fn
        o.deps = deps
        o.sig = False
        o.dsem = dsem
        if dsem is not None:
            dsem.val += 16
            o.dval = dsem.val
            key, val = dsem, dsem.val
        else:
            o.dval = 0
            key, val = eng, o.idx
        self.ops[eng].append(o)
        for k, v in deps.items():
            if isinstance(k, str):
                self.ops[k][v].sig = True
        for b in reads:
            _mg(b.r, key, val)
        for b in writes:
            b.w = {key: val}
            b.r = {}
        for b in newver:
            b.w = {}
            b.r = {}
        for b in pwrites:
            _mg(b.w, key, val)
        return o

    def dma(self, q, out, in_, dsem, reads=(), writes=(), pwrites=(), nonc=False):
        nc = self.nc

        def fn(e):
            if nonc:
                with nc.allow_non_contiguous_dma("small strided setup load"):
                    return e.dma_start(out=out, in_=in_)
            return e.dma_start(out=out, in_=in_)

        return self.op(q, fn, reads, writes, pwrites, dsem=dsem)

    def mm(self, out, pairs, reads=(), writes=(), pwrites=(), first=True, last=True):
        def fn(e):
            n = len(pairs)
            ins = None
            for i, (l, r) in enumerate(pairs):
                ins = e.matmul(out, l, r, start=(first and i == 0), stop=(last and i == n - 1))
            return ins

        return self.op("pe", fn, reads, writes, pwrites)

    def emit_phase(self):
        if self.disabled:
            return
        nc = self.nc
        for e in ENGS:
            for o in reversed(self.ops[e]):
                if o.dsem is None:
                    o.sig = True
                    break
        signo = {}
        for e in ENGS:
            c = self.cnt[e]
            lst = []
            for o in self.ops[e]:
                if o.sig and o.dsem is None:
                    c += 1
                lst.append(c)
            signo[e] = lst
        fence = None
        if not self.first_phase:
            fence = ([(self.psem[k], self.cnt[k], k) for k in ENGS if self.cnt[k] > 0]
                     + [(d.h, d.val0, d) for d in self.dsems if getattr(d, "val0", 0) > 0])
        with nc.Block() as block:
            reg = dict(pe=block.tensor, act=block.scalar, dve=block.vector, pool=block.gpsimd, sp=block.sync)
            for e in ENGS:
                ops = self.ops[e]

                def body(engh, e=e, ops=ops):
                    wd = self.waited[e]
                    if fence is not None:
                        for h, v, k in fence:
                            if k == e:
                                continue
                            if wd.get(k, 0) >= v:
                                continue
                            engh.wait_ge(h, v)
                            wd[k] = v
                    for o in ops:
                        for k, v in o.deps.items():
                            if isinstance(k, str):
                                if k == e and (e == "pe" or not SAME_ENGINE_SYNC):
                                    continue
                                need = signo[k][v]
                                h = self.psem[k]
                            else:
                                need = v
                                h = k.h
                            if wd.get(k, 0) >= need:
                                continue
                            engh.wait_ge(h, need)
                            wd[k] = need
                        ins = o.fn(engh)
                        if o.dsem is not None:
                            ins.then_inc(o.dsem.h, 16)
                        elif o.sig:
                            ins.then_inc(self.psem[e], 1)

                reg[e](body)
        for e in ENGS:
            if self.ops[e]:
                self.cnt[e] = signo[e][-1]
            self.ops[e] = []
        for d in self.dsems:
            d.val0 = d.val
        self.first_phase = False

    def final_wait(self):
        nc = self.nc
        with nc.Block() as block:
            def body(engh):
                for k in ENGS:
                    if k != "sp" and self.cnt[k] > 0:
                        engh.wait_ge(self.psem[k], self.cnt[k])
                for d in self.dsems:
                    if d.val > 0:
                        engh.wait_ge(d.h, d.val)
            block.sync(body)


class Ring:
    def __init__(self, items):
        self.items = items
        self.bufs = [Buf() for _ in items]
        self.i = 0

    def next(self):
        k = self.i % len(self.items)
        self.i += 1
        return self.items[k], self.bufs[k]


def seq_tiles(S):
    n = -(-S // 510)
    W = -(-S // n)
    out = []
    t = 0
    while t < S:
        w = min(W, S - t)
        out.append((t, w))
        t += w
    return out


DEBUG_STOP = 99
DEBUG_MAXOPS = 10 ** 9


class _Stop(Exception):
    pass


def build_program(seqs):
    NT = sum(seqs)
    offs = [sum(seqs[:i]) for i in range(len(seqs))]
    NS = len(seqs)
    SMAX = max(seqs)
    nc = bass.Bass("TRN2", target_bir_lowering=False)

    def din(name, shape, dt=F32):
        return nc.dram_tensor(name, list(shape), dt, kind="ExternalInput").ap()

    def dscr(name, shape, dt=BF16):
        return nc.dram_tensor(name, list(shape), dt, kind="Internal").ap()

    x_d = din("x", [NT, D])
    c_d = din("c", [NS, D])
    w_ada_d = din("w_ada", [D, 6 * D])
    b_ada_d = din("b_ada", [1, 6 * D])
    n1g_d = din("norm1_g", [1, D])
    w_in_d = din("w_in", [D, INC])
    conva_d = din("conv_a_w", [3, A_W])
    qng_d = din("q_norm_g", [1, QL])
    w_uq_d = din("w_uq", [QL, NH * (DN + DR)])
    kvng_d = din("kv_norm_g", [1, KVL])
    w_ukv_d = din("w_ukv", [KVL, NH * (DN + DV)])
    ang_d = din("out_norm_a_g", [1, A_W])
    bng_d = din("out_norm_b_g", [1, A_W])
    w_o_d = din("w_o", [D, D])
    n2g_d = din("norm2_g", [1, D])
    w_up_d = din("w_up", [D, 2 * DFF])
    convf_d = din("ffn_conv_w", [3, 2 * DFF])
    w_dn_d = din("w_down", [DFF, D])
    fg_d = din("final_g", [1, D])
    cos_d = din("cos2", [64, SMAX + 2])
    sin_d = din("sin2", [64, SMAX + 2])
    y_d = nc.dram_tensor("y", [NT, D], F32, kind="ExternalOutput").ap()

    qS = dscr("qS", [NH, 128, NT])
    qrS = dscr("qrS", [NH, 64, NT])
    kS = dscr("kS", [NH, 128, NT])
    krS = dscr("krS", [64, NT])
    vS = dscr("vS", [NT, NH * DV])
    mixS = dscr("mixS", [8, 128, NT])
    modS = dscr("modS", [NS, 6 * D], F32)
    wupS = dscr("wupS", [NU, 128, 8, 256])

    with ExitStack() as es:
        P = Prog(nc, es)

        def sbg(name, shape, dt):
            return es.enter_context(nc.sbuf_tensor(name, list(shape), dt))

        ident = sbg("ident", [128, 128], BF16)
        ones = sbg("ones", [128, 128], BF16)
        epsT = sbg("epsT", [128, 1], F32)
        onesf = sbg("onesf", [128, 128], F32)
        n1g = sbg("n1g", [128, 8], F32)
        n2g = sbg("n2g", [128, 8], F32)
        qg = sbg("qg", [128, 3], F32)
        kvg = sbg("kvg", [128, 2], F32)
        ag = sbg("ag", [128, 4], F32)
        bg = sbg("bg", [128, 4], F32)
        convA = sbg("convA", [128, 3, 4], F32)
        convF = sbg("convF", [128, 3, 44], F32)
        modT = sbg("modT", [128, NS, 48], F32)
        gm1 = sbg("gm1", [128, NS, 8], F32)
        gm2 = sbg("gm2", [128, NS, 8], F32)

        def sh1(s, c):
            return modT[:, s, 0 * 8 + c:0 * 8 + c + 1]

        def sh2(s, c):
            return modT[:, s, 3 * 8 + c:3 * 8 + c + 1]

        try:
            with ExitStack() as ph:
                def sb(name, shape, dt):
                    return ph.enter_context(nc.sbuf_tensor(name, list(shape), dt))

                identf = sb("identf", [128, 128], F32)
                cT = sb("cT", [128, 8, NS], F32)
                cA = sb("cA", [128, 8, NS], F32)
                modsb = sb("modsb", [NS, 6 * D], F32)
                bsb = sb("bsb", [NS, 6 * D], F32)
                wa = [sb("wa%d" % i, [128, 8, 512], F32) for i in range(3)]
                pm = [ph.enter_context(nc.psum_tensor("p0m%d" % i, [128, 512], F32)) for i in range(2)]
                B = {k: Buf() for k in ["identf", "ident", "ones", "eps", "cT", "cA", "modsb", "bsb", "small", "modT", "gm",
                                         "modS"]}
                ds_small = P.dsem("small")
                ds_c = P.dsem("c")
                ds_b = P.dsem("b")
                ds_mod = P.dsem("mod")
                ds_modT = P.dsem("modT")
                ds_wa = [P.dsem("wa%d" % i) for i in range(3)]
                wa_ring = Ring(list(zip(wa, ds_wa)))
                pm_ring = Ring(pm)


                P.op("pool", lambda e: e.memset(identf[:], 0.0), writes=[B["identf"]])
                P.op("pool", lambda e: e.affine_select(out=identf[:], in_=identf[:], pattern=[[-1, 128]],
                                                       compare_op=ALU.not_equal, fill=1.0, base=0,
                                                       channel_multiplier=1), reads=[B["identf"]], writes=[B["identf"]])
                P.op("dve", lambda e: e.tensor_copy(out=ident[:], in_=identf[:]), reads=[B["identf"]], writes=[B["ident"]])
                P.op("dve", lambda e: e.memset(ones[:], 1.0), writes=[B["ones"]])
                P.op("dve", lambda e: e.memset(onesf[:], 1.0), writes=[B["ones"]])
                P.op("dve", lambda e: e.memset(epsT[:], EPS), writes=[B["eps"]])

                def fm(dst, src, n):
                    P.dma("sp", dst, src.rearrange("o (c p) -> p (o c)", p=128), ds_small, pwrites=[B["small"]], nonc=True)

                fm(n1g[:], n1g_d, 8)
                fm(n2g[:], n2g_d, 8)
                fm(qg[:], qng_d, 3)
                fm(kvg[:], kvng_d, 2)
                fm(ag[:], ang_d, 4)
                fm(bg[:], bng_d, 4)
                for j in range(3):
                    P.dma("sp", convA[:, j, :], conva_d[j:j + 1, :].rearrange("o (c p) -> p (o c)", p=128), ds_small,
                          pwrites=[B["small"]], nonc=True)
                    for q4 in range(4):
                        P.dma("sp", convF[:, j, q4 * 11:(q4 + 1) * 11],
                              convf_d[j:j + 1, q4 * 1408:(q4 + 1) * 1408].rearrange("o (c p) -> p (o c)", p=128), ds_small,
                              pwrites=[B["small"]], nonc=True)
                for s in range(NS):
                    P.dma("sp", cT[:, :, s], c_d[s:s + 1, :].rearrange("o (c p) -> p (o c)", p=128), ds_c, pwrites=[B["cT"]],
                          nonc=True)
                P.dma("sp", bsb[:], b_ada_d.to_broadcast([NS, 6 * D]), ds_b, writes=[B["bsb"]])
                P.op("act", lambda e: e.activation(out=cA[:], in_=cT[:], func=AF.Silu), reads=[B["cT"]], writes=[B["cA"]])
                wav = w_ada_d.rearrange("(kc p) n -> p kc n", p=128)
                for j in range(12):
                    (wt, dsw), wb = wa_ring.next()
                    P.dma("sp", wt[:], wav[:, :, j * 512:(j + 1) * 512], dsw, writes=[wb])
                    pmt, pb = pm_ring.next()
                    P.mm(pmt[0:NS, :], [(cA[:, kc, :], wt[:, kc, :]) for kc in range(8)], reads=[B["cA"], wb], writes=[pb])
                    P.op("dve", lambda e, pmt=pmt, j=j: e.tensor_tensor(out=modsb[:, j * 512:(j + 1) * 512], in0=pmt[0:NS, :],
                                                                         in1=bsb[:, j * 512:(j + 1) * 512], op=ALU.add),
                         reads=[pb, B["bsb"]], pwrites=[B["modsb"]])
                P.dma("pool", modS, modsb[:], ds_mod, reads=[B["modsb"]], writes=[B["modS"]])
                for s in range(NS):
                    for v6 in range(6):
                        P.dma("sp", modT[:, s, v6 * 8:(v6 + 1) * 8],
                              modS[s:s + 1, v6 * D:(v6 + 1) * D].rearrange("o (j p) -> p (o j)", p=128), ds_modT,
                              reads=[B["modS"]], pwrites=[B["modT"]], nonc=True)
                for s in range(NS):
                    P.op("dve", lambda e, s=s: e.scalar_tensor_tensor(out=gm1[:, s, :], in0=modT[:, s, 8:16], scalar=1.0,
                                                                      in1=n1g[:], op0=ALU.add, op1=ALU.mult),
                         reads=[B["modT"], B["small"]], pwrites=[B["gm"]])
                    P.op("dve", lambda e, s=s: e.scalar_tensor_tensor(out=gm2[:, s, :], in0=modT[:, s, 32:40], scalar=1.0,
                                                                      in1=n2g[:], op0=ALU.add, op1=ALU.mult),
                         reads=[B["modT"], B["small"]], pwrites=[B["gm"]])
                P.op("dve", lambda e: e.tensor_scalar(out=qg[:], in0=qg[:], scalar1=ATTN_SCALE, scalar2=None, op0=ALU.mult),
                     reads=[B["small"]], writes=[B["small"]])
                P.emit_phase()
                if DEBUG_STOP == 0:
                    P.disabled = True

            with ExitStack() as ph:
                def sb(name, shape, dt):
                    return ph.enter_context(nc.sbuf_tensor(name, list(shape), dt))

                def pst(name, shape, dt):
                    return ph.enter_context(nc.psum_tensor(name, list(shape), dt))

                w_in_sb = sb("w_in_sb", [128, 8, INC], BF16)
                w_krr = sb("w_krr", [128, 8, 64], BF16)
                w_uq_sb = sb("w_uq_sb", [128, 3, 768], BF16)
                w_uqr = sb("w_uqr", [128, 3, 256], BF16)
                w_uk_sb = sb("w_uk_sb", [128, 2, 512], BF16)
                w_uv_sb = sb("w_uv_sb", [128, 2, 512], BF16)
                xin = [sb("xin%d" % i, [128, D], F32) for i in range(3)]
                junk = sb("junk", [128, D], BF16)
                ms = sb("ms", [128, 8], F32)
                sd = sb("sd", [128, 8], F32)
                rstd = sb("rstd", [128, 8], F32)
                xn = [sb("xn%d" % i, [128, 4, D], BF16) for i in range(2)]
                hT = [sb("hT%d" % i, [128, 8, 512], BF16) for i in range(2)]
                hasb = sb("hasb", [128, 4, 512], F32)
                psb = sb("psb", [128, 4, 512], F32)
                yasb = sb("yasb", [128, 4, 512], F32)
                basb = sb("basb", [128, 4, 512], F32)
                cqsb = sb("cqsb", [128, 3, 512], F32)
                ckvsb = sb("ckvsb", [128, 2, 512], F32)
                sq = [sb("sq%d" % i, [128, 512], BF16) for i in range(4)]
                rs = [sb("rs%d" % i, [128, 512], F32) for i in range(2)]
                rr = [sb("rr%d" % i, [128, 512], F32) for i in range(3)]
                cqn = sb("cqn", [128, 3, 512], BF16)
                ckvn = sb("ckvn", [128, 2, 512], BF16)
                yan = sb("yan", [128, 4, 512], BF16)
                qTo = sb("qTo", [128, 4, 512], BF16)
                qro = sb("qro", [64, 4, 512], BF16)
                t1 = [sb("t1_%d" % i, [64, 512], F32) for i in range(2)]
                t2 = [sb("t2_%d" % i, [64, 512], F32) for i in range(2)]
                kTo = sb("kTo", [128, 4, 512], BF16)
                kro = sb("kro", [64, 512], BF16)
                vo = sb("vo", [128, 4, 512], BF16)
                cs = [sb("cs%d" % i, [64, 512], F32) for i in range(2)]
                sn = [sb("sn%d" % i, [64, 512], F32) for i in range(2)]
                tp = [pst("tp%d" % i, [128, 2, 512], BF16) for i in range(2)]
                zp = [pst("zp%d" % i, [128, 512], F32) for i in range(6)]

                tp_ring = Ring(tp)
                zp_ring = Ring(zp)
                sq_ring = Ring(sq)
                rs_ring = Ring(rs)
                rr_ring = Ring(rr)
                t1_ring = Ring(t1)
                t2_ring = Ring(t2)
                xin_ds = [P.dsem("xin%d" % i) for i in range(3)]
                xin_ring = Ring(list(zip(xin, xin_ds)))
                cs_ds = [P.dsem("cs%d" % i) for i in range(2)]
                cs_ring = Ring(list(zip(cs, sn, cs_ds)))
                xn_ring = Ring(xn)
                hT_ring = Ring(hT)
                ds_w = P.dsem("p1w")
                ds_wup = P.dsem("wup")
                ds_st = {k: P.dsem("st_" + k) for k in ["ya", "q", "qr", "k", "kr", "v"]}
                Bw = Buf()
                Bst1 = [Buf() for _ in range(4)]
                Bha = [Buf() for _ in range(4)]
                Bp = [Buf() for _ in range(4)]
                Byas = [Buf() for _ in range(4)]
                pend_conv = []
                Bba = [Buf() for _ in range(4)]
                Bx = {k: Buf() for k in ["cqsb", "ckvsb", "cqn", "ckvn", "yan", "qTo", "qro", "kTo", "kro", "vo"]}

                P.dma("pool", w_in_sb[:], w_in_d.rearrange("(kc p) n -> p kc n", p=128), ds_w, pwrites=[Bw])
                P.dma("pool", w_uq_sb[:], w_uq_d.rearrange("(kc p) n -> p kc n", p=128), ds_w, pwrites=[Bw])
                wkv = w_ukv_d.rearrange("(kc p) (h t d) -> p kc h t d", p=128, h=NH, t=2)
                for kc in range(2):
                    P.dma("pool", w_uk_sb[:, kc, :].rearrange("p (h d) -> p h d", h=NH), wkv[:, kc, :, 0, :], ds_w, pwrites=[Bw])
                    P.dma("pool", w_uv_sb[:, kc, :].rearrange("p (h d) -> p h d", h=NH), wkv[:, kc, :, 1, :], ds_w, pwrites=[Bw])
                wupv = w_up_d.rearrange("(kc p) n -> p kc n", p=128)
                for u in range(NU):
                    P.dma("pool", wupS[u, :, :, 0:128], wupv[:, :, u * 128:(u + 1) * 128], ds_wup)
                    P.dma("pool", wupS[u, :, :, 128:256], wupv[:, :, DFF + u * 128:DFF + (u + 1) * 128], ds_wup)
                Bw2 = Buf()
                P.op("dve", lambda e: e.tensor_scalar(out=w_krr[:, :, 0:32], in0=w_in_sb[:, :, 2208:2240], scalar1=-1.0,
                                                      scalar2=None, op0=ALU.mult), reads=[Bw], pwrites=[Bw2])
                P.op("dve", lambda e: e.tensor_copy(out=w_krr[:, :, 32:64], in_=w_in_sb[:, :, 2176:2208]), reads=[Bw],
                     pwrites=[Bw2])
                for h in range(NH):
                    b0 = h * 192 + 128
                    P.op("dve", lambda e, h=h, b0=b0: e.tensor_scalar(out=w_uqr[:, :, h * 64:h * 64 + 32],
                                                                      in0=w_uq_sb[:, :, b0 + 32:b0 + 64], scalar1=-1.0,
                                                                      scalar2=None, op0=ALU.mult), reads=[Bw], pwrites=[Bw2])
                    P.op("dve", lambda e, h=h, b0=b0: e.tensor_copy(out=w_uqr[:, :, h * 64 + 32:h * 64 + 64],
                                                                    in_=w_uq_sb[:, :, b0:b0 + 32]), reads=[Bw], pwrites=[Bw2])
                for i in range(3):
                    P.op("pool", lambda e, i=i: e.memset(xin[i][:], 0.0), writes=[xin_ring.bufs[i]])

                WR = [Bw, Bw2]

                def p1_front_a(s, ti, t0, w, ntl):
                    S = seqs[s]
                    off = offs[s]
                    C = w + 2
                    first = (ti == 0)
                    last = (ti == ntl - 1)
                    jlo = 1 if first else 0
                    jhi = C - 1 if last else C
                    G = -(-C // 128)
                    xnt, xnb = xn_ring.next()
                    hTt, hTb = hT_ring.next()
                    (cst, snt, csd), csb = cs_ring.next()
                    P.dma("sp", cst[:, 0:C], cos_d[:, t0:t0 + C], csd, writes=[csb])
                    P.dma("sp", snt[:, 0:C], sin_d[:, t0:t0 + C], csd, pwrites=[csb])
                    for g in range(G):
                        r = min(128, C - 128 * g)
                        lo = max(jlo, 128 * g) - 128 * g
                        hi = min(jhi, 128 * g + r) - 128 * g
                        (xt, xd), xb = xin_ring.next()
                        tok0 = off + t0 - 1 + 128 * g
                        P.dma("sp", xt[lo:hi, :], x_d[tok0 + lo:tok0 + hi, :], xd, writes=[xb])
                        yield
                        P.op("act", lambda e, xt=xt, g=g: e.activation(out=junk[:], in_=xt[:], func=AF.Square,
                                                                       scale=1.0 / 32.0, accum_out=ms[:, g:g + 1]),
                             reads=[xb], writes=[Bst1[g]])
                        yield
                        P.op("act", lambda e, g=g: e.activation(out=sd[:, g:g + 1], in_=ms[:, g:g + 1], func=AF.Ln,
                                                                bias=epsT[:], scale=1.0), reads=[Bst1[g]], writes=[Bst1[g]])
                        yield
                        P.op("act", lambda e, g=g: e.activation(out=rstd[:, g:g + 1], in_=sd[:, g:g + 1], func=AF.Exp,
                                                                scale=-0.5), reads=[Bst1[g]], writes=[Bst1[g]])
                        yield
                        P.op("dve", lambda e, xt=xt, g=g, xnt=xnt: e.tensor_scalar(out=xnt[:, g, :], in0=xt[:],
                                                                                   scalar1=rstd[:, g:g + 1], scalar2=None,
                                                                                   op0=ALU.mult),
                             reads=[xb, Bst1[g]], pwrites=[xnb])
                        yield
                    return (s, off, t0, w, C, first, last, jlo, jhi, G, xnt, xnb, hTt, hTb, cst, snt, csb)

                def p1_front_b(cx):
                    (s, off, t0, w, C, first, last, jlo, jhi, G, xnt, xnb, hTt, hTb, cst, snt, csb) = cx
                    for cp in range(4):
                        tpt, tpb = tp_ring.next()
                        for c2 in range(2):
                            c = 2 * cp + c2
                            for g in range(G):
                                r = min(128, C - 128 * g)
                                P.op("pe", lambda e, tpt=tpt, g=g, r=r, c=c, c2=c2, xnt=xnt: e.transpose(
                                    tpt[:, c2, 128 * g:128 * g + r], xnt[0:r, g, c * 128:(c + 1) * 128], ident[0:r, 0:r]),
                                     reads=[xnb], pwrites=[tpb])
                        for c2 in range(2):
                            c = 2 * cp + c2
                            P.op("act", lambda e, tpt=tpt, c=c, c2=c2, hTt=hTt, s=s, C=C: e.activation(
                                out=hTt[:, c, 0:C], in_=tpt[:, c2, 0:C], func=AF.Identity, bias=sh1(s, c),
                                scale=gm1[:, s, c:c + 1]), reads=[tpb], pwrites=[hTb])

                class Stepper:
                    def __init__(self, gen):
                        self.gen = gen
                        self.done = gen is None
                        self.val = None

                    def step(self, n=1):
                        for _ in range(n):
                            if self.done:
                                return
                            try:
                                next(self.gen)
                            except StopIteration as e_:
                                self.done = True
                                self.val = e_.value

                    def finish(self):
                        while not self.done:
                            self.step()
                        return self.val

                def rstd_bc(stt, stb, n, scale):
                    rst, rsb = rs_ring.next()
                    P.op("act", lambda e: e.activation(out=rst[:, 0:n], in_=stt[:, 0:n], func=AF.Ln, bias=epsT[:], scale=scale),
                         reads=[stb], writes=[rsb])
                    rqt, rqb = rr_ring.next()
                    P.op("act", lambda e: e.activation(out=rqt[:, 0:n], in_=rst[:, 0:n], func=AF.Exp, scale=-0.5),
                         reads=[rsb], writes=[rqb])
                    return rqt, rqb

                def p1_back(cx, stp):
                    (s, off, t0, w, C, first, last, jlo, jhi, G, xnt, xnb, hTt, hTb, cst, snt, csb) = cx
                    def zgroup(col0, M, wsb=None, rot=False):
                        zt, zb = zp_ring.next()
                        if rot:
                            pairs = [(w_krr[:, kc, 0:64], hTt[:, kc, 0:C]) for kc in range(8)]
                        else:
                            pairs = [(w_in_sb[:, kc, col0:col0 + M], hTt[:, kc, 0:C]) for kc in range(8)]
                        P.mm(zt[0:M, 0:C], pairs, reads=[hTb] + WR, writes=[zb])
                        stp.step(2)
                        return zt, zb

                    for c in range(4):
                        zt, zb = zgroup(c * 128, 128)
                        P.op("act", lambda e, zt=zt, c=c, C=C: e.activation(out=hasb[:, c, 0:C], in_=zt[:, 0:C], func=AF.Copy),
                             reads=[zb], writes=[Bha[c]])
                    for c in range(4):
                        zt, zb = zgroup(1024 + c * 128, 128)
                        P.op("dve", lambda e, zt=zt, c=c, jlo=jlo, jhi=jhi: e.tensor_tensor(
                            out=psb[:, c, jlo:jhi], in0=zt[:, jlo:jhi], in1=hasb[:, c, jlo:jhi], op=ALU.mult),
                             reads=[zb, Bha[c]], pwrites=[Bp[c]])
                    if first:
                        P.op("pool", lambda e: e.memset(psb[:, :, 0:1], 0.0), pwrites=Bp)
                    if last:
                        P.op("pool", lambda e, C=C: e.memset(psb[:, :, C - 1:C], 0.0), pwrites=Bp)
                    while pend_conv:
                        pend_conv.pop(0)()
                    for c in range(4):
                        zt, zb = zgroup(512 + c * 128, 128)
                        P.op("act", lambda e, zt=zt, c=c, C=C: e.activation(out=basb[:, c, 0:C], in_=zt[:, 0:C], func=AF.Copy),
                             reads=[zb], writes=[Bba[c]])
                    sqs = []
                    for c in range(3):
                        zt, zb = zgroup(1536 + c * 128, 128)
                        P.op("act", lambda e, zt=zt, c=c, C=C: e.activation(out=cqsb[:, c, 0:C], in_=zt[:, 0:C], func=AF.Copy),
                             reads=[zb], pwrites=[Bx["cqsb"]])
                        sqt, sqb = sq_ring.next()
                        P.op("act", lambda e, zt=zt, sqt=sqt, C=C: e.activation(out=sqt[:, 0:C], in_=zt[:, 0:C], func=AF.Square),
                             reads=[zb], writes=[sqb])
                        sqs.append((sqt, sqb))
                    stt, stb = zp_ring.next()
                    P.mm(stt[:, 0:C], [(ones[:], q[0][:, 0:C]) for q in sqs], reads=[q[1] for q in sqs], writes=[stb])
                    rqt, rqb = rstd_bc(stt, stb, C, 1.0 / QL)
                    for c in range(3):
                        P.op("dve", lambda e, c=c, rqt=rqt, C=C: e.scalar_tensor_tensor(
                            out=cqn[:, c, 0:C], in0=cqsb[:, c, 0:C], scalar=qg[:, c:c + 1], in1=rqt[:, 0:C],
                            op0=ALU.mult, op1=ALU.mult), reads=[Bx["cqsb"], rqb], pwrites=[Bx["cqn"]])
                    sqs = []
                    for c in range(2):
                        zt, zb = zgroup(1920 + c * 128, 128)
                        P.op("act", lambda e, zt=zt, c=c, C=C: e.activation(out=ckvsb[:, c, 0:C], in_=zt[:, 0:C], func=AF.Copy),
                             reads=[zb], pwrites=[Bx["ckvsb"]])
                        sqt, sqb = sq_ring.next()
                        P.op("act", lambda e, zt=zt, sqt=sqt, C=C: e.activation(out=sqt[:, 0:C], in_=zt[:, 0:C], func=AF.Square),
                             reads=[zb], writes=[sqb])
                        sqs.append((sqt, sqb))
                    stt, stb = zp_ring.next()
                    P.mm(stt[:, 0:C], [(ones[:], q[0][:, 0:C]) for q in sqs], reads=[q[1] for q in sqs], writes=[stb])
                    rkt, rkb = rstd_bc(stt, stb, C, 1.0 / KVL)
                    for c in range(2):
                        P.op("dve", lambda e, c=c, rkt=rkt, C=C: e.scalar_tensor_tensor(
                            out=ckvn[:, c, 0:C], in0=ckvsb[:, c, 0:C], scalar=kvg[:, c:c + 1], in1=rkt[:, 0:C],
                            op0=ALU.mult, op1=ALU.mult), reads=[Bx["ckvsb"], rkb], pwrites=[Bx["ckvn"]])
                    za, zab = zgroup(2176, 64)
                    zr, zrb = zgroup(0, 64, rot=True)
                    t1t, t1b = t1_ring.next()
                    t2t, t2b = t2_ring.next()
                    P.op("dve", lambda e, za=za, t1t=t1t, cst=cst, C=C: e.tensor_tensor(
                        out=t1t[:, 0:C], in0=za[0:64, 0:C], in1=cst[:, 0:C], op=ALU.mult), reads=[zab, csb], writes=[t1b])
                    P.op("dve", lambda e, zr=zr, t2t=t2t, snt=snt, C=C: e.tensor_tensor(
                        out=t2t[:, 0:C], in0=zr[0:64, 0:C], in1=snt[:, 0:C], op=ALU.mult), reads=[zrb, csb], writes=[t2b])
                    P.op("dve", lambda e, t1t=t1t, t2t=t2t, C=C: e.tensor_tensor(
                        out=kro[:, 0:C], in0=t1t[:, 0:C], in1=t2t[:, 0:C], op=ALU.add), reads=[t1b, t2b], writes=[Bx["kro"]])
                    P.dma("pool", krS[:, off + t0:off + t0 + w], kro[:, 1:w + 1], ds_st["kr"], reads=[Bx["kro"]])
                    ncx = stp.finish()
                    if ncx is not None:
                        p1_front_b(ncx)
                    for h in range(NH):
                        zt, zb = zp_ring.next()
                        P.mm(zt[:, 0:C], [(w_uq_sb[:, kc, h * 192:h * 192 + 128], cqn[:, kc, 0:C]) for kc in range(3)],
                             reads=[Bx["cqn"]] + WR, writes=[zb])
                        P.op("act", lambda e, zt=zt, h=h, C=C: e.activation(out=qTo[:, h, 0:C], in_=zt[:, 0:C], func=AF.Copy),
                             reads=[zb], pwrites=[Bx["qTo"]])
                        za, zab = zp_ring.next()
                        P.mm(za[0:64, 0:C], [(w_uq_sb[:, kc, h * 192 + 128:h * 192 + 192], cqn[:, kc, 0:C]) for kc in range(3)],
                             reads=[Bx["cqn"]] + WR, writes=[zab])
                        zr, zrb = zp_ring.next()
                        P.mm(zr[0:64, 0:C], [(w_uqr[:, kc, h * 64:(h + 1) * 64], cqn[:, kc, 0:C]) for kc in range(3)],
                             reads=[Bx["cqn"]] + WR, writes=[zrb])
                        t1t, t1b = t1_ring.next()
                        t2t, t2b = t2_ring.next()
                        P.op("dve", lambda e, za=za, t1t=t1t, cst=cst, C=C: e.tensor_tensor(
                            out=t1t[:, 0:C], in0=za[0:64, 0:C], in1=cst[:, 0:C], op=ALU.mult), reads=[zab, csb], writes=[t1b])
                        P.op("dve", lambda e, zr=zr, t2t=t2t, snt=snt, C=C: e.tensor_tensor(
                            out=t2t[:, 0:C], in0=zr[0:64, 0:C], in1=snt[:, 0:C], op=ALU.mult), reads=[zrb, csb], writes=[t2b])
                        P.op("dve", lambda e, t1t=t1t, t2t=t2t, h=h, C=C: e.tensor_tensor(
                            out=qro[:, h, 0:C], in0=t1t[:, 0:C], in1=t2t[:, 0:C], op=ALU.add),
                             reads=[t1b, t2b], pwrites=[Bx["qro"]])
                    P.dma("pool", qS[:, :, off + t0:off + t0 + w].rearrange("h p t -> p h t"), qTo[:, :, 1:w + 1],
                          ds_st["q"], reads=[Bx["qTo"]])
                    P.dma("pool", qrS[:, :, off + t0:off + t0 + w].rearrange("h p t -> p h t"), qro[:, :, 1:w + 1],
                          ds_st["qr"], reads=[Bx["qro"]])
                    for h in range(NH):
                        zt, zb = zp_ring.next()
                        P.mm(zt[:, 0:C], [(w_uk_sb[:, kc, h * 128:(h + 1) * 128], ckvn[:, kc, 0:C]) for kc in range(2)],
                             reads=[Bx["ckvn"]] + WR, writes=[zb])
                        P.op("act", lambda e, zt=zt, h=h, C=C: e.activation(out=kTo[:, h, 0:C], in_=zt[:, 0:C], func=AF.Copy),
                             reads=[zb], pwrites=[Bx["kTo"]])
                    P.dma("pool", kS[:, :, off + t0:off + t0 + w].rearrange("h p t -> p h t"), kTo[:, :, 1:w + 1],
                          ds_st["k"], reads=[Bx["kTo"]])
                    for g in range(G):
                        r = min(128, C - 128 * g)
                        zt, zb = zp_ring.next()
                        P.mm(zt[0:r, :], [(ckvn[:, kc, 128 * g:128 * g + r], w_uv_sb[:, kc, :]) for kc in range(2)],
                             reads=[Bx["ckvn"]] + WR, writes=[zb])
                        P.op("dve", lambda e, zt=zt, g=g, r=r: e.tensor_copy(out=vo[0:r, g, :], in_=zt[0:r, :]),
                             reads=[zb], pwrites=[Bx["vo"]])
                    for g in range(G):
                        r = min(128, C - 128 * g)
                        lo = max(1, 128 * g) - 128 * g
                        hi = min(w + 1, 128 * g + r) - 128 * g
                        if hi <= lo:
                            continue
                        tok0 = off + t0 - 1 + 128 * g
                        P.dma("pool", vS[tok0 + lo:tok0 + hi, :], vo[lo:hi, g, :], ds_st["v"], reads=[Bx["vo"]])

                    ya = yasb
                    Bya = Byas
                    for c in range(4):
                        P.op("dve", lambda e, c=c, w=w: e.tensor_scalar(out=ya[:, c, 1:w + 1], in0=psb[:, c, 1:w + 1],
                                                                        scalar1=convA[:, 1, c:c + 1], scalar2=None,
                                                                        op0=ALU.mult), reads=[Bp[c]], writes=[Bya[c]])
                    for c in range(4):
                        P.op("dve", lambda e, c=c, w=w: e.scalar_tensor_tensor(
                            out=ya[:, c, 1:w + 1], in0=psb[:, c, 0:w], scalar=convA[:, 0, c:c + 1], in1=ya[:, c, 1:w + 1],
                            op0=ALU.mult, op1=ALU.add), reads=[Bp[c], Bya[c]], writes=[Bya[c]])
                    for c in range(4):
                        P.op("dve", lambda e, c=c, w=w: e.scalar_tensor_tensor(
                            out=ya[:, c, 1:w + 1], in0=psb[:, c, 2:w + 2], scalar=convA[:, 2, c:c + 1], in1=ya[:, c, 1:w + 1],
                            op0=ALU.mult, op1=ALU.add), reads=[Bp[c], Bya[c]], writes=[Bya[c]])
                    sqs = []
                    for c in range(4):
                        P.op("dve", lambda e, c=c, w=w: e.tensor_tensor(out=ya[:, c, 1:w + 1], in0=ya[:, c, 1:w + 1],
                                                                        in1=basb[:, c, 1:w + 1], op=ALU.mult),
                             reads=[Bba[c], Bya[c]], writes=[Bya[c]])
                    for c in range(4):
                        sqt, sqb = sq_ring.next()
                        P.op("act", lambda e, c=c, w=w, sqt=sqt: e.activation(out=sqt[:, 0:w], in_=ya[:, c, 1:w + 1],
                                                                              func=AF.Square), reads=[Bya[c]], writes=[sqb])
                        sqs.append((sqt, sqb))
                    def conv_b(sqs=sqs, w=w, off=off, t0=t0):
                        stt, stb = zp_ring.next()
                        P.mm(stt[:, 0:w], [(ones[:], q[0][:, 0:w]) for q in sqs], reads=[q[1] for q in sqs], writes=[stb])
                        rat, rab = rstd_bc(stt, stb, w, 1.0 / A_W)
                        for c in range(4):
                            P.op("dve", lambda e, c=c, rat=rat, w=w: e.scalar_tensor_tensor(
                                out=yan[:, c, 0:w], in0=yasb[:, c, 1:w + 1], scalar=ag[:, c:c + 1], in1=rat[:, 0:w],
                                op0=ALU.mult, op1=ALU.mult), reads=[Byas[c], rab], pwrites=[Bx["yan"]])
                        P.dma("pool", mixS[0:4, :, off + t0:off + t0 + w].rearrange("c p t -> p c t"), yan[:, :, 0:w],
                              ds_st["ya"], reads=[Bx["yan"]])

                    pend_conv.append(conv_b)
                    return ncx

                tiles_all = []
                for s in range(NS):
                    tl = seq_tiles(seqs[s])
                    for ti, (t0, w) in enumerate(tl):
                        tiles_all.append((s, ti, t0, w, len(tl)))
                st0 = Stepper(p1_front_a(*tiles_all[0]))
                cx = st0.finish()
                p1_front_b(cx)
                for n_ in range(len(tiles_all)):
                    stp = Stepper(p1_front_a(*tiles_all[n_ + 1]) if n_ + 1 < len(tiles_all) else None)
                    cx = p1_back(cx, stp)
                while pend_conv:
                    pend_conv.pop(0)()
                P.emit_phase()
                if DEBUG_STOP == 1:
                    P.disabled = True

            with ExitStack() as ph:
                def sb(name, shape, dt):
                    return ph.enter_context(nc.sbuf_tensor(name, list(shape), dt))

                def pst(name, shape, dt):
                    return ph.enter_context(nc.psum_tensor(name, list(shape), dt))

                kT = sb("kT", [128, NH, SMAX], BF16)
                krT = sb("krT", [128, SMAX], BF16)
                V = sb("V", [128, SMAX // 128, NH * DV], BF16)
                qT = [sb("qT%d" % i, [128, NH, 512], BF16) for i in range(2)]
                qrT = [sb("qrT%d" % i, [128, NH, 512], BF16) for i in range(2)]
                pT = [sb("pT%d" % i, [128, 1024], BF16) for i in range(4)]
                acc = [sb("acc%d" % i, [128, 1024], F32) for i in range(2)]
                yb = sb("yb", [128, NH, 512], F32)
                rec = [sb("rec%d" % i, [128, 512], F32) for i in range(2)]
                sq = [sb("sq2_%d" % i, [128, 512], BF16) for i in range(4)]
                rs2 = sb("rs2", [128, 512], F32)
                rb = sb("rb", [128, 512], F32)
                ybn = [sb("ybn%d" % i, [128, NH, 512], BF16) for i in range(1)]
                sps = [pst("sps%d" % i, [128, 512], F32) for i in range(6)]
                ops_ = [pst("ops%d" % i, [128, 512], F32) for i in range(2)]
                s_ring = Ring(sps)
                o_ring = Ring(ops_)
                acc_ring = Ring(acc)
                pT_ring = Ring(pT)
                pT_hb = [[Buf(), Buf()] for _ in range(4)]
                rec_ring = Ring(rec)
                sq_ring = Ring(sq)
                q_ds = [P.dsem("q2_%d" % i) for i in range(2)]
                q_ring = Ring(list(zip(qT, qrT, q_ds)))
                ybn_ds = [P.dsem("ybn%d" % i) for i in range(1)]
                ybn_ring = Ring(list(zip(ybn, ybn_ds)))
                ds_kv = P.dsem("kv")
                Bkv = Buf()
                P.op("pool", lambda e: e.memset(krT[64:128, :], 0.0), writes=[Bkv])
                for i in range(2):
                    P.op("pool", lambda e, i=i: e.memset(qrT[i][64:128, :, :], 0.0), writes=[q_ring.bufs[i]])
                Byb = [Buf() for _ in range(NH)]
                Brs = Buf()
                Brb = Buf()

                for s in range(NS):
                    S = seqs[s]
                    off = offs[s]
                    NKC = S // 128
                    P.dma("sp", kT[:, :, 0:S], kS[:, :, off:off + S].rearrange("h p t -> p h t"), ds_kv, pwrites=[Bkv])
                    P.dma("sp", krT[0:64, 0:S], krS[:, off:off + S], ds_kv, pwrites=[Bkv])
                    vsrc = vS[off:off + S, :].rearrange("(c p) d -> p c d", p=128)
                    step = 16
                    for c0 in range(0, NKC, step):
                        c1 = min(NKC, c0 + step)
                        P.dma("sp", V[:, c0:c1, :], vsrc[:, c0:c1, :], ds_kv, pwrites=[Bkv])
                    NQ = S // 512
                    assert NKC % 2 == 0
                    LOOK = 2
                    DEFER = 3
                    items = [(qi, h, kc) for qi in range(NQ) for h in range(NH) for kc in range(NKC)]
                    qts = {}
                    hst = {}
                    sqs_q = {}
                    pend = []
                    deferred = {}

                    def load_q(qi):
                        if qi >= NQ or qi in qts:
                            return
                        q0 = qi * 512
                        (qt, qrt, qd), qb = q_ring.next()
                        P.dma("sp", qt[:], qS[:, :, off + q0:off + q0 + 512].rearrange("h p t -> p h t"), qd, writes=[qb])
                        P.dma("sp", qrt[0:64, :, :], qrS[:, :, off + q0:off + q0 + 512].rearrange("h p t -> p h t"), qd,
                              pwrites=[qb])
                        qts[qi] = (qt, qrt, qb)

                    def head_state(qi, h):
                        if (qi, h) not in hst:
                            ot, ob = o_ring.next()
                            acc2, accb_ = acc_ring.next()
                            hst[(qi, h)] = dict(ot=ot, ob=ob, acc2=acc2, accb=accb_, cur=None)
                        return hst[(qi, h)]

                    def qk(item):
                        qi, h, kc = item
                        qt, qrt, qb = qts[qi]
                        hs = head_state(qi, h)
                        st_, stb_ = s_ring.next()
                        P.mm(st_[:], [(kT[:, h, kc * 128:(kc + 1) * 128], qt[:, h, :]),
                                      (krT[:, kc * 128:(kc + 1) * 128], qrt[:, h, :])], reads=[Bkv, qb], writes=[stb_])
                        if kc % 2 == 0:
                            slot = pT_ring.i % 4
                            ptp, _ = pT_ring.next()
                            hs["cur"] = (ptp, pT_hb[slot])
                        ptp, hb = hs["cur"]
                        half = kc % 2
                        P.op("act", lambda e, st_=st_, ptp=ptp, half=half: e.activation(
                            out=ptp[:, half * 512:(half + 1) * 512], in_=st_[:], func=AF.Exp), reads=[stb_], writes=[hb[half]])
                        pend.append((item, ptp, hb))

                    def pv():
                        (qi, h, kc), ptp, hb = pend.pop(0)
                        hs = hst[(qi, h)]
                        ot, ob, acc2, accb_ = hs["ot"], hs["ob"], hs["acc2"], hs["accb"]
                        half = kc % 2
                        f = (kc == 0)
                        l = (kc == NKC - 1)
                        if f:
                            P.mm(ot[:], [(V[:, kc, h * DV:(h + 1) * DV], ptp[:, half * 512:(half + 1) * 512])],
                                 reads=[Bkv, hb[half]], writes=[ob], first=f, last=l)
                        else:
                            P.mm(ot[:], [(V[:, kc, h * DV:(h + 1) * DV], ptp[:, half * 512:(half + 1) * 512])],
                                 reads=[Bkv, hb[half]], pwrites=[ob], first=f, last=l)
                        if half == 1:
                            if kc == 1:
                                P.op("dve", lambda e, ptp=ptp, acc2=acc2: e.tensor_copy(out=acc2[:], in_=ptp[:]),
                                     reads=[hb[0], hb[1]], writes=[accb_])
                            else:
                                P.op("dve", lambda e, ptp=ptp, acc2=acc2: e.tensor_tensor(out=acc2[:], in0=acc2[:], in1=ptp[:],
                                                                                          op=ALU.add),
                                     reads=[hb[0], hb[1], accb_], writes=[accb_])

                    def tail(qi, h):
                        hs = hst.pop((qi, h))
                        ot, ob, acc2, accb_ = hs["ot"], hs["ob"], hs["acc2"], hs["accb"]
                        P.op("dve", lambda e, acc2=acc2: e.tensor_tensor(out=acc2[:, 0:512], in0=acc2[:, 0:512],
                                                                         in1=acc2[:, 512:1024], op=ALU.add),
                             reads=[accb_], writes=[accb_])
                        mt, mb = s_ring.next()
                        P.mm(mt[:], [(onesf[:], acc2[:, 0:512])], reads=[accb_], writes=[mb])
                        ret, reb = rec_ring.next()
                        P.op("act", lambda e, mt=mt, ret=ret: e.activation(out=ret[:], in_=mt[:], func=AF.Ln), reads=[mb],
                             writes=[reb])
                        P.op("act", lambda e, ret=ret: e.activation(out=ret[:], in_=ret[:], func=AF.Exp, scale=-1.0),
                             reads=[reb], writes=[reb])
                        P.op("dve", lambda e, ot=ot, ret=ret, h=h: e.tensor_tensor(out=yb[:, h, :], in0=ot[:], in1=ret[:],
                                                                                  op=ALU.mult),
                             reads=[ob, reb], writes=[Byb[h]])
                        sqt, sqb = sq_ring.next()
                        P.op("act", lambda e, sqt=sqt, h=h: e.activation(out=sqt[:], in_=yb[:, h, :], func=AF.Square),
                             reads=[Byb[h]], writes=[sqb])
                        sqs_q.setdefault(qi, []).append((sqt, sqb))
                        if h == NH - 1:
                            qtile_end(qi)

                    def qtile_end(qi):
                        q0 = qi * 512
                        sqs = sqs_q.pop(qi)
                        stt, stb = s_ring.next()
                        P.mm(stt[:], [(ones[:], q[0][:]) for q in sqs], reads=[q[1] for q in sqs], writes=[stb])
                        P.op("act", lambda e, stt=stt: e.activation(out=rs2[:], in_=stt[:], func=AF.Ln, bias=epsT[:],
                                                                   scale=1.0 / A_W), reads=[stb], writes=[Brs])
                        P.op("act", lambda e: e.activation(out=rb[:], in_=rs2[:], func=AF.Exp, scale=-0.5), reads=[Brs],
                             writes=[Brb])
                        (ybt, ybd), ybb = ybn_ring.next()
                        for h in range(NH):
                            P.op("dve", lambda e, h=h, ybt=ybt: e.scalar_tensor_tensor(
                                out=ybt[:, h, :], in0=yb[:, h, :], scalar=bg[:, h:h + 1], in1=rb[:], op0=ALU.mult, op1=ALU.mult),
                                 reads=[Byb[h], Brb], pwrites=[ybb])
                        P.dma("pool", mixS[4:8, :, off + q0:off + q0 + 512].rearrange("c p t -> p c t"), ybt[:], ybd, reads=[ybb])

                    load_q(0)
                    load_q(1)
                    for i in range(min(LOOK, len(items))):
                        qk(items[i])
                    for i, it in enumerate(items):
                        if i + LOOK < len(items):
                            nx = items[i + LOOK]
                            if nx[1] == 0 and nx[2] == 0:
                                load_q(nx[0] + 1)
                            qk(nx)
                        pv()
                        for fn_ in deferred.pop(i, []):
                            fn_()
                        if it[2] == NKC - 1:
                            deferred.setdefault(i + DEFER, []).append(lambda qi=it[0], h=it[1]: tail(qi, h))
                    for k_ in sorted(deferred):
                        for fn_ in deferred[k_]:
                            fn_()
                P.emit_phase()
                if DEBUG_STOP == 2:
                    P.disabled = True

            with ExitStack() as ph:
                def sb(name, shape, dt):
                    return ph.enter_context(nc.sbuf_tensor(name, list(shape), dt))

                def pst(name, shape, dt):
                    return ph.enter_context(nc.psum_tensor(name, list(shape), dt))

                w_o_sb = sb("w_o_sb", [128, 8, D], BF16)
                w_dn_sb = sb("w_dn_sb", [128, NU, D], BF16)
                wup = [sb("wup%d" % i, [128, 8, 256], BF16) for i in range(4)]
                g1bc = sb("g1bc", [128, D], F32)
                g2bc = sb("g2bc", [128, D], F32)
                fgbc = sb("fgbc", [128, D], F32)
                mixT = sb("mixT", [128, 8, 512], BF16)
                xin = [sb("x3in%d" % i, [128, D], F32) for i in range(2)]
                x1 = sb("x1", [128, 4, D], F32)
                x1b = sb("x1b", [128, 4, D], F32)
                xn2 = sb("xn2", [128, 4, D], BF16)
                h2T = sb("h2T", [128, 8, 512], BF16)
                aT = sb("aT", [128, NU, 512], BF16)
                c1 = [sb("c1_%d" % i, [128, 512], F32) for i in range(3)]
                c2 = [sb("c2_%d" % i, [128, 512], F32) for i in range(4)]
                sg = [sb("sg%d" % i, [128, 512], F32) for i in range(2)]
                tmp = [sb("tmp%d" % i, [128, 512], F32) for i in range(2)]
                junk = sb("junk3", [128, D], BF16)
                ms = sb("ms3", [128, 8], F32)
                sd = sb("sd3", [128, 8], F32)
                rstd = sb("rstd3", [128, 8], F32)
                tp = [pst("tp3_%d" % i, [128, 2, 512], BF16) for i in range(2)]
                zp = [pst("zp3_%d" % i, [128, 512], F32) for i in range(6)]
                tp_ring = Ring(tp)
                zp_ring = Ring(zp)
                c1_ring = Ring(c1)
                c2_ring = Ring(c2)
                sg_ring = Ring(sg)
                tmp_ring = Ring(tmp)
                wup_ds = [P.dsem("wup%d" % i) for i in range(4)]
                wup_ring = Ring(list(zip(wup, wup_ds)))
                xin_ds = [P.dsem("x3in%d" % i) for i in range(2)]
                xin_ring = Ring(list(zip(xin, xin_ds)))
                ds_w = P.dsem("p3w")
                ds_g = P.dsem("p3g")
                ds_g2 = P.dsem("p3g2")
                ds_mix = P.dsem("mix")
                ds_out = [P.dsem("out%d" % g) for g in range(4)]
                ds_outb = [P.dsem("outb%d" % g) for g in range(4)]
                Bw = Buf()
                Bg = Buf()
                Bfg = Buf()
                Bmix = Buf()
                Bx1 = [Buf() for _ in range(4)]
                Bxn2 = Buf()
                Bh2 = Buf()
                BaT = Buf()
                Bst = [Buf() for _ in range(4)]
                Bst2 = [Buf() for _ in range(4)]

                class Stepper3:
                    def __init__(self, gen):
                        self.gen = gen
                        self.done = gen is None
                        self.val = None

                    def step(self, n=1):
                        for _ in range(n):
                            if self.done:
                                return
                            try:
                                next(self.gen)
                            except StopIteration as e_:
                                self.done = True
                                self.val = e_.value

                    def finish(self):
                        while not self.done:
                            self.step()
                        return self.val

                P.dma("pool", w_o_sb[:], w_o_d.rearrange("(kc p) n -> p kc n", p=128), ds_w, pwrites=[Bw])
                P.dma("pool", w_dn_sb[:, 0:11, :], w_dn_d[0:11 * 128, :].rearrange("(kc p) n -> p kc n", p=128), ds_w, pwrites=[Bw])
                P.dma("pool", w_dn_sb[:, 11:22, :], w_dn_d[11 * 128:22 * 128, :].rearrange("(kc p) n -> p kc n", p=128), ds_w,
                      pwrites=[Bw])
                P.dma("sp", fgbc[:], fg_d.to_broadcast([128, D]), ds_g, writes=[Bfg])
                P.op("pool", lambda e: e.memset(mixT[:], 0.0), writes=[Bmix])
                for i in range(2):
                    P.op("pool", lambda e, i=i: e.memset(xin[i][:], 0.0), writes=[xin_ring.bufs[i]])
                P.op("pool", lambda e: e.memset(aT[:], 0.0), writes=[BaT])

                Bg1 = Buf()
                Bg2 = Buf()
                x1s = [x1, x1b]
                Bx1s = [[Buf() for _ in range(4)] for _ in range(2)]
                x1_i = [0]

                def p3_A1(s, ti, t0, w, ntl):
                    off = offs[s]
                    C = w + 2
                    first = (ti == 0)
                    last = (ti == ntl - 1)
                    jlo = 1 if first else 0
                    jhi = C - 1 if last else C
                    G = -(-C // 128)
                    tokc = off + t0 - 1
                    x1t = x1s[x1_i[0] % 2]
                    Bx1 = Bx1s[x1_i[0] % 2]
                    x1_i[0] += 1
                    if first:
                        P.dma("sp", g1bc[:], modS[s:s + 1, 2 * D:3 * D].to_broadcast([128, D]), ds_g, writes=[Bg1])
                    P.dma("sp", mixT[:, :, jlo:jhi], mixS[:, :, tokc + jlo:tokc + jhi].rearrange("c p t -> p c t"), ds_mix,
                          writes=[Bmix])
                    yield
                    for g in range(G):
                        r = min(128, C - 128 * g)
                        lo = max(jlo, 128 * g) - 128 * g
                        hi = min(jhi, 128 * g + r) - 128 * g
                        (xt, xd), xb = xin_ring.next()
                        tok0 = tokc + 128 * g
                        P.dma("sp", xt[lo:hi, :], x_d[tok0 + lo:tok0 + hi, :], xd, writes=[xb])
                        yield
                        for hf in range(2):
                            zt, zb = zp_ring.next()
                            P.mm(zt[0:r, :], [(mixT[:, c, 128 * g:128 * g + r], w_o_sb[:, c, hf * 512:(hf + 1) * 512])
                                              for c in range(8)], reads=[Bmix, Bw], writes=[zb])
                            yield
                            tt, tb = tmp_ring.next()
                            P.op("dve", lambda e, zt=zt, tt=tt, r=r, hf=hf: e.tensor_tensor(
                                out=tt[0:r, :], in0=zt[0:r, :], in1=g1bc[0:r, hf * 512:(hf + 1) * 512], op=ALU.mult),
                                 reads=[zb, Bg1], writes=[tb])
                            yield
                            P.op("dve", lambda e, tt=tt, xt=xt, r=r, hf=hf, g=g: e.tensor_tensor(
                                out=x1t[0:r, g, hf * 512:(hf + 1) * 512], in0=tt[0:r, :], in1=xt[0:r, hf * 512:(hf + 1) * 512],
                                op=ALU.add), reads=[tb, xb], pwrites=[Bx1[g]])
                            yield
                        P.op("act", lambda e, g=g: e.activation(out=junk[:], in_=x1t[:, g, :], func=AF.Square, scale=1.0 / 32.0,
                                                                accum_out=ms[:, g:g + 1]), reads=[Bx1[g]], writes=[Bst[g]])
                        yield
                        P.op("act", lambda e, g=g: e.activation(out=sd[:, g:g + 1], in_=ms[:, g:g + 1], func=AF.Sqrt,
                                                                bias=epsT[:], scale=1.0), reads=[Bst[g]], writes=[Bst[g]])
                        yield
                        P.op("dve", lambda e, g=g: e.reciprocal(out=rstd[:, g:g + 1], in_=sd[:, g:g + 1]),
                             reads=[Bst[g]], writes=[Bst[g]])
                        yield
                        P.op("act", lambda e, g=g: e.activation(out=xn2[:, g, :], in_=x1t[:, g, :], func=AF.Identity,
                                                                scale=rstd[:, g:g + 1]), reads=[Bx1[g], Bst[g]], pwrites=[Bxn2])
                        yield
                    return (s, off, t0, w, C, first, last, jlo, jhi, G, tokc, x1t, Bx1)

                def p3_A2(cx):
                    (s, off, t0, w, C, first, last, jlo, jhi, G, tokc, x1t, Bx1) = cx
                    for cp in range(4):
                        tpt, tpb = tp_ring.next()
                        for c2 in range(2):
                            c = 2 * cp + c2
                            for g in range(G):
                                r = min(128, C - 128 * g)
                                P.op("pe", lambda e, tpt=tpt, g=g, r=r, c=c, c2=c2: e.transpose(
                                    tpt[:, c2, 128 * g:128 * g + r], xn2[0:r, g, c * 128:(c + 1) * 128], ident[0:r, 0:r]),
                                     reads=[Bxn2], pwrites=[tpb])
                            yield
                        for c2 in range(2):
                            c = 2 * cp + c2
                            P.op("act", lambda e, tpt=tpt, c=c, c2=c2, s=s, jlo=jlo, jhi=jhi: e.activation(
                                out=h2T[:, c, jlo:jhi], in_=tpt[:, c2, jlo:jhi], func=AF.Identity, bias=sh2(s, c),
                                scale=gm2[:, s, c:c + 1]), reads=[tpb], pwrites=[Bh2])
                            yield
                    if first:
                        P.op("pool", lambda e: e.memset(h2T[:, :, 0:1], 0.0), pwrites=[Bh2])
                    if last:
                        P.op("pool", lambda e, C=C: e.memset(h2T[:, :, C - 1:C], 0.0), pwrites=[Bh2])
                    return None

                def p3_B(cx, stp):
                    (s, off, t0, w, C, first, last, jlo, jhi, G, tokc, x1t, Bx1) = cx
                    if first:
                        P.dma("sp", g2bc[:], modS[s:s + 1, 5 * D:6 * D].to_broadcast([128, D]), ds_g2, writes=[Bg2])
                    for u in range(NU):
                        (wt, wd), wb = wup_ring.next()
                        P.dma("sp", wt[:], wupS[u], wd, writes=[wb])
                        zg, zgb = zp_ring.next()
                        P.mm(zg[:, 0:C], [(wt[:, kc, 0:128], h2T[:, kc, 0:C]) for kc in range(8)], reads=[wb, Bh2], writes=[zgb])
                        zv, zvb = zp_ring.next()
                        P.mm(zv[:, 0:C], [(wt[:, kc, 128:256], h2T[:, kc, 0:C]) for kc in range(8)], reads=[wb, Bh2], writes=[zvb])
                        outs = []
                        for (zt, zb, ch) in ((zg, zgb, u), (zv, zvb, NU + u)):
                            c1t, c1b = c1_ring.next()
                            P.op("act", lambda e, zt=zt, c1t=c1t, ch=ch, w=w: e.activation(
                                out=c1t[:, 0:w], in_=zt[:, 1:w + 1], func=AF.Identity, scale=convF[:, 1, ch:ch + 1]),
                                 reads=[zb], writes=[c1b])
                            c2t, c2b = c2_ring.next()
                            P.op("dve", lambda e, zt=zt, c1t=c1t, c2t=c2t, ch=ch, w=w: e.scalar_tensor_tensor(
                                out=c2t[:, 0:w], in0=zt[:, 0:w], scalar=convF[:, 0, ch:ch + 1], in1=c1t[:, 0:w],
                                op0=ALU.mult, op1=ALU.add), reads=[zb, c1b], writes=[c2b])
                            P.op("dve", lambda e, zt=zt, c2t=c2t, ch=ch, w=w: e.scalar_tensor_tensor(
                                out=c2t[:, 0:w], in0=zt[:, 2:w + 2], scalar=convF[:, 2, ch:ch + 1], in1=c2t[:, 0:w],
                                op0=ALU.mult, op1=ALU.add), reads=[zb, c2b], writes=[c2b])
                            outs.append((c2t, c2b))
                        sgt, sgb = sg_ring.next()
                        P.op("act", lambda e, sgt=sgt, c2t=outs[0][0], w=w: e.activation(out=sgt[:, 0:w], in_=c2t[:, 0:w],
                                                                                          func=AF.Silu),
                             reads=[outs[0][1]], writes=[sgb])
                        P.op("pool", lambda e, sgt=sgt, c2t=outs[1][0], u=u, w=w: e.tensor_tensor(
                            out=aT[:, u, 1:w + 1], in0=sgt[:, 0:w], in1=c2t[:, 0:w], op=ALU.mult),
                             reads=[sgb, outs[1][1]], pwrites=[BaT])
                        if u == 2:
                            stp.step(2)

                def p3_C(cx, stp):
                    (s, off, t0, w, C, first, last, jlo, jhi, G, tokc, x1t, Bx1) = cx
                    for g in range(G):
                        r = min(128, C - 128 * g)
                        lo = max(1, 128 * g) - 128 * g
                        hi = min(w + 1, 128 * g + r) - 128 * g
                        for hf in range(2):
                            zt, zb = zp_ring.next()
                            P.mm(zt[0:r, :], [(aT[:, u, 128 * g:128 * g + r], w_dn_sb[:, u, hf * 512:(hf + 1) * 512])
                                              for u in range(NU)], reads=[BaT, Bw], writes=[zb])
                            if 2 * g + hf >= 2:
                                stp.step(4)
                            tt, tb = tmp_ring.next()
                            P.op("dve", lambda e, zt=zt, tt=tt, r=r, hf=hf: e.tensor_tensor(
                                out=tt[0:r, :], in0=zt[0:r, :], in1=g2bc[0:r, hf * 512:(hf + 1) * 512], op=ALU.mult),
                                 reads=[zb, Bg2], writes=[tb])
                            P.op("dve", lambda e, tt=tt, r=r, hf=hf, g=g: e.tensor_tensor(
                                out=x1t[0:r, g, hf * 512:(hf + 1) * 512], in0=tt[0:r, :], in1=x1t[0:r, g, hf * 512:(hf + 1) * 512],
                                op=ALU.add), reads=[tb, Bx1[g]], writes=[Bx1[g]])
                        P.op("act", lambda e, g=g: e.activation(out=junk[:], in_=x1t[:, g, :], func=AF.Square, scale=1.0 / 32.0,
                                                                accum_out=ms[:, 4 + g:5 + g]), reads=[Bx1[g]], writes=[Bst2[g]])
                        P.op("act", lambda e, g=g: e.activation(out=sd[:, 4 + g:5 + g], in_=ms[:, 4 + g:5 + g], func=AF.Sqrt,
                                                                bias=epsT[:], scale=1.0), reads=[Bst2[g]], writes=[Bst2[g]])
                        P.op("dve", lambda e, g=g: e.reciprocal(out=rstd[:, 4 + g:5 + g], in_=sd[:, 4 + g:5 + g]),
                             reads=[Bst2[g]], writes=[Bst2[g]])
                        P.op("dve", lambda e, g=g: e.scalar_tensor_tensor(
                            out=x1t[:, g, :], in0=x1t[:, g, :], scalar=rstd[:, 4 + g:5 + g], in1=fgbc[:], op0=ALU.mult, op1=ALU.mult),
                             reads=[Bst2[g], Bfg, Bx1[g]], writes=[Bx1[g]])
                        if hi > lo:
                            tok0 = tokc + 128 * g
                            P.dma("pool", y_d[tok0 + lo:tok0 + hi, :], x1t[lo:hi, g, :],
                                  (ds_out if x1t is x1 else ds_outb)[g], reads=[Bx1[g]])

                tiles3 = []
                for s in range(NS):
                    tl = seq_tiles(seqs[s])
                    for ti, (t0, w) in enumerate(tl):
                        tiles3.append((s, ti, t0, w, len(tl)))
                cx = Stepper3(p3_A1(*tiles3[0])).finish()
                Stepper3(p3_A2(cx)).finish()
                for n_ in range(len(tiles3)):
                    stA1 = Stepper3(p3_A1(*tiles3[n_ + 1]) if n_ + 1 < len(tiles3) else None)
                    p3_B(cx, stA1)
                    ncx = stA1.finish()
                    stA2 = Stepper3(p3_A2(ncx) if ncx is not None else None)
                    p3_C(cx, stA2)
                    stA2.finish()
                    cx = ncx

                P.emit_phase()
                if DEBUG_STOP == 3:
                    P.disabled = True
        except _Stop:
            pass
        P.final_wait()
    return nc


def rope_tables_np(smax):
    inv = (1.0 / (np.float32(THETA) ** (np.arange(0, DR, 2, dtype=np.float32) / np.float32(DR)))).astype(np.float32)
    ang = (np.arange(-1, smax + 1, dtype=np.float32)[None, :] * inv[:, None]).astype(np.float32)
    cos = np.cos(ang).astype(np.float32)
    sin = np.sin(ang).astype(np.float32)
    return np.concatenate([cos, cos], 0), np.concatenate([sin, sin], 0)


_CACHE = {}


def run_cores(seqs, xs, cs, weights):
    key = tuple(seqs)
    if key not in _CACHE:
        _CACHE[key] = build_program(list(seqs))
    nc = _CACHE[key]
    cos2, sin2 = rope_tables_np(max(seqs))
    shared = dict(weights)
    shared["cos2"] = cos2
    shared["sin2"] = sin2
    in_maps = []
    for i in range(len(xs)):
        m = dict(shared)
        m["x"] = np.ascontiguousarray(xs[i], dtype=np.float32)
        m["c"] = np.ascontiguousarray(cs[i], dtype=np.float32)
        in_maps.append(m)
    res = run_bass_kernel_spmd(nc, in_maps, core_ids=list(range(len(xs))))
    return [r["y"] for r in res.results]


def prep_weights(w_ada, b_ada, norm1_g, w_in, conv_a_w, q_norm_g, w_uq, kv_norm_g, w_ukv, out_norm_a_g, out_norm_b_g,
                 w_o, norm2_g, w_up, ffn_conv_w, w_down, final_g):
    f = lambda a: np.ascontiguousarray(np.asarray(a, dtype=np.float32))
    return dict(
        w_ada=f(w_ada[0]), b_ada=f(b_ada[0]).reshape(1, -1), norm1_g=f(norm1_g[0]).reshape(1, -1), w_in=f(w_in[0]),
        conv_a_w=f(conv_a_w[0]), q_norm_g=f(q_norm_g[0]).reshape(1, -1), w_uq=f(w_uq[0]),
        kv_norm_g=f(kv_norm_g[0]).reshape(1, -1), w_ukv=f(w_ukv[0]), out_norm_a_g=f(out_norm_a_g[0]).reshape(1, -1),
        out_norm_b_g=f(out_norm_b_g[0]).reshape(1, -1), w_o=f(w_o[0]), norm2_g=f(norm2_g[0]).reshape(1, -1),
        w_up=f(w_up[0]), ffn_conv_w=f(ffn_conv_w[0]), w_down=f(w_down[0]), final_g=f(final_g).reshape(1, -1))


def kernel(x_prompt, x_sample, c_prompt, c_sample, w_ada, b_ada, norm1_g, w_in, conv_a_w, q_norm_g, w_uq, kv_norm_g,
           w_ukv, out_norm_a_g, out_norm_b_g, w_o, norm2_g, w_up, ffn_conv_w, w_down, final_g):
    x_prompt = np.asarray(x_prompt, dtype=np.float32)
    x_sample = np.asarray(x_sample, dtype=np.float32)
    c_prompt = np.asarray(c_prompt, dtype=np.float32)
    c_sample = np.asarray(c_sample, dtype=np.float32)
    weights = prep_weights(w_ada, b_ada, norm1_g, w_in, conv_a_w, q_norm_g, w_uq, kv_norm_g, w_ukv, out_norm_a_g,
                           out_norm_b_g, w_o, norm2_g, w_up, ffn_conv_w, w_down, final_g)
    SS = x_sample.shape[1]
    SP = x_prompt.shape[1]
    seqs = (SS, SP, SP)
    xs, cs = [], []
    for i in range(N_CORES):
        xs.append(np.concatenate([x_sample[i], x_prompt[2 * i], x_prompt[2 * i + 1]], axis=0))
        cs.append(np.stack([c_sample[i], c_prompt[2 * i], c_prompt[2 * i + 1]], axis=0))
    ys = run_cores(seqs, xs, cs, weights)
    y_prompt = np.empty_like(x_prompt)
    y_sample = np.empty_like(x_sample)
    for i in range(N_CORES):
        y = ys[i]
        y_sample[i] = y[0:SS]
        y_prompt[2 * i] = y[SS:SS + SP]
        y_prompt[2 * i + 1] = y[SS + SP:SS + 2 * SP]
    return (y_prompt, y_sample)
```
